# Optimizing a Trainium2 kernel written in Bass

```python
import math, functools
import jax, jax.numpy as jnp
from jax import lax
import numpy as np

D_MODEL = 1024
BATCH = 8
SEQ = 4096
DEPTH = 1

PLE_DIM = 256
MIX_W = D_MODEL
GDN_HEADS = 8
GDN_DK = 64
GDN_DV = 64
GDN_QK_W = GDN_HEADS * GDN_DK
GDN_V_W = GDN_HEADS * GDN_DV
GDN_CONV = 4
GDN_CHUNK = 64
MLA_HEADS = 8
MLA_NOPE = 64
MLA_ROPE = 32
MLA_V = 64
MLA_W = MLA_HEADS * MLA_V
MLA_Q_RANK = 256
MLA_KV_RANK = 128
ROPE_THETA = 10000.0
Q_BLOCK = 128
D_FF = 4 * D_MODEL
EPS = 1e-6
IN_SIZES = (2 * GDN_QK_W + GDN_V_W,
            GDN_V_W,
            GDN_HEADS,
            GDN_HEADS,
            MLA_Q_RANK,
            MLA_KV_RANK,
            MLA_ROPE)
IN_W = sum(IN_SIZES)

kernel_name = "hybrid_gdn_mla_parallel_heads_block"


def rmsnorm(x, w):
    xf = x.astype(jnp.float32)
    y = xf * lax.rsqrt(jnp.mean(xf * xf, axis=-1, keepdims=True) + EPS)
    return (y * w.astype(jnp.float32)).astype(x.dtype)


def l2norm(x):
    return x * lax.rsqrt(jnp.sum(x * x, axis=-1, keepdims=True) + EPS)


def rope(x, positions):
    r = x.shape[-1]
    inv_freq = ROPE_THETA ** (-jnp.arange(0, r, 2, dtype=jnp.float32) / r)
    ang = positions.astype(jnp.float32)[:, None, :, None] * inv_freq
    cos, sin = jnp.cos(ang), jnp.sin(ang)
    xf = x.astype(jnp.float32)
    x1, x2 = xf[..., : r // 2], xf[..., r // 2:]
    return jnp.concatenate([x1 * cos - x2 * sin, x2 * cos + x1 * sin], axis=-1).astype(x.dtype)


def causal_depthwise_conv(x, w):
    k, c = w.shape
    return lax.conv_general_dilated(
        x, w[:, None, :].astype(x.dtype), window_strides=(1,), padding=[(k - 1, 0)],
        dimension_numbers=("NWC", "WIO", "NWC"), feature_group_count=c)


def gated_delta_rule_chunked(q, k, v, g, beta):
    B, H, S, dk = k.shape
    dv = v.shape[-1]
    C = GDN_CHUNK
    N = S // C
    q = q * (dk ** -0.5)
    to_chunks = lambda t: t.reshape(B, H, N, C, t.shape[-1])
    q, k, v = to_chunks(q), to_chunks(k), to_chunks(v)
    g = jnp.cumsum(g.reshape(B, H, N, C), axis=-1)
    beta = beta.reshape(B, H, N, C)
    incl = jnp.tril(jnp.ones((C, C), dtype=bool))
    strict = jnp.tril(jnp.ones((C, C), dtype=bool), -1)
    decay = jnp.exp(jnp.where(incl, g[..., :, None] - g[..., None, :], -jnp.inf))
    kk = jnp.einsum('bhnid,bhnjd->bhnij', k, k)
    lower = jnp.where(strict, kk * decay * beta[..., :, None], 0.0)
    eye = jnp.eye(C, dtype=q.dtype)
    T = lax.linalg.triangular_solve(eye + lower, jnp.broadcast_to(eye, lower.shape),
                                    left_side=True, lower=True)
    u = T @ (v * beta[..., None])
    w = T @ (k * (beta * jnp.exp(g))[..., None])
    qk = jnp.einsum('bhnid,bhnjd->bhnij', q, k) * decay
    q_dec = q * jnp.exp(g)[..., None]
    g_last = g[..., -1:]
    k_dec = k * jnp.exp(g_last - g)[..., None]
    state_decay = jnp.exp(g_last[..., 0])

    def step(state, xs):
        qd, a, uc, wc, kd, sd = xs
        v_new = uc - wc @ state
        o = qd @ state + a @ v_new
        state = state * sd[..., None, None] + jnp.swapaxes(kd, -1, -2) @ v_new
        return state, o

    state0 = jnp.zeros((B, H, dk, dv), q.dtype)
    xs = tuple(jnp.moveaxis(t, 2, 0) for t in (q_dec, qk, u, w, k_dec, state_decay))
    _, o = lax.scan(step, state0, xs)
    return jnp.moveaxis(o, 0, 2).reshape(B, H, S, dv)


def gdn_mixer(qkv, z, b, a, conv_w, A_log, dt_bias, norm_w):
    B, S, _ = qkv.shape
    dt = qkv.dtype
    qkv = jax.nn.silu(causal_depthwise_conv(qkv, conv_w))
    q, k, v = qkv[..., :GDN_QK_W], qkv[..., GDN_QK_W:2 * GDN_QK_W], qkv[..., 2 * GDN_QK_W:]
    heads = lambda t, d: t.astype(jnp.float32).reshape(B, S, GDN_HEADS, d).transpose(0, 2, 1, 3)
    q = l2norm(heads(q, GDN_DK))
    k = l2norm(heads(k, GDN_DK))
    v = heads(v, GDN_DV)
    beta = jax.nn.sigmoid(b.astype(jnp.float32)).transpose(0, 2, 1)
    g = -(jnp.exp(A_log.astype(jnp.float32)) *
          jax.nn.softplus(a.astype(jnp.float32) + dt_bias.astype(jnp.float32))).transpose(0, 2, 1)
    o = gated_delta_rule_chunked(q, k, v, g, beta).transpose(0, 2, 1, 3)
    zf = z.astype(jnp.float32).reshape(B, S, GDN_HEADS, GDN_DV)
    o = rmsnorm(o, norm_w) * jax.nn.silu(zf)
    return o.reshape(B, S, GDN_V_W).astype(dt)


def blocked_causal_attention(q, k, v):
    B, H, S, dq = q.shape
    dv = v.shape[-1]
    nb = S // Q_BLOCK
    qb = q.reshape(B, H, nb, Q_BLOCK, dq).transpose(2, 0, 1, 3, 4)
    key_idx = jnp.arange(S)
    scale = dq ** -0.5

    def block(args):
        q_blk, start = args
        s = jnp.einsum('bhqd,bhkd->bhqk', q_blk, k, preferred_element_type=jnp.float32) * scale
        q_idx = start + jnp.arange(Q_BLOCK)
        s = jnp.where(key_idx[None, :] <= q_idx[:, None], s, -jnp.inf)
        pr = jax.nn.softmax(s, axis=-1).astype(v.dtype)
        return jnp.einsum('bhqk,bhkd->bhqd', pr, v)

    out = lax.map(block, (qb, jnp.arange(nb, dtype=jnp.int32) * Q_BLOCK))
    return out.transpose(1, 2, 0, 3, 4).reshape(B, H, S, dv)


def mla_mixer(c_q, c_kv, k_pe, positions, q_norm_w, w_q_b, kv_norm_w, w_kv_b, out_norm_w):
    B, S, _ = c_q.shape
    q = (rmsnorm(c_q, q_norm_w) @ w_q_b).reshape(B, S, MLA_HEADS, MLA_NOPE + MLA_ROPE)
    q = q.transpose(0, 2, 1, 3)
    q_nope, q_pe = q[..., :MLA_NOPE], q[..., MLA_NOPE:]
    kv = (rmsnorm(c_kv, kv_norm_w) @ w_kv_b).reshape(B, S, MLA_HEADS, MLA_NOPE + MLA_V)
    kv = kv.transpose(0, 2, 1, 3)
    k_nope, v = kv[..., :MLA_NOPE], kv[..., MLA_NOPE:]
    q_pe = rope(q_pe, positions)
    k_pe = rope(k_pe[:, None], positions)
    qf = jnp.concatenate([q_nope, q_pe], axis=-1)
    kf = jnp.concatenate([k_nope, jnp.broadcast_to(k_pe, (B, MLA_HEADS, S, MLA_ROPE))], axis=-1)
    o = blocked_causal_attention(qf, kf, v).transpose(0, 2, 1, 3).reshape(B, S, MLA_W)
    return rmsnorm(o, out_norm_w)


def setup_inputs(seed: int = 0) -> dict:
    key = jax.random.key(seed)
    ks = jax.random.split(key, 32)
    f32 = jnp.float32

    def dense(k, fan_in, fan_out):
        return jax.random.normal(k, (DEPTH, fan_in, fan_out), f32) * fan_in ** -0.5

    def gain(k, n):
        return jnp.ones((DEPTH, n), f32) + 0.01 * jax.random.normal(k, (DEPTH, n), f32)

    x = jax.random.normal(ks[0], (BATCH, SEQ, D_MODEL), f32)
    p = jax.random.normal(ks[1], (DEPTH, BATCH, SEQ, PLE_DIM), f32)
    offset = jax.random.randint(ks[2], (BATCH, 1), 0, 1024, dtype=jnp.int32)
    positions = offset + jnp.arange(SEQ, dtype=jnp.int32)[None, :]
    dt0 = jnp.exp(jax.random.uniform(ks[3], (DEPTH, GDN_HEADS), f32, math.log(1e-3), math.log(1e-1)))
    return {
        "x": x,
        "p": p,
        "positions": positions,
        "mix_norm_w": gain(ks[4], D_MODEL),
        "w_in": dense(ks[5], D_MODEL, IN_W),
        "conv_w": jax.random.normal(ks[6], (DEPTH, GDN_CONV, 2 * GDN_QK_W + GDN_V_W), f32) * GDN_CONV ** -0.5,
        "A_log": jnp.log(jax.random.uniform(ks[7], (DEPTH, GDN_HEADS), f32, 1.0, 16.0)),
        "dt_bias": jnp.log(jnp.expm1(dt0)),
        "gdn_norm_w": gain(ks[8], GDN_DV),
        "q_norm_w": gain(ks[9], MLA_Q_RANK),
        "w_q_b": dense(ks[10], MLA_Q_RANK, MLA_HEADS * (MLA_NOPE + MLA_ROPE)),
        "kv_norm_w": gain(ks[11], MLA_KV_RANK),
        "w_kv_b": dense(ks[12], MLA_KV_RANK, MLA_HEADS * (MLA_NOPE + MLA_V)),
        "mla_out_norm_w": gain(ks[13], MLA_W),
        "w_out": dense(ks[14], MIX_W, D_MODEL),
        "mlp_norm_w": gain(ks[15], D_MODEL),
        "w_up": dense(ks[16], D_MODEL, D_FF),
        "w_down": dense(ks[17], D_FF, D_MODEL),
        "w_ple_proj": dense(ks[18], PLE_DIM, D_MODEL),
        "ple_post_norm_w": gain(ks[19], D_MODEL),
        "ple_gate_norm_w": gain(ks[20], D_MODEL),
        "w_ple_gate": dense(ks[21], D_MODEL, D_MODEL),
        "final_norm_w": gain(ks[22], D_MODEL)[0],
    }


def reference(x, p, positions, mix_norm_w, w_in, conv_w, A_log, dt_bias, gdn_norm_w,
              q_norm_w, w_q_b, kv_norm_w, w_kv_b, mla_out_norm_w, w_out, mlp_norm_w,
              w_up, w_down, w_ple_proj, ple_post_norm_w, ple_gate_norm_w, w_ple_gate,
              final_norm_w):
    h = x
    for i in range(DEPTH):
        u = rmsnorm(h, mix_norm_w[i])
        proj = u @ w_in[i]
        cols = []
        start = 0
        for size in IN_SIZES:
            cols.append(proj[..., start:start + size])
            start += size
        qkv, z, b, a, c_q, c_kv, k_pe = cols
        y_gdn = gdn_mixer(qkv, z, b, a, conv_w[i], A_log[i], dt_bias[i], gdn_norm_w[i])
        y_mla = mla_mixer(c_q, c_kv, k_pe, positions, q_norm_w[i], w_q_b[i], kv_norm_w[i],
                          w_kv_b[i], mla_out_norm_w[i])
        h = h + jnp.concatenate([y_gdn, y_mla], axis=-1) @ w_out[i]
        u = rmsnorm(h, mlp_norm_w[i])
        h = h + jnp.square(jax.nn.relu(u @ w_up[i])) @ w_down[i]
        e = rmsnorm(p[i].astype(h.dtype) @ w_ple_proj[i], ple_post_norm_w[i])
        gate = jax.nn.sigmoid(rmsnorm(h, ple_gate_norm_w[i]) @ w_ple_gate[i])
        h = h + gate * e
    return rmsnorm(h, final_norm_w)
```

```python
import numpy as np
import concourse.bass as bass
import concourse.mybir as mybir
from concourse.bass_utils import run_bass_kernel_spmd
from contextlib import ExitStack

F32 = mybir.dt.float32
BF16 = mybir.dt.bfloat16
I32 = mybir.dt.int32
AF = mybir.ActivationFunctionType
ALU = mybir.AluOpType
AX = mybir.AxisListType

S = 4096
D = 1024
NT = 8
EPS = 1e-6
STRICT_SAME = True
RELAX_SAME = False


class Buf:
    __slots__ = ("name", "w", "r", "sem", "dtot", "excl")

    def __init__(self, name):
        self.name = name
        self.excl = False
        self.w = None
        self.r = {}
        self.sem = None
        self.dtot = 0


class T:
    def __init__(self, t, name):
        self.t = t
        self.name = name
        self.b = Buf(name)
        self.subs = {}

    def __getitem__(self, idx):
        return self.t[idx]

    def sub(self, key):
        if key not in self.subs:
            self.subs[key] = Buf(f"{self.name}.{key}")
        return self.subs[key]


class TView(T):
    def __init__(self, base, dt):
        self.t = base.t
        self.name = base.name
        self.b = base.b
        self.subs = base.subs
        self.v = base.t[:, :].bitcast(dt)

    def __getitem__(self, idx):
        return self.v[idx]


class TRe(T):
    def __init__(self, base, pattern, **kw):
        self.t = base.t
        self.name = base.name
        self.b = base.b
        self.subs = base.subs
        self.v = base.t[:].rearrange(pattern, **kw)

    def __getitem__(self, idx):
        return self.v[idx]


class KB:
    def __init__(self, nc, es):
        self.nc = nc
        self.es = es
        self.eng = {"pe": nc.tensor, "act": nc.scalar, "dve": nc.vector, "pool": nc.gpsimd, "sp": nc.sync}
        self.sem = {}
        for n in ["pe", "act", "dve", "pool"]:
            self.sem[n] = es.enter_context(nc.semaphore("c_" + n))
        self.cnt = {n: 0 for n in self.sem}
        self.seen = {n: {} for n in self.eng}
        self.pend = {n: ([], []) for n in self.eng}
        self.nsem = 0
        self.psrr = 0
        self.dbufs = []

    def sb(self, name, shape, dt, es=None):
        es = es or self.es
        return T(es.enter_context(self.nc.sbuf_tensor("s_" + name, list(shape), dt)), name)

    def ps(self, name, shape, dt):
        t = T(self.es.enter_context(self.nc.psum_tensor("p_" + name, list(shape), dt)), name)
        t.b.excl = True
        return t

    def dsem(self, buf):
        if buf.sem is None:
            buf.sem = self.es.enter_context(self.nc.semaphore("d_%d" % self.nsem))
            self.nsem += 1
            self.dbufs.append(buf)
        return buf.sem

    def barrier(self):
        for e in self.eng:
            for n in self.sem:
                if n != e and self.cnt[n] > self.seen[e].get(n, 0):
                    self.eng[e].wait_ge(self.sem[n], self.cnt[n])
                    self.seen[e][n] = self.cnt[n]
            for b in self.dbufs:
                key = "d:" + b.name
                if b.dtot > self.seen[e].get(key, 0):
                    self.eng[e].wait_ge(b.sem, b.dtot)
                    self.seen[e][key] = b.dtot

    @staticmethod
    def _bufs(xs):
        out = []
        for x in xs:
            out.append(x.b if isinstance(x, T) else x)
        return out

    def _wait(self, e, deps):
        for key, (sem, val) in deps.items():
            if key == e and (e == "pe" or not STRICT_SAME):
                continue
            if self.seen[e].get(key, 0) < val:
                self.eng[e].wait_ge(sem, val)
                self.seen[e][key] = val

    @staticmethod
    def _add(deps, ev):
        if ev is None:
            return
        key, sem, val = ev
        if key not in deps or deps[key][1] < val:
            deps[key] = (sem, val)

    def _deps(self, reads, writes, e=None):
        deps = {}
        for b in reads:
            self._add(deps, b.w)
        for b in writes:
            if b.w is not None and b.w[0] != e:
                self._add(deps, b.w)
            for ev in b.r.values():
                if ev[0] != e:
                    self._add(deps, ev)
        return deps

    def op(self, e, fn, r=(), w=(), inc=True):
        reads = self._bufs(r)
        writes = self._bufs(w)
        xr = [b for b in reads if b.excl]
        if xr:
            reads = [b for b in reads if not b.excl]
            writes = writes + [b for b in xr if b not in writes]
        self._wait(e, self._deps(reads, writes, e if RELAX_SAME else None))
        inst = fn()
        if not inc:
            self.pend[e][0].extend(reads)
            self.pend[e][1].extend(writes)
            return inst
        self.cnt[e] += 1
        inst.then_inc(self.sem[e], 1)
        ev = (e, self.sem[e], self.cnt[e])
        pr, pw = self.pend[e]
        for b in reads + pr:
            b.r[e] = ev
        for b in writes + pw:
            b.w = ev
            b.r = {}
        self.pend[e] = ([], [])
        return inst

    def dma(self, q, out, in_, r=(), w=(), own=None, **kw):
        reads = self._bufs(r)
        writes = self._bufs(w)
        ob = own.b if isinstance(own, T) else own
        self._wait(q, self._deps(reads, writes))
        sem = self.dsem(ob)
        ob.dtot += 16
        self.eng[q].dma_start(out=out, in_=in_, **kw).then_inc(sem, 16)
        ev = ("d:" + ob.name, sem, ob.dtot)
        for b in reads:
            b.r["d:" + ob.name] = ev
        for b in writes:
            b.w = ev
            b.r = {}

    def wait_all(self, e, bufs):
        deps = {}
        for b in self._bufs(bufs):
            self._add(deps, b.w)
            for ev in b.r.values():
                self._add(deps, ev)
        self._wait(e, deps)


def build_nc(dbg=None):
    nc = bass.Bass("TRN2", target_bir_lowering=False)

    def din(name, shape, dt=F32):
        return nc.dram_tensor(name, list(shape), dt, kind="ExternalInput").ap()

    x_d = din("x", [S, D])
    p_d = din("p", [S, 256])
    pos_d = din("pos", [1, S], I32)
    win_d = din("w_in", [128, 8, 2480])
    wout_d = din("w_out", [128, 8, 1024])
    wup_d = din("w_up", [128, 8, 4096])
    wdn_d = din("w_down", [128, 32, 1024])
    wpp_d = din("w_ple_proj", [128, 2, 1024])
    wpg_d = din("w_ple_gate", [128, 8, 1024])
    wqb_d = din("w_q_b", [128, 2, 768])
    wkvb_d = din("w_kv_b", [128, 1024])
    g_mix_d = din("g_mix", [128, D])
    g_mlp_d = din("g_mlp", [128, D])
    g_post_d = din("g_post", [128, D])
    g_gate_d = din("g_gate", [128, D])
    g_fin_d = din("g_fin", [128, D])
    g_gdn_d = din("g_gdn", [64, 512])
    cw_d = din("cw", [64, 24, 4])
    alog_d = din("alog", [64, 8])
    dtb_d = din("dtb", [64, 8])
    gq_d = din("gq", [128, 2])
    gkv_d = din("gkv", [128, 1])
    gmo_d = din("gmo", [128, 4])
    invf_d = din("invf", [96, 1])
    out_d = nc.dram_tensor("out", [S, D], F32, kind="ExternalOutput").ap()
    yT_d = nc.dram_tensor("yT_scr", [8, 128, S], BF16, kind="Internal").ap()
    h1_d = nc.dram_tensor("h1_scr", [S, D], F32, kind="Internal").ap()
    if dbg is not None:
        dbg_y = nc.dram_tensor("dbg_y", [8, 128, S], BF16, kind="ExternalOutput").ap()
        dbg_h1 = nc.dram_tensor("dbg_h1", [S, D], F32, kind="ExternalOutput").ap()
        dbg_h2 = nc.dram_tensor("dbg_h2", [S, D], F32, kind="ExternalOutput").ap()

    es = ExitStack()
    with es:
        k = KB(nc, es)
        E = es.enter_context
        yT_buf = Buf("yT_dram")
        h1_buf = Buf("h1_dram")
        ident_b = k.sb("ident_b", [128, 128], BF16)
        ident_f = k.sb("ident_f", [128, 128], F32)
        ones_f = k.sb("ones_f", [128, 128], F32)
        ones_b = k.sb("ones_b", [128, 128], BF16)
        epsb = k.sb("epsb", [128, 1], F32)
        k.op("pool", lambda: nc.gpsimd.memset(epsb[:], EPS), w=[epsb])
        oneb = k.sb("oneb", [128, 1], F32)
        k.op("pool", lambda: nc.gpsimd.memset(oneb[:], 1.0), w=[oneb])
        esAB = ExitStack()
        ident3 = k.sb("ident3", [64, 8, 64], F32, esAB)
        tri_f = k.sb("tri_f", [64, 64], F32, esAB)
        k.op("pool", lambda: nc.gpsimd.memset(ident_f[:], 0.0), w=[ident_f])
        k.op("pool", lambda: nc.gpsimd.affine_select(out=ident_f[:], in_=ident_f[:], pattern=[[-1, 128]],
                                                     compare_op=ALU.not_equal, fill=1.0, base=0,
                                                     channel_multiplier=1), r=[ident_f], w=[ident_f])
        k.op("pool", lambda: nc.gpsimd.tensor_copy(out=ident_b[:], in_=ident_f[:]), r=[ident_f], w=[ident_b])
        k.op("pool", lambda: nc.gpsimd.memset(ones_f[:], 1.0), w=[ones_f])
        k.op("pool", lambda: nc.gpsimd.memset(ones_b[:], 1.0), w=[ones_b])
        k.op("pool", lambda: nc.gpsimd.tensor_copy(
            out=ident3[:], in_=ident_f[0:64, 0:64].unsqueeze(1).to_broadcast([64, 8, 64])), r=[ident_f], w=[ident3])
        k.op("pool", lambda: nc.gpsimd.affine_select(out=tri_f[:], in_=ones_f[0:64, 0:64], pattern=[[1, 64]],
                                                     compare_op=ALU.is_ge, fill=0.0, base=0,
                                                     channel_multiplier=-1), r=[ones_f], w=[tri_f])

        def load_const(name, src, shape, dt=F32, q="sp"):
            t = k.sb(name, shape, dt, esAB)
            k.dma(q, t[:], src, w=[t], own=t)
            return t

        g_gdn = load_const("g_gdn", g_gdn_d[:, :], [64, 512])
        cw = load_const("cw", cw_d[:, :, :], [64, 24, 4])
        alog = load_const("alog", alog_d[:, :], [64, 8])
        dtb = load_const("dtb", dtb_d[:, :], [64, 8])
        gq = load_const("gq", gq_d[:, :], [128, 2])
        gkv = load_const("gkv", gkv_d[:, :], [128, 1])
        gmo = load_const("gmo", gmo_d[:, :], [128, 4])
        invf = load_const("invf", invf_d[:, :], [96, 1])
        nexpA = k.sb("nexpA", [64, 8], F32, esAB)
        k.op("act", lambda: nc.scalar.activation(out=nexpA[:], in_=alog[:], func=AF.Exp), r=[alog], w=[nexpA])
        k.op("dve", lambda: nc.vector.tensor_scalar(out=nexpA[:], in0=nexpA[:], scalar1=-1.0, scalar2=None,
                                                    op0=ALU.mult), r=[nexpA], w=[nexpA])
        REG_NEG = nc.gpsimd.to_reg(-30000.0)
        REG_ZERO = nc.gpsimd.to_reg(0.0)
        dbg_outs = {}

        def dd(name, t, ap, shape, dt):
            if dbg is None or name in dbg_outs or (isinstance(dbg, (set, list, tuple)) and name not in dbg):
                return
            o = nc.dram_tensor("dd_" + name, list(shape), dt, kind="ExternalOutput").ap()
            dbg_outs[name] = o
            k.dma("sp", o, ap, r=[t], w=[], own=Buf("dd_" + name))

        PS = [k.ps("ps%d" % i, [128, 512], F32) for i in range(6)]
        PB = [k.ps("pb%d" % i, [128, 1024], BF16) for i in range(2)]
        rr = {"f": 0, "b": 0}

        srr = {0: 0, 1: 0}

        def nps(p=None):
            if p is None:
                rr["f"] = (rr["f"] + 1) % 6
                return PS[rr["f"]]
            srr[p] = (srr[p] + 1) % 3
            return PS[3 * p + srr[p]]

        srr3 = {0: 0, 1: 0, 2: 0}

        def nps3(p):
            srr3[p] = (srr3[p] + 1) % 2
            return PS[2 * p + srr3[p]]

        def npb(p=None):
            if p is not None:
                return PB[p]
            rr["b"] = (rr["b"] + 1) % 2
            return PB[rr["b"]]

        def run_interleaved(gens, width, bg=None, bg_every=4, stagger=0):
            active = []
            it = iter(gens)
            rnd = 0
            launched = 0
            exhausted = False
            while True:
                while len(active) < width and not exhausted:
                    if launched < width and stagger and launched * stagger > rnd:
                        break
                    g = next(it, None)
                    if g is None:
                        exhausted = True
                        break
                    active.append(g)
                    launched += 1
                if not active and exhausted:
                    break
                for g in list(active):
                    try:
                        next(g)
                    except StopIteration:
                        active.remove(g)
                rnd += 1
                if bg is not None and rnd % bg_every == 0:
                    try:
                        next(bg)
                    except StopIteration:
                        bg = None
            if bg is not None:
                for _ in bg:
                    pass

        def rstd_from_ssq(ssq, n, dim):
            k.op("act", lambda: nc.scalar.activation(out=ssq[0:n, :], in_=ssq[0:n, :], func=AF.Ln,
                                                     scale=1.0 / dim, bias=epsb[0:n, :]), r=[ssq, epsb], w=[ssq])
            k.op("act", lambda: nc.scalar.activation(out=ssq[0:n, :], in_=ssq[0:n, :], func=AF.Exp, scale=-0.5),
                 r=[ssq], w=[ssq])


        def rms_tm(xin, xin_bufs, gB, out, out_bufs, junk, ssq, dim=D):
            k.op("act", lambda: nc.scalar.activation(out=junk[:, 0:dim], in_=xin, func=AF.Square,
                                                     accum_out=ssq[:, 0:1]), r=xin_bufs, w=[junk, ssq])
            rstd_from_ssq(ssq, 128, dim)
            k.op("dve", lambda: nc.vector.scalar_tensor_tensor(out=out, in0=xin, scalar=ssq[:, 0:1], in1=gB[:, 0:dim],
                                                               op0=ALU.mult, op1=ALU.mult),
                 r=list(xin_bufs) + [ssq, gB], w=out_bufs)

        def to_fm(src_bf, src_bufs, dstT, dst_bufs, col0, nkc, p=None):
            pb = npb(p)
            for kc in range(nkc):
                k.op("pe", lambda kc=kc: nc.tensor.transpose(out=pb[:, kc * 128:(kc + 1) * 128],
                                                            in_=src_bf[:, kc * 128:(kc + 1) * 128],
                                                            identity=ident_b[:]),
                     r=list(src_bufs) + [ident_b], w=[pb], inc=(kc == nkc - 1))
            k.op("act", lambda: nc.scalar.copy(
                out=dstT[:, 0:nkc, col0:col0 + 128],
                in_=pb[:, 0:nkc * 128].rearrange("p (k t) -> p k t", k=nkc)), r=[pb], w=dst_bufs)


        TWO_PI = 6.283185307179586

        def sincos_tile(t0, posi, rv, rki, rkf, rfr, rcs, outf=None):
            R = slice(64, 96)
            k.dma("sp", posi[R, :], pos_d[0:1, t0:t0 + 512].to_broadcast([32, 512]), w=[posi], own=posi)
            k.op("dve", lambda: nc.vector.tensor_copy(out=rv[R, :], in_=posi[R, :]), r=[posi], w=[rv])
            k.op("dve", lambda: nc.vector.tensor_scalar(out=rv[R, :], in0=rv[R, :], scalar1=invf[R, 0:1],
                                                        scalar2=1.0 / TWO_PI, op0=ALU.mult, op1=ALU.mult),
                 r=[rv, invf], w=[rv])
            for which in range(2):
                if which == 0:
                    k.op("dve", lambda: nc.vector.tensor_scalar(out=rfr[R, :], in0=rv[R, :], scalar1=0.25,
                                                                scalar2=None, op0=ALU.add), r=[rv], w=[rfr])
                    src = rfr
                else:
                    src = rv
                k.op("dve", lambda src=src: nc.vector.tensor_copy(out=rki[R, :], in_=src[R, :]), r=[src], w=[rki])
                k.op("dve", lambda: nc.vector.tensor_copy(out=rkf[R, :], in_=rki[R, :]), r=[rki], w=[rkf])
                k.op("dve", lambda src=src: nc.vector.tensor_tensor(out=rfr[R, :], in0=src[R, :], in1=rkf[R, :],
                                                                    op=ALU.subtract), r=[src, rkf], w=[rfr])
                k.op("dve", lambda: nc.vector.tensor_scalar(out=rkf[R, :], in0=rfr[R, :], scalar1=0.5, scalar2=None,
                                                            op0=ALU.is_gt), r=[rfr], w=[rkf])
                k.op("dve", lambda: nc.vector.tensor_tensor(out=rfr[R, :], in0=rfr[R, :], in1=rkf[R, :],
                                                            op=ALU.subtract), r=[rfr, rkf], w=[rfr])
                oap = rcs[R, which, :] if outf is None else outf(which)
                k.op("act", lambda oap=oap: nc.scalar.activation(out=oap, in_=rfr[R, :],
                                                                 func=AF.Sin, scale=TWO_PI),
                     r=[rfr], w=[rcs])

        cq_d = nc.dram_tensor("cq_scr", [2, 128, S], BF16, kind="Internal").ap()
        ckv_d = nc.dram_tensor("ckv_scr", [128, S], BF16, kind="Internal").ap()
        kpe_d = nc.dram_tensor("kpe_scr", [32, S], BF16, kind="Internal").ap()
        lat_buf = Buf("lat_dram")
        esA = ExitStack()
        with esA:
            WHT = 4
            cdl = [k.sb("cdl%d" % i, [64, 4, 64], BF16, esA) for i in range(WHT)]

            wi = k.sb("wi", [128, 8, 2480], BF16, esA)
            for kc in range(8):
                for hf in range(2):
                    k.dma("pool", wi[:, kc, hf * 1240:(hf + 1) * 1240], win_d[:, kc, hf * 1240:(hf + 1) * 1240],
                          w=[wi], own=wi.sub("ld"))
            g_mix = k.sb("g_mix", [128, D], F32, esA)
            k.dma("sp", g_mix[:], g_mix_d[:, :], w=[g_mix], own=g_mix)
            xt = [k.sb("xt%d" % i, [128, D], F32, esA) for i in range(2)]
            ssq = [k.sb("ssq%d" % i, [128, 1], F32, esA) for i in range(2)]
            ub = [k.sb("ub%d" % i, [128, D], BF16, esA) for i in range(2)]
            uT2 = [k.sb("uT%d" % i, [128, 8, 512], BF16, esA) for i in range(2)]
            pre2 = [k.sb("pre%d" % i, [64, 515], BF16, esA) for i in range(WHT)]
            pre3 = [k.sb("preb%d" % i, [64, 515], BF16, esA) for i in range(WHT)]
            halo = k.sb("halo", [64, 24, 3], BF16, esA)
            silt = [k.sb("silt%d" % i, [64, 512], BF16, esA) for i in range(WHT)]
            silv = k.sb("silv", [64, 8, 512], BF16, esA)
            sq2 = [k.sb("sq%d" % i, [64, 512], BF16, esA) for i in range(WHT)]
            rn2 = [k.sb("rn%d" % i, [64, 512], F32, esA) for i in range(WHT)]
            qk_n = k.sb("qk_n", [64, 16, 512], BF16, esA)
            rv = k.sb("rv", [128, 512], F32, esA)
            rki = k.sb("rki", [96, 512], I32, esA)
            posi = k.sb("posi", [96, 512], I32, esA)
            rkf = k.sb("rkf", [96, 512], F32, esA)
            rfr = k.sb("rfr", [96, 512], F32, esA)
            rcs = k.sb("rcs", [96, 2, 512], F32, esA)
            lsq = k.sb("lsq", [128, 3, 512], BF16, esA)
            lrs = rv
            cq_t = k.sb("cq_t", [128, 2, 512], BF16, esA)
            ckv_t = k.sb("ckv_t", [128, 512], BF16, esA)
            kpe_t = k.sb("kpe_t", [96, 512], BF16, esA)
            wrot = k.sb("wrot", [128, 8, 2, 96], BF16, esA)
            k.op("pool", lambda: nc.gpsimd.memset(wrot[:], 0.0), w=[wrot])
            k.op("act", lambda: nc.scalar.copy(out=wrot[:, :, 0, 64:96], in_=wi[:, :, 2448:2480]), r=[wi], w=[wrot])
            k.op("act", lambda: nc.scalar.mul(out=wrot[:, :, 1, 64:80], in_=wi[:, :, 2464:2480], mul=-1.0),
                 r=[wi], w=[wrot])
            k.op("act", lambda: nc.scalar.copy(out=wrot[:, :, 1, 80:96], in_=wi[:, :, 2448:2464]), r=[wi], w=[wrot])
            Sst = k.sb("Sst", [64, 8, 64], F32, esA)
            Sbf = k.sb("Sbf", [64, 8, 64], BF16, esA)
            k.op("pool", lambda: nc.gpsimd.memset(Sst[:], 0.0), w=[Sst])
            k.op("pool", lambda: nc.gpsimd.memset(Sbf[:], 0.0), w=[Sbf])
            k.op("pool", lambda: nc.gpsimd.memset(halo[:], 0.0), w=[halo])
            NSTR = 3

            def parn(name, shape, dt):
                return [k.sb("%s_%d" % (name, i), shape, dt, esA) for i in range(NSTR)]
            sm2 = parn("sm", [64, 12, 8], F32)
            Rm2 = parn("Rm", [64, 8, 64], F32)
            DT2 = parn("DT", [64, 8, 64], F32)
            DTb2 = parn("DTb", [64, 8, 64], F32)
            Z2 = [parn("Z%d" % i, [64, 8, 64], BF16) for i in range(2)]
            ZT2 = [parn("ZT%d" % i, [64, 8, 64], BF16) for i in range(2)]
            Pm2 = [parn("Pm%d" % i, [64, 8, 64], BF16) for i in range(2)]
            AT2 = parn("AT", [64, 8, 64], BF16)
            kvt2 = parn("kvt", [64, 16, 64], BF16)
            tmpf2 = Rm2
            of2 = DTb2
            zg2 = [TRe(DT2[i], "p h i -> p (h i)") for i in range(NSTR)]
            rbf2 = Z2[0]
            vnew2 = Z2[1]
            kd2 = ZT2[0]
            ybf2 = [TRe(ZT2[1][i], "p h i -> p (h i)") for i in range(NSTR)]
            scan_done = [0]
            smT = k.sb("smT", [64, 7, 8, 8], F32, esA)
            sel63 = k.sb("sel63", [64, 64], F32, esA)
            k.op("pool", lambda: nc.gpsimd.affine_select(out=sel63[:], in_=ones_f[0:64, 0:64], pattern=[[0, 64]],
                                                         compare_op=ALU.is_equal, fill=REG_ZERO, base=-63,
                                                         channel_multiplier=1), r=[ones_f], w=[sel63])
            ygT = k.sb("ygT", [128, 4, 512], BF16, esA)

            def bc(ap2):
                return ap2.unsqueeze(2).to_broadcast([64, 8, 64])

            def a1_stream(Tt):
                for blk in range(4):
                    xb = xt[blk % 2]
                    r0 = Tt * 512 + blk * 128
                    k.dma("sp", xb[:], x_d[r0:r0 + 128, :], w=[xb], own=xb)
                    rms_tm(xb[:], [xb], g_mix, ub[blk % 2][:], [ub[blk % 2]], ub[blk % 2], ssq[blk % 2])
                    yield
                    to_fm(ub[blk % 2], [ub[blk % 2]], uT2[Tt % 2], [uT2[Tt % 2]], blk * 128, 8)
                    yield

            for _ in a1_stream(0):
                pass
            for Tt in range(NT):
                t0 = Tt * 512
                uT = uT2[Tt % 2]
                def ht_stream(cp):
                    st = cp % WHT
                    cd = cdl[st]
                    rn = rn2[st]
                    sq = sq2[st]
                    bank = PS[st]
                    c0 = 2 * cp
                    for kc in range(8):
                        k.op("pe", lambda kc=kc: nc.tensor.matmul(
                            bank[:, :], lhsT=wi[:, kc, c0 * 64:c0 * 64 + 128], rhs=uT[:, kc, :],
                            start=(kc == 0), stop=(kc == 7)), r=[wi.sub("ld"), wi, uT], w=[bank], inc=(kc == 7))
                    yield
                    prs = [pre2[st], pre3[st]]
                    for sub in range(2):
                        c = c0 + sub
                        pr_ = prs[sub]
                        k.op("act", lambda c=c, pr_=pr_: nc.scalar.copy(out=pr_[:, 0:3], in_=halo[:, c, :]),
                             r=[halo], w=[pr_])
                        if sub == 0:
                            k.op("act", lambda pr_=pr_: nc.scalar.copy(out=pr_[:, 3:515], in_=bank[0:64, :]),
                                 r=[bank], w=[pr_])
                        else:
                            k.op("dve", lambda pr_=pr_: nc.vector.tensor_copy(out=pr_[:, 3:515],
                                                                              in_=bank[64:128, :]),
                                 r=[bank], w=[pr_])
                        k.op("act", lambda c=c, pr_=pr_: nc.scalar.copy(out=halo[:, c, :], in_=pr_[:, 512:515]),
                             r=[pr_], w=[halo])
                    yield
                    for sub in range(2):
                        c = c0 + sub
                        pr_ = prs[sub]
                        for kk in range(4):
                            k.op("dve", lambda kk=kk, c=c: nc.vector.tensor_scalar(
                                out=cd[:, kk, :], in0=ident_f[0:64, 0:64], scalar1=cw[:, c, kk:kk + 1], scalar2=None,
                                op0=ALU.mult), r=[ident_f, cw], w=[cd], inc=(kk == 3))
                        pc = bank
                        for kk in range(4):
                            k.op("pe", lambda kk=kk, pr_=pr_: nc.tensor.matmul(
                                pc[0:64, :], lhsT=cd[:, kk, :], rhs=pr_[:, kk:kk + 512],
                                start=(kk == 0), stop=(kk == 3)), r=[cd, pr_], w=[pc], inc=(kk == 3))
                        yield
                        k.op("act", lambda: nc.scalar.activation(out=rn[:], in_=pc[0:64, :], func=AF.Exp,
                                                                 scale=-1.0), r=[pc], w=[rn])
                        k.op("act", lambda: nc.scalar.activation(out=rn[:], in_=rn[:], func=AF.Ln,
                                                                 bias=oneb[0:64, :]), r=[rn, oneb], w=[rn])
                        k.op("act", lambda: nc.scalar.activation(out=rn[:], in_=rn[:], func=AF.Exp, scale=-1.0),
                             r=[rn], w=[rn])
                        yield
                        so_ = silt[st] if c < 16 else silv
                        so_ap = silt[st][:, :] if c < 16 else silv[:, c - 16, :]
                        k.op("dve", lambda so_ap=so_ap: nc.vector.tensor_tensor(out=so_ap, in0=pc[0:64, :],
                                                                                in1=rn[:], op=ALU.mult),
                             r=[pc, rn], w=[so_])
                        yield
                        if c < 16:
                            k.op("dve", lambda so_ap=so_ap: nc.vector.tensor_tensor(out=sq[:], in0=so_ap,
                                                                                    in1=so_ap, op=ALU.mult),
                                 r=[so_], w=[sq])
                            yield
                            pn = bank
                            k.op("pe", lambda: nc.tensor.matmul(pn[0:64, :], lhsT=ones_b[0:64, 0:64], rhs=sq[:],
                                                                start=True, stop=True), r=[ones_b, sq], w=[pn])
                            k.op("act", lambda: nc.scalar.activation(out=rn[:], in_=pn[0:64, :], func=AF.Ln,
                                                                     bias=epsb[0:64, :]), r=[pn, epsb], w=[rn])
                            k.op("act", lambda: nc.scalar.activation(out=rn[:], in_=rn[:], func=AF.Exp, scale=-0.5),
                                 r=[rn], w=[rn])
                            yield
                            sc = 0.125 if c < 8 else 1.0
                            k.op("dve", lambda c=c, sc=sc, so_ap=so_ap: nc.vector.scalar_tensor_tensor(
                                out=qk_n[:, c, :], in0=so_ap, scalar=sc, in1=rn[:], op0=ALU.mult, op1=ALU.mult),
                                r=[so_, rn], w=[qk_n])
                            yield
                run_interleaved((ht_stream(cp) for cp in range(12)), WHT, stagger=3)
                lat_cols = [(2064, 128), (2192, 128), (2320, 128)]
                plat = []
                for m, (c0, wd) in enumerate(lat_cols):
                    pp = nps()
                    plat.append(pp)
                    for kc in range(8):
                        k.op("pe", lambda kc=kc, c0=c0, pp=pp: nc.tensor.matmul(
                            pp[:, :], lhsT=wi[:, kc, c0:c0 + 128], rhs=uT[:, kc, :],
                            start=(kc == 0), stop=(kc == 7)), r=[wi, uT], w=[pp], inc=(kc == 7))
                    k.op("act", lambda m=m, pp=pp: nc.scalar.activation(out=lsq[:, m, :], in_=pp[:, :],
                                                                        func=AF.Square), r=[pp], w=[lsq])
                pn = nps()
                for m in range(2):
                    k.op("pe", lambda m=m, pn=pn: nc.tensor.matmul(pn[:, :], lhsT=ones_b[:, :], rhs=lsq[:, m, :],
                                                                   start=(m == 0), stop=(m == 1)),
                         r=[ones_b, lsq], w=[pn], inc=(m == 1))
                k.op("act", lambda pn=pn: nc.scalar.activation(out=lrs[:], in_=pn[:, :], func=AF.Ln, scale=1.0 / 256,
                                                               bias=epsb[:, :]), r=[pn, epsb], w=[lrs])
                k.op("act", lambda: nc.scalar.activation(out=lrs[:], in_=lrs[:], func=AF.Exp, scale=-0.5),
                     r=[lrs], w=[lrs])
                for m in range(2):
                    k.op("dve", lambda m=m: nc.vector.scalar_tensor_tensor(
                        out=cq_t[:, m, :], in0=plat[m][:, :], scalar=gq[:, m:m + 1], in1=lrs[:],
                        op0=ALU.mult, op1=ALU.mult), r=[plat[m], gq, lrs], w=[cq_t])
                pn = nps()
                k.op("pe", lambda pn=pn: nc.tensor.matmul(pn[:, :], lhsT=ones_b[:, :], rhs=lsq[:, 2, :],
                                                          start=True, stop=True), r=[ones_b, lsq], w=[pn])
                k.op("act", lambda pn=pn: nc.scalar.activation(out=lrs[:], in_=pn[:, :], func=AF.Ln, scale=1.0 / 128,
                                                               bias=epsb[:, :]), r=[pn, epsb], w=[lrs])
                k.op("act", lambda: nc.scalar.activation(out=lrs[:], in_=lrs[:], func=AF.Exp, scale=-0.5),
                     r=[lrs], w=[lrs])
                k.op("dve", lambda: nc.vector.scalar_tensor_tensor(
                    out=ckv_t[:], in0=plat[2][:, :], scalar=gkv[:, 0:1], in1=lrs[:],
                    op0=ALU.mult, op1=ALU.mult), r=[plat[2], gkv, lrs], w=[ckv_t])
                sincos_tile(t0, posi, rv, rki, rkf, rfr, rcs)
                pks = []
                for m in range(2):
                    pp = nps()
                    pks.append(pp)
                    for kc in range(8):
                        lw = wrot[:, kc, m, :]
                        k.op("pe", lambda kc=kc, pp=pp, lw=lw: nc.tensor.matmul(
                            pp[0:96, :], lhsT=lw, rhs=uT[:, kc, :], start=(kc == 0), stop=(kc == 7)),
                            r=[wi, wrot, uT], w=[pp], inc=(kc == 7))
                k.op("dve", lambda: nc.vector.tensor_tensor(out=rkf[64:96, :], in0=pks[0][64:96, :],
                                                            in1=rcs[64:96, 0, :], op=ALU.mult),
                     r=[pks[0], rcs], w=[rkf])
                k.op("dve", lambda: nc.vector.tensor_tensor(out=rfr[64:96, :], in0=pks[1][64:96, :],
                                                            in1=rcs[64:96, 1, :], op=ALU.mult),
                     r=[pks[1], rcs], w=[rfr])
                k.op("dve", lambda: nc.vector.tensor_tensor(out=kpe_t[64:96, :], in0=rkf[64:96, :],
                                                            in1=rfr[64:96, :], op=ALU.add),
                     r=[rkf, rfr], w=[kpe_t])
                for m in range(2):
                    k.dma("act", cq_d[m, :, t0:t0 + 512], cq_t[:, m, :], r=[cq_t], w=[lat_buf], own=cq_t.sub("st"))
                k.dma("act", ckv_d[:, t0:t0 + 512], ckv_t[:], r=[ckv_t], w=[lat_buf], own=ckv_t.sub("st"))
                k.dma("act", kpe_d[:, t0:t0 + 512], kpe_t[64:96, :], r=[kpe_t], w=[lat_buf], own=kpe_t.sub("st"))
                pa = nps()
                for n in range(8):
                    for kc in range(8):
                        k.op("pe", lambda kc=kc, n=n, pa=pa: nc.tensor.matmul(
                            pa[0:64, n * 16:(n + 1) * 16], lhsT=uT[:, kc, n * 64:(n + 1) * 64],
                            rhs=wi[:, kc, 2048:2064], start=(kc == 0), stop=(kc == 7)),
                            r=[wi, uT], w=[pa], inc=(kc == 7 and n == 7))
                pa3 = pa[0:64, 0:128].rearrange("p (n c) -> p n c", c=16)
                bT, gT, eT, edT, sdT, t1T, glT = [smT[:, j, :, :] for j in range(7)]
                k.op("act", lambda: nc.scalar.activation(out=bT, in_=pa3[:, :, 0:8], func=AF.Exp, scale=-1.0),
                     r=[pa], w=[smT])
                k.op("dve", lambda: nc.vector.tensor_tensor(
                    out=t1T, in0=pa3[:, :, 8:16], in1=dtb[:].unsqueeze(1).to_broadcast([64, 8, 8]), op=ALU.add),
                    r=[pa, dtb], w=[smT])
                k.op("dve", lambda: nc.vector.tensor_scalar(out=bT, in0=bT, scalar1=1.0, scalar2=None, op0=ALU.add),
                     r=[smT], w=[smT])
                k.op("dve", lambda: nc.vector.reciprocal(out=bT, in_=bT), r=[smT], w=[smT])
                k.op("act", lambda: nc.scalar.activation(out=t1T, in_=t1T, func=AF.Exp), r=[smT], w=[smT])
                k.op("act", lambda: nc.scalar.activation(out=t1T, in_=t1T, func=AF.Ln, bias=oneb[0:64, :]),
                     r=[smT, oneb], w=[smT])
                k.op("dve", lambda: nc.vector.tensor_tensor(
                    out=t1T, in0=t1T, in1=nexpA[:].unsqueeze(1).to_broadcast([64, 8, 8]), op=ALU.mult),
                    r=[smT, nexpA], w=[smT])
                pg = nps()
                k.op("pe", lambda: nc.tensor.matmul(pg[0:64, 0:64], lhsT=tri_f[:, :],
                                                    rhs=smT[:, 5, :, :].rearrange("p n h -> p (n h)"),
                                                    start=True, stop=True), r=[tri_f, smT], w=[pg])
                k.op("dve", lambda: nc.vector.tensor_copy(
                    out=gT, in_=pg[0:64, 0:64].rearrange("p (n h) -> p n h", h=8)), r=[pg], w=[smT])
                k.op("act", lambda: nc.scalar.activation(out=eT, in_=gT, func=AF.Exp), r=[smT], w=[smT])
                pl = nps()
                k.op("pe", lambda: nc.tensor.matmul(pl[0:64, 0:64], lhsT=sel63[:, :],
                                                    rhs=smT[:, 1, :, :].rearrange("p n h -> p (n h)"),
                                                    start=True, stop=True), r=[sel63, smT], w=[pl])
                k.op("dve", lambda: nc.vector.tensor_copy(
                    out=glT, in_=pl[0:64, 0:64].rearrange("p (n h) -> p n h", h=8)), r=[pl], w=[smT])
                k.op("act", lambda: nc.scalar.activation(out=sdT, in_=glT, func=AF.Exp), r=[smT], w=[smT])
                k.op("dve", lambda: nc.vector.tensor_tensor(out=edT, in0=glT, in1=gT, op=ALU.subtract),
                     r=[smT], w=[smT])
                k.op("act", lambda: nc.scalar.activation(out=edT, in_=edT, func=AF.Exp), r=[smT], w=[smT])
                def chunk_stream(n):
                    par = n % NSTR
                    gidx = Tt * 8 + n
                    sm = sm2[par]; Rm = Rm2[par]; Dm = Rm; DT = DT2[par]; DTb = DTb2[par]; Xf = DTb
                    Z = [Z2[0][par], Z2[1][par]]; ZT = [ZT2[0][par], ZT2[1][par]]; Pm = [Pm2[0][par], Pm2[1][par]]
                    AT = AT2[par]; kvt = kvt2[par]; kd = kd2[par]; tmpf = tmpf2[par]; rbf = rbf2[par]
                    vnew = vnew2[par]; of = of2[par]; zg = zg2[par]; ybf = ybf2[par]
                    cs = slice(n * 64, n * 64 + 64)
                    bcol = smT[:, 0, n, :]
                    gcol = smT[:, 1, n, :]
                    ecol = smT[:, 2, n, :]
                    edc = smT[:, 3, n, :]
                    sdc = smT[:, 4, n, :]
                    SB = SG = SE = SED = SSD = smT
                    yield
                    k.op("pool", lambda: nc.gpsimd.tensor_tensor(out=Rm[:], in0=ident3[:], in1=bc(gcol), op=ALU.mult),
                         r=[ident3, smT], w=[Rm])
                    pG = nps3(par)
                    yield
                    k.op("pe", lambda pG=pG: nc.tensor.matmul(pG[0:64, :], lhsT=ones_f[0:64, 0:64],
                                                              rhs=Rm[:].rearrange("p h i -> p (h i)"),
                                                              start=True, stop=True), r=[ones_f, Rm], w=[pG])
                    pG3 = pG[0:64, :].rearrange("p (h i) -> p h i", h=8)
                    yield
                    k.op("dve", lambda pG3=pG3: nc.vector.tensor_tensor(out=Dm[:], in0=pG3, in1=bc(gcol),
                                                                        op=ALU.subtract),
                         r=[pG, smT], w=[Dm])
                    yield
                    k.op("pool", lambda: nc.gpsimd.affine_select(
                        out=Dm[:], in_=Dm[:], pattern=[[0, 8], [1, 64]], compare_op=ALU.is_ge, fill=REG_NEG,
                        base=0, channel_multiplier=-1), r=[Dm], w=[Dm])
                    yield
                    k.op("act", lambda: nc.scalar.activation(out=DT[:], in_=Dm[:], func=AF.Exp), r=[Dm], w=[DT])
                    yield
                    k.op("pool", lambda: nc.gpsimd.tensor_tensor(out=DTb[:], in0=DT[:], in1=bc(bcol), op=ALU.mult),
                         r=[DT, smT], w=[DTb])
                    pK = nps3(par)
                    pQ = nps3(par)
                    yield
                    for h in range(8):
                        k.op("pe", lambda h=h, pK=pK: nc.tensor.matmul(
                            pK[0:64, h * 64:(h + 1) * 64], lhsT=qk_n[:, 8 + h, cs], rhs=qk_n[:, 8 + h, cs],
                            start=True, stop=True), r=[qk_n], w=[pK], inc=(h == 7))
                    yield
                    for h in range(8):
                        k.op("pe", lambda h=h, pQ=pQ: nc.tensor.matmul(
                            pQ[0:64, h * 64:(h + 1) * 64], lhsT=qk_n[:, 8 + h, cs], rhs=qk_n[:, h, cs],
                            start=True, stop=True), r=[qk_n], w=[pQ], inc=(h == 7))
                    yield
                    k.op("dve", lambda pK=pK: nc.vector.tensor_tensor(
                        out=Xf[:], in0=pK[0:64, :].rearrange("p (h i) -> p h i", h=8), in1=DTb[:], op=ALU.mult),
                        r=[pK, DTb], w=[Xf])
                    yield
                    k.op("pool", lambda: nc.gpsimd.affine_select(
                        out=Z[0][:], in_=Xf[:], pattern=[[0, 8], [1, 64]], compare_op=ALU.is_gt, fill=REG_ZERO,
                        base=0, channel_multiplier=-1), r=[Xf], w=[Z[0]])
                    yield
                    k.op("dve", lambda pQ=pQ: nc.vector.tensor_tensor(
                        out=AT[:], in0=pQ[0:64, :].rearrange("p (h i) -> p h i", h=8), in1=DT[:], op=ALU.mult),
                        r=[pQ, DT], w=[AT])
                    yield
                    pb = npb()
                    for h in range(8):
                        k.op("pe", lambda h=h, pb=pb: nc.tensor.transpose(
                            out=pb[0:64, h * 64:(h + 1) * 64], in_=qk_n[:, 8 + h, cs], identity=ident_b[0:64, 0:64]),
                            r=[qk_n, ident_b], w=[pb], inc=False)
                    for h in range(8):
                        k.op("pe", lambda h=h, pb=pb: nc.tensor.transpose(
                            out=pb[0:64, (8 + h) * 64:(9 + h) * 64], in_=silv[:, h, cs],
                            identity=ident_b[0:64, 0:64]), r=[silv, ident_b], w=[pb], inc=(h == 7))
                    k.op("act", lambda pb=pb: nc.scalar.copy(
                        out=kvt[:], in_=pb[0:64, :].rearrange("p (c d) -> p c d", c=16)), r=[pb], w=[kvt])
                    yield
                    pb = npb()
                    for h in range(8):
                        k.op("pe", lambda h=h, pb=pb: nc.tensor.transpose(
                            out=pb[0:64, h * 64:(h + 1) * 64], in_=Z[0][:, h, :], identity=ident_b[0:64, 0:64]),
                            r=[Z[0], ident_b], w=[pb], inc=(h == 7))
                    k.op("act", lambda pb=pb: nc.scalar.copy(
                        out=ZT[0][:], in_=pb[0:64, 0:512].rearrange("p (h i) -> p h i", h=8)), r=[pb], w=[ZT[0]])
                    yield
                    k.op("pool", lambda: nc.gpsimd.tensor_tensor(out=Pm[0][:], in0=ident3[:], in1=Z[0][:],
                                                                 op=ALU.subtract), r=[ident3, Z[0]], w=[Pm[0]])
                    cur = 0
                    yield
                    for lev in range(1, 6):
                        nxt = 1 - cur
                        yield
                        pzt = nps3(par)
                        for h in range(8):
                            k.op("pe", lambda h=h, pzt=pzt, cur=cur: nc.tensor.matmul(
                                pzt[0:64, h * 64:(h + 1) * 64], lhsT=Z[cur][:, h, :], rhs=ZT[cur][:, h, :],
                                start=True, stop=True), r=[Z[cur], ZT[cur]], w=[pzt], inc=(h == 7))
                        if lev < 5:
                            pz = nps3(par)
                            for h in range(8):
                                k.op("pe", lambda h=h, pz=pz, cur=cur: nc.tensor.matmul(
                                    pz[0:64, h * 64:(h + 1) * 64], lhsT=ZT[cur][:, h, :], rhs=Z[cur][:, h, :],
                                    start=True, stop=True), r=[Z[cur], ZT[cur]], w=[pz], inc=(h == 7))
                        yield
                        k.op("act", lambda pzt=pzt, nxt=nxt: nc.scalar.copy(
                            out=ZT[nxt][:], in_=pzt[0:64, :].rearrange("p (h i) -> p h i", h=8)),
                            r=[pzt], w=[ZT[nxt]])
                        if lev < 5:
                            k.op("dve", lambda pz=pz, nxt=nxt: nc.vector.tensor_copy(
                                out=Z[nxt][:], in_=pz[0:64, :].rearrange("p (h i) -> p h i", h=8)),
                                r=[pz], w=[Z[nxt]])
                        yield
                        pp = nps3(par)
                        for h in range(8):
                            k.op("pe", lambda h=h, pp=pp, nxt=nxt, cur=cur: nc.tensor.matmul(
                                pp[0:64, h * 64:(h + 1) * 64], lhsT=ZT[nxt][:, h, :], rhs=Pm[cur][:, h, :],
                                start=True, stop=False), r=[ZT[nxt], Pm[cur]], w=[pp], inc=False)
                            k.op("pe", lambda h=h, pp=pp, nxt=nxt, cur=cur: nc.tensor.matmul(
                                pp[0:64, h * 64:(h + 1) * 64], lhsT=ident_b[0:64, 0:64], rhs=Pm[cur][:, h, :],
                                start=False, stop=True), r=[ident_b, Pm[cur]], w=[pp], inc=(h == 7))
                        yield
                        k.op("act", lambda pp=pp, nxt=nxt, cur=cur: nc.scalar.copy(
                            out=Pm[nxt][:], in_=pp[0:64, :].rearrange("p (h i) -> p h i", h=8)),
                            r=[pp], w=[Pm[nxt]])
                        cur = nxt
                    G = Pm[cur]
                    yield
                    k.op("pool", lambda: nc.gpsimd.tensor_tensor(out=kd[:], in0=kvt[:, 0:8, :], in1=bc(edc),
                                                                 op=ALU.mult), r=[kvt, smT], w=[kd])
                    while scan_done[0] < gidx:
                        yield
                    pS = nps3(par)
                    for h in range(8):
                        k.op("pe", lambda h=h, pS=pS: nc.tensor.matmul(
                            pS[0:64, h * 64:(h + 1) * 64], lhsT=qk_n[:, 8 + h, cs], rhs=Sbf[:, h, :],
                            start=True, stop=True), r=[qk_n, Sbf], w=[pS], inc=(h == 7))
                    pO1 = nps3(par)
                    for h in range(8):
                        k.op("pe", lambda h=h, pO1=pO1: nc.tensor.matmul(
                            pO1[0:64, h * 64:(h + 1) * 64], lhsT=qk_n[:, h, cs], rhs=Sbf[:, h, :],
                            start=True, stop=True), r=[qk_n, Sbf], w=[pO1], inc=(h == 7))
                    k.op("dve", lambda pS=pS: nc.vector.tensor_tensor(
                        out=tmpf[:], in0=pS[0:64, :].rearrange("p (h i) -> p h i", h=8), in1=bc(ecol), op=ALU.mult),
                        r=[pS, smT], w=[tmpf])
                    k.op("dve", lambda: nc.vector.tensor_tensor(out=rbf[:], in0=kvt[:, 8:16, :], in1=tmpf[:],
                                                                op=ALU.subtract), r=[kvt, tmpf], w=[rbf])
                    yield
                    pT_ = nps3(par)
                    for h in range(8):
                        k.op("pe", lambda h=h, pT_=pT_: nc.tensor.matmul(
                            pT_[0:64, h * 64:(h + 1) * 64], lhsT=G[:, h, :], rhs=rbf[:, h, :],
                            start=True, stop=True), r=[G, rbf], w=[pT_], inc=(h == 7))
                    k.op("dve", lambda pT_=pT_: nc.vector.tensor_tensor(
                        out=vnew[:], in0=pT_[0:64, :].rearrange("p (h i) -> p h i", h=8), in1=bc(bcol), op=ALU.mult),
                        r=[pT_, smT], w=[vnew])
                    k.op("dve", lambda pO1=pO1: nc.vector.tensor_tensor(
                        out=of[:], in0=pO1[0:64, :].rearrange("p (h i) -> p h i", h=8), in1=bc(ecol), op=ALU.mult),
                        r=[pO1, smT], w=[of])
                    k.op("dve", lambda: nc.vector.tensor_tensor(out=Sst[:], in0=Sst[:], in1=bc(sdc), op=ALU.mult),
                         r=[Sst, smT], w=[Sst])
                    yield
                    pU = nps3(par)
                    for h in range(8):
                        k.op("pe", lambda h=h, pU=pU: nc.tensor.matmul(
                            pU[0:64, h * 64:(h + 1) * 64], lhsT=kd[:, h, :], rhs=vnew[:, h, :],
                            start=True, stop=True), r=[kd, vnew], w=[pU], inc=(h == 7))
                    k.op("dve", lambda pU=pU: nc.vector.tensor_tensor(
                        out=Sbf[:], in0=pU[0:64, :].rearrange("p (h i) -> p h i", h=8), in1=Sst[:], op=ALU.add),
                        r=[pU, Sst], w=[Sbf])
                    scan_done[0] = gidx + 1
                    k.op("dve", lambda pU=pU: nc.vector.tensor_tensor(
                        out=Sst[:], in0=pU[0:64, :].rearrange("p (h i) -> p h i", h=8), in1=Sst[:], op=ALU.add),
                        r=[pU, Sst], w=[Sst])
                    yield
                    pO2 = nps3(par)
                    for h in range(8):
                        k.op("pe", lambda h=h, pO2=pO2: nc.tensor.matmul(
                            pO2[0:64, h * 64:(h + 1) * 64], lhsT=AT[:, h, :], rhs=vnew[:, h, :],
                            start=True, stop=True), r=[AT, vnew], w=[pO2], inc=(h == 7))
                    yield
                    k.op("dve", lambda pO2=pO2: nc.vector.tensor_tensor(
                        out=of[:], in0=pO2[0:64, :].rearrange("p (h i) -> p h i", h=8), in1=of[:], op=ALU.add),
                        r=[pO2, of], w=[of])
                    yield
                    k.op("pool", lambda: nc.gpsimd.tensor_tensor(out=tmpf[:], in0=of[:], in1=of[:], op=ALU.mult),
                         r=[of], w=[tmpf])
                    osq = sm[:, 7, :]
                    yield
                    k.op("dve", lambda: nc.vector.tensor_reduce(out=osq, in_=tmpf[:], axis=AX.X, op=ALU.add),
                         r=[tmpf], w=[sm.sub("osq")])
                    yield
                    k.op("act", lambda: nc.scalar.activation(out=osq, in_=osq, func=AF.Ln, scale=1.0 / 64,
                                                             bias=epsb[0:64, :]),
                         r=[sm.sub("osq"), epsb], w=[sm.sub("osq")])
                    yield
                    k.op("act", lambda: nc.scalar.activation(out=osq, in_=osq, func=AF.Exp, scale=-0.5),
                         r=[sm.sub("osq")], w=[sm.sub("osq")])
                    yield
                    k.op("dve", lambda: nc.vector.tensor_tensor(out=of[:], in0=of[:], in1=bc(osq), op=ALU.mult),
                         r=[of, sm.sub("osq")], w=[of])
                    pz_ = nps3(par)
                    yield
                    for kc in range(8):
                        k.op("pe", lambda kc=kc, pz_=pz_: nc.tensor.matmul(
                            pz_[0:64, :], lhsT=uT[:, kc, cs], rhs=wi[:, kc, 1536:2048],
                            start=(kc == 0), stop=(kc == 7)), r=[wi, uT], w=[pz_], inc=(kc == 7))
                    yield
                    k.op("act", lambda pz_=pz_: nc.scalar.activation(out=zg[:], in_=pz_[0:64, :], func=AF.Exp,
                                                                     scale=-1.0), r=[pz_], w=[zg])
                    yield
                    k.op("act", lambda: nc.scalar.activation(out=zg[:], in_=zg[:], func=AF.Ln, bias=oneb[0:64, :]),
                         r=[zg, oneb], w=[zg])
                    yield
                    k.op("act", lambda: nc.scalar.activation(out=zg[:], in_=zg[:], func=AF.Exp, scale=-1.0),
                         r=[zg], w=[zg])
                    yield
                    k.op("dve", lambda pz_=pz_: nc.vector.tensor_tensor(out=zg[:], in0=pz_[0:64, :], in1=zg[:],
                                                                        op=ALU.mult), r=[pz_, zg], w=[zg])
                    yield
                    k.op("pool", lambda: nc.gpsimd.tensor_tensor(out=zg[:], in0=zg[:], in1=g_gdn[:], op=ALU.mult),
                         r=[zg, g_gdn], w=[zg])
                    yield
                    k.op("dve", lambda: nc.vector.tensor_tensor(
                        out=ybf[:], in0=of[:].rearrange("p h i -> p (h i)"), in1=zg[:], op=ALU.mult),
                        r=[of, zg], w=[ybf])
                    yield
                    pb = npb()
                    for m in range(4):
                        k.op("pe", lambda m=m, pb=pb: nc.tensor.transpose(
                            out=pb[:, m * 64:(m + 1) * 64], in_=ybf[:, m * 128:(m + 1) * 128],
                            identity=ident_b[0:64, 0:64]), r=[ybf, ident_b], w=[pb], inc=(m == 3))
                    k.op("act", lambda pb=pb: nc.scalar.copy(
                        out=ygT[:, :, cs], in_=pb[:, 0:256].rearrange("p (m t) -> p m t", m=4)), r=[pb], w=[ygT])
                run_interleaved((chunk_stream(n) for n in range(8)), NSTR,
                                bg=(a1_stream(Tt + 1) if Tt + 1 < NT else None), stagger=12)
                for m in range(4):
                    k.dma("act", yT_d[m, :, t0:t0 + 512], ygT[:, m, :], r=[ygT], w=[yT_buf], own=ygT.sub("st"))
            k.wait_all("sp", [ygT, cq_t, ckv_t, kpe_t])

        h2_d = nc.dram_tensor("h2_scr", [S, D], F32, kind="Internal").ap()
        h2_buf = Buf("h2_dram")
        SCALE = 96.0 ** -0.5
        k.barrier()
        esB = ExitStack()
        with esB:
            cqT = k.sb("cqT", [128, 2, S], BF16, esB)
            ckvT = k.sb("ckvT", [128, S], BF16, esB)
            kper = k.sb("kper", [96, S], BF16, esB)
            for m in range(2):
                k.dma("sp", cqT[:, m, :], cq_d[m, :, :], r=[lat_buf], w=[cqT], own=cqT)
            k.dma("sp", ckvT[:], ckv_d[:, :], r=[lat_buf], w=[ckvT], own=ckvT)
            k.dma("sp", kper[64:96, :], kpe_d[:, :], r=[lat_buf], w=[kper], own=kper)
            wqb = k.sb("wqb", [128, 2, 768], BF16, esB)
            k.dma("pool", wqb[:], wqb_d[:, :, :], w=[wqb], own=wqb)
            wkvb = k.sb("wkvb", [128, 1024], BF16, esB)
            k.dma("pool", wkvb[:], wkvb_d[:, :], w=[wkvb], own=wkvb)
            wqrot = k.sb("wqrot", [128, 2, 8, 96], BF16, esB)
            wq4 = wqb[:].rearrange("p m (h c) -> p m h c", c=96)
            k.op("pool", lambda: nc.gpsimd.memset(wqrot[:], 0.0), w=[wqrot])
            k.op("act", lambda: nc.scalar.mul(out=wqrot[:, :, :, 64:80], in_=wq4[:, :, :, 80:96], mul=-1.0),
                 r=[wqb], w=[wqrot])
            k.op("act", lambda: nc.scalar.copy(out=wqrot[:, :, :, 80:96], in_=wq4[:, :, :, 64:80]), r=[wqb], w=[wqrot])
            cst = k.sb("cst", [96, 2, S], F32, esB)
            esB0 = ExitStack()
            b_rv = k.sb("b_rv", [96, 512], F32, esB0)
            b_rki = k.sb("b_rki", [96, 512], I32, esB0)
            b_rkf = k.sb("b_rkf", [96, 512], F32, esB0)
            b_rfr = k.sb("b_rfr", [96, 512], F32, esB0)
            b_posi = k.sb("b_posi", [96, 512], I32, esB0)
            for Tt in range(NT):
                sincos_tile(Tt * 512, b_posi, b_rv, b_rki, b_rkf, b_rfr, cst,
                            outf=lambda which, Tt=Tt: cst[64:96, which, Tt * 512:(Tt + 1) * 512])
            k.barrier()
            esB0.close()
            Vt = k.sb("Vt", [128, 32, 8, 65], BF16, esB)
            k.op("pool", lambda: nc.gpsimd.memset(Vt[:, :, :, 64:65], 1.0), w=[Vt])
            wkv3 = wkvb[:].rearrange("p (h c) -> p h c", c=128)
            for kb in range(32):
                pv = nps()
                k.op("pe", lambda kb=kb, pv=pv: nc.tensor.matmul(
                    pv[:, :], lhsT=ckvT[:, kb * 128:(kb + 1) * 128], rhs=wkv3[:, :, 64:128], start=True, stop=True),
                    r=[ckvT, wkvb], w=[pv])
                k.op("act", lambda kb=kb, pv=pv: nc.scalar.copy(
                    out=Vt[:, kb, :, 0:64], in_=pv[:, :].rearrange("p (h c) -> p h c", c=64)), r=[pv], w=[Vt])
            Qh2 = [k.sb("Qh%d" % i, [96, S], BF16, esB) for i in range(2)]
            Kh2 = [k.sb("Kh%d" % i, [96, S], BF16, esB) for i in range(2)]
            for i in range(2):
                k.op("dve", lambda i=i: nc.vector.tensor_copy(out=Kh2[i][64:96, :], in_=kper[64:96, :]),
                     r=[kper], w=[Kh2[i]])
            PT = [k.sb("PT%d" % i, [128, 512], BF16, esB) for i in range(5)]
            osb2 = [k.sb("osb%d" % i, [65, 512], F32, esB) for i in range(2)]
            rec = k.sb("rec", [64, 512], F32, esB)
            yh = k.sb("yh", [64, 512], F32, esB)
            yaT = k.sb("yaT", [128, 4, S], BF16, esB)
            sel = k.sb("sel", [65, 64], F32, esB)
            k.op("pool", lambda: nc.gpsimd.memset(sel[:], 0.0), w=[sel])
            k.op("pool", lambda: nc.gpsimd.memset(sel[64:65, :], 1.0), w=[sel])
            qt1 = k.sb("qt1", [96, 512], F32, esB)
            qt2 = k.sb("qt2", [96, 512], F32, esB)

            def prep_tile(h, Tt):
                Qh, Kh = Qh2[h % 2], Kh2[h % 2]
                ts_ = slice(Tt * 512, (Tt + 1) * 512)
                pk = nps_p()
                k.op("pe", lambda: nc.tensor.matmul(
                    pk[0:64, :], lhsT=wkvb[:, h * 128:h * 128 + 64], rhs=ckvT[:, ts_], start=True, stop=True),
                    r=[wkvb, ckvT], w=[pk])
                k.op("dve", lambda: nc.vector.tensor_copy(out=Kh[0:64, ts_], in_=pk[0:64, :]), r=[pk], w=[Kh])
                pq = nps_p()
                for m in range(2):
                    k.op("pe", lambda m=m: nc.tensor.matmul(
                        pq[0:96, :], lhsT=wqb[:, m, h * 96:(h + 1) * 96], rhs=cqT[:, m, ts_],
                        start=(m == 0), stop=(m == 1)), r=[wqb, cqT], w=[pq], inc=(m == 1))
                k.op("dve", lambda: nc.vector.tensor_copy(out=Qh[0:64, ts_], in_=pq[0:64, :]), r=[pq], w=[Qh])
                k.op("dve", lambda: nc.vector.tensor_tensor(
                    out=qt1[64:96, :], in0=pq[64:96, :], in1=cst[64:96, 0, ts_], op=ALU.mult),
                    r=[pq, cst], w=[qt1])
                pr2 = nps_p()
                for m in range(2):
                    k.op("pe", lambda m=m: nc.tensor.matmul(
                        pr2[0:96, :], lhsT=wqrot[:, m, h, :], rhs=cqT[:, m, ts_],
                        start=(m == 0), stop=(m == 1)), r=[wqrot, cqT], w=[pr2], inc=(m == 1))
                k.op("dve", lambda: nc.vector.tensor_tensor(
                    out=qt2[64:96, :], in0=pr2[64:96, :], in1=cst[64:96, 1, ts_], op=ALU.mult),
                    r=[pr2, cst], w=[qt2])
                k.op("dve", lambda: nc.vector.tensor_tensor(
                    out=Qh[64:96, ts_], in0=qt1[64:96, :], in1=qt2[64:96, :], op=ALU.add),
                    r=[qt1, qt2], w=[Qh])

            sc_rr = [0]
            pr_rr = [0]
            PREPB = [TView(PB[0], F32), TView(PB[1], F32)]

            def nps_b():
                sc_rr[0] = (sc_rr[0] + 1) % 4
                return PS[sc_rr[0]]

            def nps_p():
                pr_rr[0] = (pr_rr[0] + 1) % 2
                return PREPB[pr_rr[0]]

            for Tt in range(NT):
                prep_tile(0, Tt)
            ptr = [0]
            for h in range(8):
                Qh, Kh = Qh2[h % 2], Kh2[h % 2]
                steps = []
                for Qt in range(NT):
                    for kb in range(4 * Qt + 4):
                        steps.append((Qt, kb))
                nst = len(steps)
                info = {}

                def emit_qk(i):
                    Qt, kb = steps[i]
                    d = kb - 4 * Qt
                    c0 = 128 * d if d > 0 else 0
                    qs = slice(Qt * 512 + c0, (Qt + 1) * 512)
                    cs_ = slice(c0, 512)
                    sp_ = nps_b()
                    info[i] = (sp_, cs_, c0, d)
                    k.op("pe", lambda: nc.tensor.matmul(
                        sp_[:, cs_], lhsT=Kh[0:96, kb * 128:(kb + 1) * 128], rhs=Qh[0:96, qs],
                        start=True, stop=True), r=[Kh, Qh], w=[sp_])

                def emit_rest(i):
                    Qt, kb = steps[i]
                    sp_, cs_, c0, d = info.pop(i)
                    nkb = 4 * Qt + 4
                    po = PS[4 + Qt % 2]
                    pt = PT[ptr[0] % 5]
                    ptr[0] += 1
                    k.op("act", lambda: nc.scalar.activation(
                        out=pt[:, cs_], in_=sp_[:, cs_], func=AF.Exp, scale=SCALE), r=[sp_], w=[pt])
                    if d >= 0:
                        k.op("pool", lambda: nc.gpsimd.affine_select(
                            out=pt[:, c0:c0 + 128], in_=pt[:, c0:c0 + 128], pattern=[[1, 128]],
                            compare_op=ALU.is_ge, fill=REG_ZERO, base=0, channel_multiplier=-1),
                            r=[pt], w=[pt])
                    k.op("pe", lambda: nc.tensor.matmul(
                        po[0:65, cs_], lhsT=Vt[:, kb, h, :], rhs=pt[:, cs_], start=(kb == 0),
                        stop=(kb == nkb - 1)), r=[Vt, pt], w=[po], inc=True)
                    if kb == nkb - 1:
                        osb = osb2[Qt % 2]
                        k.op("dve", lambda: nc.vector.tensor_copy(out=osb[:], in_=po[0:65, :]), r=[po], w=[osb])
                        dd("osb", osb, osb[:], [65, 512], F32)
                        return Qt
                    return None

                def finalize(Qt):
                    osb = osb2[Qt % 2]
                    pd = PS[4 + Qt % 2]
                    k.op("pe", lambda: nc.tensor.matmul(pd[0:64, :], lhsT=sel[:, :], rhs=osb[:, :],
                                                        start=True, stop=True), r=[sel, osb], w=[pd])
                    k.op("dve", lambda: nc.vector.reciprocal(out=rec[:], in_=pd[0:64, :]), r=[pd], w=[rec])
                    k.op("dve", lambda: nc.vector.tensor_tensor(out=yh[:], in0=osb[0:64, :], in1=rec[:],
                                                                op=ALU.mult), r=[osb, rec], w=[yh])
                    p0 = (h % 2) * 64
                    k.op("dve", lambda: nc.vector.tensor_copy(
                        out=yaT[p0:p0 + 64, h // 2, Qt * 512:(Qt + 1) * 512], in_=yh[:]), r=[yh], w=[yaT])

                LOOK = 3
                for i in range(min(LOOK, nst)):
                    emit_qk(i)
                pending_fin = []
                prep_next = list(range(NT)) if h < 7 else []
                for i in range(nst):
                    if i + LOOK < nst:
                        emit_qk(i + LOOK)
                    fin = emit_rest(i)
                    pending_fin = [(q, c - 1) for (q, c) in pending_fin]
                    while pending_fin and pending_fin[0][1] <= 0:
                        finalize(pending_fin.pop(0)[0])
                    if fin is not None:
                        pending_fin.append((fin, 2))
                    if prep_next and i % 16 == 8:
                        prep_tile(h + 1, prep_next.pop(0))
                for q, c in pending_fin:
                    finalize(q)
                for Tt in prep_next:
                    prep_tile(h + 1, Tt)
            dd("yaT", yaT, yaT[:], [128, 4, S], BF16)
            ysq = k.sb("ysq", [128, 4, 512], BF16, esB)
            yrs = k.sb("yrs", [128, 512], F32, esB)
            yo = k.sb("yo", [128, 4, 512], BF16, esB)
            for Tt in range(NT):
                ts_ = slice(Tt * 512, (Tt + 1) * 512)
                k.op("dve", lambda ts_=ts_: nc.vector.tensor_tensor(out=ysq[:], in0=yaT[:, :, ts_],
                                                                    in1=yaT[:, :, ts_], op=ALU.mult),
                     r=[yaT], w=[ysq])
                pn = PS[Tt % 4]
                for m in range(4):
                    k.op("pe", lambda m=m, pn=pn: nc.tensor.matmul(pn[:, :], lhsT=ones_b[:, :], rhs=ysq[:, m, :],
                                                                   start=(m == 0), stop=(m == 3)),
                         r=[ones_b, ysq], w=[pn], inc=(m == 3))
                k.op("act", lambda pn=pn: nc.scalar.activation(out=yrs[:], in_=pn[:, :], func=AF.Ln,
                                                               scale=1.0 / 512, bias=epsb[:, :]),
                     r=[pn, epsb], w=[yrs])
                k.op("act", lambda: nc.scalar.activation(out=yrs[:], in_=yrs[:], func=AF.Exp, scale=-0.5),
                     r=[yrs], w=[yrs])
                for m in range(4):
                    k.op("dve", lambda m=m, ts_=ts_: nc.vector.scalar_tensor_tensor(
                        out=yo[:, m, :], in0=yaT[:, m, ts_], scalar=gmo[:, m:m + 1], in1=yrs[:],
                        op0=ALU.mult, op1=ALU.mult), r=[yaT, gmo, yrs], w=[yo])
                dd("yo", yo, yo[:], [128, 4, 512], BF16)
                dd("yrs", yrs, yrs[:], [128, 512], F32)
                if Tt == 1:
                    dd("yo1", yo, yo[:], [128, 4, 512], BF16)
                    dd("yrs1", yrs, yrs[:], [128, 512], F32)
                    dd("ysq1", ysq, ysq[:], [128, 4, 512], BF16)
                for m in range(4):
                    k.dma("act", yT_d[4 + m, :, ts_], yo[:, m, :], r=[yo], w=[yT_buf], own=yo.sub("st"))
            k.wait_all("sp", [yo])

        k.barrier()
        esAB.close()
        esW2 = ExitStack()
        wpp = k.sb("wpp", [128, 2, 1024], BF16, esW2)
        wpg = k.sb("wpg", [128, 8, 1024], BF16, esW2)
        esW1 = ExitStack()
        wup = k.sb("wup", [128, 8, 4096], BF16, esW1)
        wdn = k.sb("wdn", [128, 32, 1024], BF16, esW1)
        g_mlp = k.sb("g_mlp", [128, D], F32, esW1)
        esC = ExitStack()
        with esC:
            wout = k.sb("wout", [128, 8, 1024], BF16, esC)
            for kc in range(8):
                k.dma("pool", wout[:, kc, :], wout_d[:, kc, :], w=[wout], own=wout.sub("ld"))
            for kc in range(8):
                for q4 in range(4):
                    k.dma("pool", wup[:, kc, q4 * 1024:(q4 + 1) * 1024], wup_d[:, kc, q4 * 1024:(q4 + 1) * 1024],
                          w=[wup], own=wup.sub("ld"))
            for fc in range(32):
                k.dma("pool", wdn[:, fc, :], wdn_d[:, fc, :], w=[wdn], own=wdn.sub("ld"))
            k.dma("pool", g_mlp[:], g_mlp_d[:, :], w=[g_mlp], own=g_mlp)
            for kc in range(2):
                k.dma("pool", wpp[:, kc, :], wpp_d[:, kc, :], w=[wpp], own=wpp.sub("ld"))
            for kc in range(8):
                k.dma("pool", wpg[:, kc, :], wpg_d[:, kc, :], w=[wpg], own=wpg.sub("ld"))
            yt2 = [k.sb("yt%d" % i, [128, 8, 512], BF16, esC) for i in range(2)]
            xc = [k.sb("xc%d" % i, [128, D], F32, esC) for i in range(2)]
            hc = [k.sb("hc%d" % i, [128, D], F32, esC) for i in range(2)]

            def c1_block(Tt, blk):
                yt = yt2[Tt % 2]
                r0 = Tt * 512 + blk * 128
                xb = xc[blk % 2]
                hb = hc[blk % 2]
                k.dma("sp", xb[:], x_d[r0:r0 + 128, :], w=[xb], own=xb)
                yield
                for n2 in range(2):
                    pp = nps(blk % 2)
                    for kc in range(8):
                        k.op("pe", lambda kc=kc, pp=pp, n2=n2: nc.tensor.matmul(
                            pp[:, :], lhsT=yt[:, kc, blk * 128:(blk + 1) * 128],
                            rhs=wout[:, kc, n2 * 512:(n2 + 1) * 512], start=(kc == 0), stop=(kc == 7)),
                            r=[yt, wout], w=[pp], inc=(kc == 7))
                    k.op("dve", lambda pp=pp, n2=n2: nc.vector.tensor_tensor(
                        out=hb[:, n2 * 512:(n2 + 1) * 512], in0=pp[:, :], in1=xb[:, n2 * 512:(n2 + 1) * 512],
                        op=ALU.add), r=[pp, xb], w=[hb])
                    yield
                k.dma("act", h1_d[r0:r0 + 128, :], hb[:], r=[hb], w=[h1_buf], own=hb.sub("st"))

            for Tt in range(NT):
                for kc in range(8):
                    k.dma("sp", yt2[Tt % 2][:, kc, :], yT_d[kc, :, Tt * 512:(Tt + 1) * 512], r=[yT_buf],
                          w=[yt2[Tt % 2]], own=yt2[Tt % 2])
                run_interleaved((c1_block(Tt, b) for b in range(4)), 2)
            k.wait_all("sp", hc)

        k.barrier()
        esD = ExitStack()
        with esD:
            ht = [k.sb("ht%d" % i, [128, D], F32, esD) for i in range(2)]
            ho = [k.sb("ho%d" % i, [128, D], F32, esD) for i in range(2)]
            ubm = [k.sb("ubm%d" % i, [128, D], BF16, esD) for i in range(2)]
            uTm = [k.sb("uTm%d" % i, [128, 8, 128], BF16, esD) for i in range(2)]
            ssm = [k.sb("ssm%d" % i, [128, 1], F32, esD) for i in range(2)]
            hT = [k.sb("hT%d" % i, [128, 32, 128], BF16, esD) for i in range(2)]
            rl = [[k.sb("rl%d_%d" % (i, j), [128, 512], BF16, esD) for j in range(2)] for i in range(2)]
            def mlp_block(blk):
                i2 = blk % 2
                r0 = blk * 128
                k.dma("sp", ht[i2][:], h1_d[r0:r0 + 128, :], r=[h1_buf], w=[ht[i2]], own=ht[i2])
                rms_tm(ht[i2][:], [ht[i2]], g_mlp, ubm[i2][:], [ubm[i2]], ubm[i2], ssm[i2])
                dd("ubm", ubm[i2], ubm[i2][:], [128, D], BF16)
                yield
                to_fm(ubm[i2], [ubm[i2]], uTm[i2], [uTm[i2]], 0, 8, p=i2)
                yield
                dd("uTm", uTm[i2], uTm[i2][:], [128, 8, 128], BF16)
                for f4 in range(8):
                    pp = nps(i2)
                    for j in range(4):
                        fc = f4 * 4 + j
                        for kc in range(8):
                            k.op("pe", lambda kc=kc, pp=pp, fc=fc, j=j, i2=i2: nc.tensor.matmul(
                                pp[:, j * 128:(j + 1) * 128], lhsT=wup[:, kc, fc * 128:(fc + 1) * 128],
                                rhs=uTm[i2][:, kc, :], start=(kc == 0), stop=(kc == 7)),
                                r=[wup, uTm[i2]], w=[pp], inc=(kc == 7 and j == 3))
                    rb = rl[i2][f4 % 2]
                    if dbg is not None and 'ppc' in dbg and blk == 0 and f4 == 0:
                        ppc = k.sb("ppc", [128, 512], F32, esD)
                        k.op("dve", lambda pp=pp: nc.vector.tensor_copy(out=ppc[:], in_=pp[:, :]), r=[pp], w=[ppc])
                        dd("ppc", ppc, ppc[:], [128, 512], F32)
                    k.op("act", lambda pp=pp, rb=rb: nc.scalar.activation(out=rb[:], in_=pp[:, :], func=AF.Relu),
                         r=[pp], w=[rb])
                    dd("rl", rb, rb[:], [128, 512], BF16)
                    k.op("pool", lambda rb=rb, f4=f4, i2=i2: nc.gpsimd.tensor_tensor(
                        out=hT[i2][:, f4 * 4:(f4 + 1) * 4, :], in0=rb[:].rearrange("p (j t) -> p j t", j=4),
                        in1=rb[:].rearrange("p (j t) -> p j t", j=4), op=ALU.mult), r=[rb], w=[hT[i2]])
                    yield
                dd("hT", hT[i2], hT[i2][:], [128, 32, 128], BF16)
                dd("wup", wup, wup[:, 0, :], [128, 4096], BF16)
                dd("wdn", wdn, wdn[:, 0, :], [128, 1024], BF16)
                for n2 in range(2):
                    pp = nps(i2)
                    for fc in range(32):
                        k.op("pe", lambda fc=fc, pp=pp, n2=n2, i2=i2: nc.tensor.matmul(
                            pp[:, :], lhsT=hT[i2][:, fc, :], rhs=wdn[:, fc, n2 * 512:(n2 + 1) * 512],
                            start=(fc == 0), stop=(fc == 31)), r=[hT[i2], wdn], w=[pp], inc=(fc == 31))
                    k.op("dve", lambda pp=pp, n2=n2, i2=i2: nc.vector.tensor_tensor(
                        out=ho[i2][:, n2 * 512:(n2 + 1) * 512], in0=pp[:, :], in1=ht[i2][:, n2 * 512:(n2 + 1) * 512],
                        op=ALU.add), r=[pp, ht[i2]], w=[ho[i2]])
                    yield
                k.dma("sp", h2_d[r0:r0 + 128, :], ho[i2][:], r=[ho[i2]], w=[h2_buf], own=ho[i2].sub("st"))
            run_interleaved((mlp_block(b) for b in range(32)), 2, stagger=6)
            k.wait_all("sp", ho)

        k.barrier()
        esW1.close()
        esE = ExitStack()
        with esE:
            gB = {}
            for nm, src in [("post", g_post_d), ("gate", g_gate_d), ("fin", g_fin_d)]:
                gB[nm] = k.sb("gB_" + nm, [128, D], F32, esE)
                k.dma("sp", gB[nm][:], src[:, :], w=[gB[nm]], own=gB[nm])
            h2t = [k.sb("h2t%d" % i, [128, D], F32, esE) for i in range(3)]
            pin = [k.sb("pin%d" % i, [128, 256], F32, esE) for i in range(3)]
            pbf = [k.sb("pbf%d" % i, [128, 256], BF16, esE) for i in range(3)]
            pTt = [k.sb("pTt%d" % i, [128, 2, 128], BF16, esE) for i in range(3)]
            er = [k.sb("er%d" % i, [128, D], F32, esE) for i in range(3)]
            en = [k.sb("en%d" % i, [128, D], F32, esE) for i in range(3)]
            ug = [k.sb("ug%d" % i, [128, D], BF16, esE) for i in range(3)]
            uTg = [k.sb("uTg%d" % i, [128, 8, 128], BF16, esE) for i in range(3)]
            gt = [k.sb("gt%d" % i, [128, D], F32, esE) for i in range(3)]
            h3 = [k.sb("h3%d" % i, [128, D], F32, esE) for i in range(3)]
            fo = [k.sb("fo%d" % i, [128, D], F32, esE) for i in range(3)]
            jk2 = [k.sb("jk%d" % i, [128, D], BF16, esE) for i in range(3)]
            sse = [k.sb("sse%d" % i, [128, 1], F32, esE) for i in range(3)]
            ssg = [k.sb("ssg%d" % i, [128, 1], F32, esE) for i in range(3)]
            ssf = [k.sb("ssf%d" % i, [128, 1], F32, esE) for i in range(3)]
            def ple_block(blk):
                i2 = blk % 3
                r0 = blk * 128
                k.dma("sp", h2t[i2][:], h2_d[r0:r0 + 128, :], r=[h2_buf], w=[h2t[i2]], own=h2t[i2])
                k.dma("sp", pin[i2][:], p_d[r0:r0 + 128, :], w=[pin[i2]], own=pin[i2])
                k.op("dve", lambda i2=i2: nc.vector.tensor_copy(out=pbf[i2][:], in_=pin[i2][:]),
                     r=[pin[i2]], w=[pbf[i2]])
                to_fm(pbf[i2], [pbf[i2]], pTt[i2], [pTt[i2]], 0, 2, p=None)
                yield
                for n2 in range(2):
                    pp = nps()
                    for kc in range(2):
                        k.op("pe", lambda kc=kc, pp=pp, n2=n2, i2=i2: nc.tensor.matmul(
                            pp[:, :], lhsT=pTt[i2][:, kc, :], rhs=wpp[:, kc, n2 * 512:(n2 + 1) * 512],
                            start=(kc == 0), stop=(kc == 1)), r=[pTt[i2], wpp], w=[pp], inc=(kc == 1))
                    k.op("act", lambda pp=pp, n2=n2, i2=i2: nc.scalar.copy(
                        out=er[i2][:, n2 * 512:(n2 + 1) * 512], in_=pp[:, :]), r=[pp], w=[er[i2]])
                yield
                rms_tm(er[i2][:], [er[i2]], gB["post"], en[i2][:], [en[i2]], jk2[i2], sse[i2])
                yield
                rms_tm(h2t[i2][:], [h2t[i2]], gB["gate"], ug[i2][:], [ug[i2]], jk2[i2], ssg[i2])
                yield
                to_fm(ug[i2], [ug[i2]], uTg[i2], [uTg[i2]], 0, 8, p=None)
                yield
                for n2 in range(2):
                    pp = nps()
                    for kc in range(8):
                        k.op("pe", lambda kc=kc, pp=pp, n2=n2, i2=i2: nc.tensor.matmul(
                            pp[:, :], lhsT=uTg[i2][:, kc, :], rhs=wpg[:, kc, n2 * 512:(n2 + 1) * 512],
                            start=(kc == 0), stop=(kc == 7)), r=[uTg[i2], wpg], w=[pp], inc=(kc == 7))
                    gs = gt[i2][:, n2 * 512:(n2 + 1) * 512]
                    k.op("act", lambda pp=pp, gs=gs: nc.scalar.activation(out=gs, in_=pp[:, :], func=AF.Exp,
                                                                          scale=-1.0), r=[pp], w=[gt[i2]])
                yield
                k.op("act", lambda i2=i2: nc.scalar.activation(out=gt[i2][:], in_=gt[i2][:], func=AF.Ln,
                                                               bias=oneb[:, :]), r=[gt[i2], oneb], w=[gt[i2]])
                k.op("act", lambda i2=i2: nc.scalar.activation(out=gt[i2][:], in_=gt[i2][:], func=AF.Exp,
                                                               scale=-1.0), r=[gt[i2]], w=[gt[i2]])
                k.op("pool", lambda i2=i2: nc.gpsimd.tensor_tensor(out=gt[i2][:], in0=gt[i2][:], in1=en[i2][:],
                                                                   op=ALU.mult), r=[gt[i2], en[i2]], w=[gt[i2]])
                k.op("dve", lambda i2=i2: nc.vector.tensor_tensor(out=h3[i2][:], in0=h2t[i2][:], in1=gt[i2][:],
                                                                  op=ALU.add), r=[h2t[i2], gt[i2]], w=[h3[i2]])
                yield
                rms_tm(h3[i2][:], [h3[i2]], gB["fin"], fo[i2][:], [fo[i2]], jk2[i2], ssf[i2])
                k.dma("sp", out_d[r0:r0 + 128, :], fo[i2][:], r=[fo[i2]], w=[], own=fo[i2].sub("st"))
            run_interleaved((ple_block(b) for b in range(32)), 3, stagger=3)
            k.wait_all("sp", fo)
        esW2.close()
        if dbg is not None:
            dsb = Buf("dbgdma")
            k.dma("sp", dbg_y[:, :, :], yT_d[:, :, :], r=[yT_buf], w=[], own=dsb)
            k.dma("sp", dbg_h1[:, :], h1_d[:, :], r=[h1_buf], w=[], own=dsb)
            k.dma("sp", dbg_h2[:, :], h2_d[:, :], r=[h2_buf], w=[], own=dsb)
            k.eng["sp"].wait_ge(dsb.sem, dsb.dtot)
    return nc


def _lay(w, kc):
    K, N = w.shape
    return np.ascontiguousarray(w.reshape(kc, 128, N).transpose(1, 0, 2))


def make_inmaps(inp):
    f = lambda a: np.ascontiguousarray(np.asarray(a, dtype=np.float32))
    bc128 = lambda v: np.ascontiguousarray(np.broadcast_to(f(v).reshape(1, -1), (128, v.size)))
    common = {
        "w_in": _lay(f(inp["w_in"][0]), 8),
        "w_out": _lay(f(inp["w_out"][0]), 8),
        "w_up": _lay(f(inp["w_up"][0]), 8),
        "w_down": _lay(f(inp["w_down"][0]), 32),
        "w_ple_proj": _lay(f(inp["w_ple_proj"][0]), 2),
        "w_ple_gate": _lay(f(inp["w_ple_gate"][0]), 8),
        "w_q_b": _lay(f(inp["w_q_b"][0]), 2),
        "w_kv_b": f(inp["w_kv_b"][0]),
        "g_mix": bc128(inp["mix_norm_w"][0]),
        "g_mlp": bc128(inp["mlp_norm_w"][0]),
        "g_post": bc128(inp["ple_post_norm_w"][0]),
        "g_gate": bc128(inp["ple_gate_norm_w"][0]),
        "g_fin": bc128(inp["final_norm_w"]),
        "g_gdn": np.ascontiguousarray(np.broadcast_to(np.tile(f(inp["gdn_norm_w"][0]), 8).reshape(1, 512), (64, 512))),
        "cw": np.ascontiguousarray(f(inp["conv_w"][0]).T.reshape(24, 64, 4).transpose(1, 0, 2)),
        "alog": np.ascontiguousarray(np.broadcast_to(f(inp["A_log"][0]).reshape(1, 8), (64, 8))),
        "dtb": np.ascontiguousarray(np.broadcast_to(f(inp["dt_bias"][0]).reshape(1, 8), (64, 8))),
        "gq": np.ascontiguousarray(f(inp["q_norm_w"][0]).reshape(2, 128).T),
        "gkv": np.ascontiguousarray(f(inp["kv_norm_w"][0]).reshape(1, 128).T),
        "gmo": np.ascontiguousarray(f(inp["mla_out_norm_w"][0]).reshape(4, 128).T),
    }
    invf = np.zeros((96, 1), np.float32)
    fr = (10000.0 ** (-np.arange(0, 32, 2, dtype=np.float32) / 32)).astype(np.float32)
    invf[64:80, 0] = fr
    invf[80:96, 0] = fr
    common["invf"] = invf
    maps = []
    x = np.asarray(inp["x"], dtype=np.float32)
    p = np.asarray(inp["p"], dtype=np.float32)
    pos = np.asarray(inp["positions"], dtype=np.int32)
    for b in range(8):
        m = dict(common)
        m["x"] = np.ascontiguousarray(x[b])
        m["p"] = np.ascontiguousarray(p[0, b])
        m["pos"] = np.ascontiguousarray(pos[b].reshape(1, S))
        maps.append(m)
    return maps


def kernel(**inputs):
    nc = build_nc()
    maps = make_inmaps(inputs)
    res = run_bass_kernel_spmd(nc, maps, core_ids=list(range(8)))
    return np.stack([np.asarray(r["out"], dtype=np.float32) for r in res.results], axis=0)
```

```python
import numpy as np
import concourse.bass as bass
import concourse.mybir as mybir
from concourse.bass_utils import run_bass_kernel_spmd
from contextlib import ExitStack

F32 = mybir.dt.float32
BF16 = mybir.dt.bfloat16
I32 = mybir.dt.int32
AF = mybir.ActivationFunctionType
ALU = mybir.AluOpType
AX = mybir.AxisListType

S = 4096
D = 1024
NT = 8
EPS = 1e-6
STRICT_SAME = True
RELAX_SAME = False


class Buf:
    __slots__ = ("name", "w", "r", "sem", "dtot", "excl")

    def __init__(self, name):
        self.name = name
        self.excl = False
        self.w = None
        self.r = {}
        self.sem = None
        self.dtot = 0


class T:
    def __init__(self, t, name):
        self.t = t
        self.name = name
        self.b = Buf(name)
        self.subs = {}

    def __getitem__(self, idx):
        return self.t[idx]

    def sub(self, key):
        if key not in self.subs:
            self.subs[key] = Buf(f"{self.name}.{key}")
        return self.subs[key]


class TView(T):
    def __init__(self, base, dt):
        self.t = base.t
        self.name = base.name
        self.b = base.b
        self.subs = base.subs
        self.v = base.t[:, :].bitcast(dt)

    def __getitem__(self, idx):
        return self.v[idx]


class TRe(T):
    def __init__(self, base, pattern, **kw):
        self.t = base.t
        self.name = base.name
        self.b = base.b
        self.subs = base.subs
        self.v = base.t[:].rearrange(pattern, **kw)

    def __getitem__(self, idx):
        return self.v[idx]


class KB:
    def __init__(self, nc, es):
        self.nc = nc
        self.es = es
        self.eng = {"pe": nc.tensor, "act": nc.scalar, "dve": nc.vector, "pool": nc.gpsimd, "sp": nc.sync}
        self.sem = {}
        for n in ["pe", "act", "dve", "pool"]:
            self.sem[n] = es.enter_context(nc.semaphore("c_" + n))
        self.cnt = {n: 0 for n in self.sem}
        self.seen = {n: {} for n in self.eng}
        self.pend = {n: ([], []) for n in self.eng}
        self.nsem = 0
        self.psrr = 0
        self.dbufs = []

    def sb(self, name, shape, dt, es=None):
        es = es or self.es
        return T(es.enter_context(self.nc.sbuf_tensor("s_" + name, list(shape), dt)), name)

    def ps(self, name, shape, dt):
        t = T(self.es.enter_context(self.nc.psum_tensor("p_" + name, list(shape), dt)), name)
        t.b.excl = True
        return t

    def dsem(self, buf):
        if buf.sem is None:
            buf.sem = self.es.enter_context(self.nc.semaphore("d_%d" % self.nsem))
            self.nsem += 1
            self.dbufs.append(buf)
        return buf.sem

    def barrier(self):
        for e in self.eng:
            for n in self.sem:
                if n != e and self.cnt[n] > self.seen[e].get(n, 0):
                    self.eng[e].wait_ge(self.sem[n], self.cnt[n])
                    self.seen[e][n] = self.cnt[n]
            for b in self.dbufs:
                key = "d:" + b.name
                if b.dtot > self.seen[e].get(key, 0):
                    self.eng[e].wait_ge(b.sem, b.dtot)
                    self.seen[e][key] = b.dtot

    @staticmethod
    def _bufs(xs):
        out = []
        for x in xs:
            out.append(x.b if isinstance(x, T) else x)
        return out

    def _wait(self, e, deps):
        for key, (sem, val) in deps.items():
            if key == e and (e == "pe" or not STRICT_SAME):
                continue
            if self.seen[e].get(key, 0) < val:
                self.eng[e].wait_ge(sem, val)
                self.seen[e][key] = val

    @staticmethod
    def _add(deps, ev):
        if ev is None:
            return
        key, sem, val = ev
        if key not in deps or deps[key][1] < val:
            deps[key] = (sem, val)

    def _deps(self, reads, writes, e=None):
        deps = {}
        for b in reads:
            self._add(deps, b.w)
        for b in writes:
            if b.w is not None and b.w[0] != e:
                self._add(deps, b.w)
            for ev in b.r.values():
                if ev[0] != e:
                    self._add(deps, ev)
        return deps

    def op(self, e, fn, r=(), w=(), inc=True):
        reads = self._bufs(r)
        writes = self._bufs(w)
        xr = [b for b in reads if b.excl]
        if xr:
            reads = [b for b in reads if not b.excl]
            writes = writes + [b for b in xr if b not in writes]
        self._wait(e, self._deps(reads, writes, e if RELAX_SAME else None))
        inst = fn()
        if not inc:
            self.pend[e][0].extend(reads)
            self.pend[e][1].extend(writes)
            return inst
        self.cnt[e] += 1
        inst.then_inc(self.sem[e], 1)
        ev = (e, self.sem[e], self.cnt[e])
        pr, pw = self.pend[e]
        for b in reads + pr:
            b.r[e] = ev
        for b in writes + pw:
            b.w = ev
            b.r = {}
        self.pend[e] = ([], [])
        return inst

    def dma(self, q, out, in_, r=(), w=(), own=None, **kw):
        reads = self._bufs(r)
        writes = self._bufs(w)
        ob = own.b if isinstance(own, T) else own
        self._wait(q, self._deps(reads, writes))
        sem = self.dsem(ob)
        ob.dtot += 16
        self.eng[q].dma_start(out=out, in_=in_, **kw).then_inc(sem, 16)
        ev = ("d:" + ob.name, sem, ob.dtot)
        for b in reads:
            b.r["d:" + ob.name] = ev
        for b in writes:
            b.w = ev
            b.r = {}

    def wait_all(self, e, bufs):
        deps = {}
        for b in self._bufs(bufs):
            self._add(deps, b.w)
            for ev in b.r.values():
                self._add(deps, ev)
        self._wait(e, deps)


def build_nc(dbg=None):
    nc = bass.Bass("TRN2", target_bir_lowering=False)

    def din(name, shape, dt=F32):
        return nc.dram_tensor(name, list(shape), dt, kind="ExternalInput").ap()

    x_d = din("x", [S, D])
    p_d = din("p", [S, 256])
    pos_d = din("pos", [1, S], I32)
    win_d = din("w_in", [128, 8, 2480])
    wout_d = din("w_out", [128, 8, 1024])
    wup_d = din("w_up", [128, 8, 4096])
    wdn_d = din("w_down", [128, 32, 1024])
    wpp_d = din("w_ple_proj", [128, 2, 1024])
    wpg_d = din("w_ple_gate", [128, 8, 1024])
    wqb_d = din("w_q_b", [128, 2, 768])
    wkvb_d = din("w_kv_b", [128, 1024])
    g_mix_d = din("g_mix", [128, D])
    g_mlp_d = din("g_mlp", [128, D])
    g_post_d = din("g_post", [128, D])
    g_gate_d = din("g_gate", [128, D])
    g_fin_d = din("g_fin", [128, D])
    g_gdn_d = din("g_gdn", [64, 512])
    cw_d = din("cw", [64, 24, 4])
    alog_d = din("alog", [64, 8])
    dtb_d = din("dtb", [64, 8])
    gq_d = din("gq", [128, 2])
    gkv_d = din("gkv", [128, 1])
    gmo_d = din("gmo", [128, 4])
    invf_d = din("invf", [96, 1])
    out_d = nc.dram_tensor("out", [S, D], F32, kind="ExternalOutput").ap()
    yT_d = nc.dram_tensor("yT_scr", [8, 128, S], BF16, kind="Internal").ap()
    h1_d = nc.dram_tensor("h1_scr", [S, D], F32, kind="Internal").ap()
    if dbg is not None:
        dbg_y = nc.dram_tensor("dbg_y", [8, 128, S], BF16, kind="ExternalOutput").ap()
        dbg_h1 = nc.dram_tensor("dbg_h1", [S, D], F32, kind="ExternalOutput").ap()
        dbg_h2 = nc.dram_tensor("dbg_h2", [S, D], F32, kind="ExternalOutput").ap()

    es = ExitStack()
    with es:
        k = KB(nc, es)
        E = es.enter_context
        yT_buf = Buf("yT_dram")
        h1_buf = Buf("h1_dram")
        ident_b = k.sb("ident_b", [128, 128], BF16)
        ident_f = k.sb("ident_f", [128, 128], F32)
        ones_f = k.sb("ones_f", [128, 128], F32)
        ones_b = k.sb("ones_b", [128, 128], BF16)
        epsb = k.sb("epsb", [128, 1], F32)
        k.op("pool", lambda: nc.gpsimd.memset(epsb[:], EPS), w=[epsb])
        oneb = k.sb("oneb", [128, 1], F32)
        k.op("pool", lambda: nc.gpsimd.memset(oneb[:], 1.0), w=[oneb])
        esAB = ExitStack()
        ident3 = k.sb("ident3", [64, 8, 64], F32, esAB)
        tri_f = k.sb("tri_f", [64, 64], F32, esAB)
        k.op("pool", lambda: nc.gpsimd.memset(ident_f[:], 0.0), w=[ident_f])
        k.op("pool", lambda: nc.gpsimd.affine_select(out=ident_f[:], in_=ident_f[:], pattern=[[-1, 128]],
                                                     compare_op=ALU.not_equal, fill=1.0, base=0,
                                                     channel_multiplier=1), r=[ident_f], w=[ident_f])
        k.op("pool", lambda: nc.gpsimd.tensor_copy(out=ident_b[:], in_=ident_f[:]), r=[ident_f], w=[ident_b])
        k.op("pool", lambda: nc.gpsimd.memset(ones_f[:], 1.0), w=[ones_f])
        k.op("pool", lambda: nc.gpsimd.memset(ones_b[:], 1.0), w=[ones_b])
        k.op("pool", lambda: nc.gpsimd.tensor_copy(
            out=ident3[:], in_=ident_f[0:64, 0:64].unsqueeze(1).to_broadcast([64, 8, 64])), r=[ident_f], w=[ident3])
        k.op("pool", lambda: nc.gpsimd.affine_select(out=tri_f[:], in_=ones_f[0:64, 0:64], pattern=[[1, 64]],
                                                     compare_op=ALU.is_ge, fill=0.0, base=0,
                                                     channel_multiplier=-1), r=[ones_f], w=[tri_f])

        def load_const(name, src, shape, dt=F32, q="sp"):
            t = k.sb(name, shape, dt, esAB)
            k.dma(q, t[:], src, w=[t], own=t)
            return t

        g_gdn = load_const("g_gdn", g_gdn_d[:, :], [64, 512])
        cw = load_const("cw", cw_d[:, :, :], [64, 24, 4])
        alog = load_const("alog", alog_d[:, :], [64, 8])
        dtb = load_const("dtb", dtb_d[:, :], [64, 8])
        gq = load_const("gq", gq_d[:, :], [128, 2])
        gkv = load_const("gkv", gkv_d[:, :], [128, 1])
        gmo = load_const("gmo", gmo_d[:, :], [128, 4])
        invf = load_const("invf", invf_d[:, :], [96, 1])
        nexpA = k.sb("nexpA", [64, 8], F32, esAB)
        k.op("act", lambda: nc.scalar.activation(out=nexpA[:], in_=alog[:], func=AF.Exp), r=[alog], w=[nexpA])
        k.op("dve", lambda: nc.vector.tensor_scalar(out=nexpA[:], in0=nexpA[:], scalar1=-1.0, scalar2=None,
                                                    op0=ALU.mult), r=[nexpA], w=[nexpA])
        REG_NEG = nc.gpsimd.to_reg(-30000.0)
        REG_ZERO = nc.gpsimd.to_reg(0.0)
        dbg_outs = {}

        def dd(name, t, ap, shape, dt):
            if dbg is None or name in dbg_outs or (isinstance(dbg, (set, list, tuple)) and name not in dbg):
                return
            o = nc.dram_tensor("dd_" + name, list(shape), dt, kind="ExternalOutput").ap()
            dbg_outs[name] = o
            k.dma("sp", o, ap, r=[t], w=[], own=Buf("dd_" + name))

        PS = [k.ps("ps%d" % i, [128, 512], F32) for i in range(6)]
        PB = [k.ps("pb%d" % i, [128, 1024], BF16) for i in range(2)]
        rr = {"f": 0, "b": 0}

        srr = {0: 0, 1: 0}

        def nps(p=None):
            if p is None:
                rr["f"] = (rr["f"] + 1) % 6
                return PS[rr["f"]]
            srr[p] = (srr[p] + 1) % 3
            return PS[3 * p + srr[p]]

        srr3 = {0: 0, 1: 0, 2: 0}

        def nps3(p):
            srr3[p] = (srr3[p] + 1) % 2
            return PS[2 * p + srr3[p]]

        def npb(p=None):
            if p is not None:
                return PB[p]
            rr["b"] = (rr["b"] + 1) % 2
            return PB[rr["b"]]

        def run_interleaved(gens, width, bg=None, bg_every=4, stagger=0):
            active = []
            it = iter(gens)
            rnd = 0
            launched = 0
            exhausted = False
            while True:
                while len(active) < width and not exhausted:
                    if launched < width and stagger and launched * stagger > rnd:
                        break
                    g = next(it, None)
                    if g is None:
                        exhausted = True
                        break
                    active.append(g)
                    launched += 1
                if not active and exhausted:
                    break
                for g in list(active):
                    try:
                        next(g)
                    except StopIteration:
                        active.remove(g)
                rnd += 1
                if bg is not None and rnd % bg_every == 0:
                    try:
                        next(bg)
                    except StopIteration:
                        bg = None
            if bg is not None:
                for _ in bg:
                    pass

        def rstd_from_ssq(ssq, n, dim):
            k.op("act", lambda: nc.scalar.activation(out=ssq[0:n, :], in_=ssq[0:n, :], func=AF.Ln,
                                                     scale=1.0 / dim, bias=epsb[0:n, :]), r=[ssq, epsb], w=[ssq])
            k.op("act", lambda: nc.scalar.activation(out=ssq[0:n, :], in_=ssq[0:n, :], func=AF.Exp, scale=-0.5),
                 r=[ssq], w=[ssq])


        def rms_tm(xin, xin_bufs, gB, out, out_bufs, junk, ssq, dim=D):
            k.op("act", lambda: nc.scalar.activation(out=junk[:, 0:dim], in_=xin, func=AF.Square,
                                                     accum_out=ssq[:, 0:1]), r=xin_bufs, w=[junk, ssq])
            rstd_from_ssq(ssq, 128, dim)
            k.op("dve", lambda: nc.vector.scalar_tensor_tensor(out=out, in0=xin, scalar=ssq[:, 0:1], in1=gB[:, 0:dim],
                                                               op0=ALU.mult, op1=ALU.mult),
                 r=list(xin_bufs) + [ssq, gB], w=out_bufs)

        def to_fm(src_bf, src_bufs, dstT, dst_bufs, col0, nkc, p=None):
            pb = npb(p)
            for kc in range(nkc):
                k.op("pe", lambda kc=kc: nc.tensor.transpose(out=pb[:, kc * 128:(kc + 1) * 128],
                                                            in_=src_bf[:, kc * 128:(kc + 1) * 128],
                                                            identity=ident_b[:]),
                     r=list(src_bufs) + [ident_b], w=[pb], inc=(kc == nkc - 1))
            k.op("act", lambda: nc.scalar.copy(
                out=dstT[:, 0:nkc, col0:col0 + 128],
                in_=pb[:, 0:nkc * 128].rearrange("p (k t) -> p k t", k=nkc)), r=[pb], w=dst_bufs)


        TWO_PI = 6.283185307179586

        def sincos_tile(t0, posi, rv, rki, rkf, rfr, rcs, outf=None):
            R = slice(64, 96)
            k.dma("sp", posi[R, :], pos_d[0:1, t0:t0 + 512].to_broadcast([32, 512]), w=[posi], own=posi)
            k.op("dve", lambda: nc.vector.tensor_copy(out=rv[R, :], in_=posi[R, :]), r=[posi], w=[rv])
            k.op("dve", lambda: nc.vector.tensor_scalar(out=rv[R, :], in0=rv[R, :], scalar1=invf[R, 0:1],
                                                        scalar2=1.0 / TWO_PI, op0=ALU.mult, op1=ALU.mult),
                 r=[rv, invf], w=[rv])
            for which in range(2):
                if which == 0:
                    k.op("dve", lambda: nc.vector.tensor_scalar(out=rfr[R, :], in0=rv[R, :], scalar1=0.25,
                                                                scalar2=None, op0=ALU.add), r=[rv], w=[rfr])
                    src = rfr
                else:
                    src = rv
                k.op("dve", lambda src=src: nc.vector.tensor_copy(out=rki[R, :], in_=src[R, :]), r=[src], w=[rki])
                k.op("dve", lambda: nc.vector.tensor_copy(out=rkf[R, :], in_=rki[R, :]), r=[rki], w=[rkf])
                k.op("dve", lambda src=src: nc.vector.tensor_tensor(out=rfr[R, :], in0=src[R, :], in1=rkf[R, :],
                                                                    op=ALU.subtract), r=[src, rkf], w=[rfr])
                k.op("dve", lambda: nc.vector.tensor_scalar(out=rkf[R, :], in0=rfr[R, :], scalar1=0.5, scalar2=None,
                                                            op0=ALU.is_gt), r=[rfr], w=[rkf])
                k.op("dve", lambda: nc.vector.tensor_tensor(out=rfr[R, :], in0=rfr[R, :], in1=rkf[R, :],
                                                            op=ALU.subtract), r=[rfr, rkf], w=[rfr])
                oap = rcs[R, which, :] if outf is None else outf(which)
                k.op("act", lambda oap=oap: nc.scalar.activation(out=oap, in_=rfr[R, :],
                                                                 func=AF.Sin, scale=TWO_PI),
                     r=[rfr], w=[rcs])

        cq_d = nc.dram_tensor("cq_scr", [2, 128, S], BF16, kind="Internal").ap()
        ckv_d = nc.dram_tensor("ckv_scr", [128, S], BF16, kind="Internal").ap()
        kpe_d = nc.dram_tensor("kpe_scr", [32, S], BF16, kind="Internal").ap()
        lat_buf = Buf("lat_dram")
        esA = ExitStack()
        with esA:
            cdiag = k.sb("cdiag", [64, 24, 4, 64], BF16, esA)
            for c in range(24):
                for kk in range(4):
                    k.op("dve", lambda c=c, kk=kk: nc.vector.tensor_scalar(
                        out=cdiag[:, c, kk, :], in0=ident_f[0:64, 0:64], scalar1=cw[:, c, kk:kk + 1], scalar2=None,
                        op0=ALU.mult), r=[ident_f, cw], w=[cdiag], inc=(c == 23 and kk == 3))

            wi = k.sb("wi", [128, 8, 2480], BF16, esA)
            for kc in range(8):
                for hf in range(2):
                    k.dma("pool", wi[:, kc, hf * 1240:(hf + 1) * 1240], win_d[:, kc, hf * 1240:(hf + 1) * 1240],
                          w=[wi], own=wi.sub("ld"))
            g_mix = k.sb("g_mix", [128, D], F32, esA)
            k.dma("sp", g_mix[:], g_mix_d[:, :], w=[g_mix], own=g_mix)
            xt = [k.sb("xt%d" % i, [128, D], F32, esA) for i in range(2)]
            ssq = [k.sb("ssq%d" % i, [128, 1], F32, esA) for i in range(2)]
            ub = [k.sb("ub%d" % i, [128, D], BF16, esA) for i in range(2)]
            uT2 = [k.sb("uT%d" % i, [128, 8, 512], BF16, esA) for i in range(2)]
            pre2 = [k.sb("pre%d" % i, [64, 515], BF16, esA) for i in range(3)]
            halo = k.sb("halo", [64, 24, 3], BF16, esA)
            silt = [k.sb("silt%d" % i, [64, 512], BF16, esA) for i in range(3)]
            silv = k.sb("silv", [64, 8, 512], BF16, esA)
            sq2 = [k.sb("sq%d" % i, [64, 512], BF16, esA) for i in range(3)]
            rn2 = [k.sb("rn%d" % i, [64, 512], F32, esA) for i in range(3)]
            qk_n = k.sb("qk_n", [64, 16, 512], BF16, esA)
            rv = k.sb("rv", [128, 512], F32, esA)
            rki = k.sb("rki", [96, 512], I32, esA)
            posi = k.sb("posi", [96, 512], I32, esA)
            rkf = k.sb("rkf", [96, 512], F32, esA)
            rfr = k.sb("rfr", [96, 512], F32, esA)
            rcs = k.sb("rcs", [96, 2, 512], F32, esA)
            lsq = k.sb("lsq", [128, 3, 512], BF16, esA)
            lrs = rv
            cq_t = k.sb("cq_t", [128, 2, 512], BF16, esA)
            ckv_t = k.sb("ckv_t", [128, 512], BF16, esA)
            kpe_t = k.sb("kpe_t", [96, 512], BF16, esA)
            wrot = k.sb("wrot", [128, 8, 2, 96], BF16, esA)
            k.op("pool", lambda: nc.gpsimd.memset(wrot[:], 0.0), w=[wrot])
            k.op("act", lambda: nc.scalar.copy(out=wrot[:, :, 0, 64:96], in_=wi[:, :, 2448:2480]), r=[wi], w=[wrot])
            k.op("act", lambda: nc.scalar.mul(out=wrot[:, :, 1, 64:80], in_=wi[:, :, 2464:2480], mul=-1.0),
                 r=[wi], w=[wrot])
            k.op("act", lambda: nc.scalar.copy(out=wrot[:, :, 1, 80:96], in_=wi[:, :, 2448:2464]), r=[wi], w=[wrot])
            Sst = k.sb("Sst", [64, 8, 64], F32, esA)
            Sbf = k.sb("Sbf", [64, 8, 64], BF16, esA)
            k.op("pool", lambda: nc.gpsimd.memset(Sst[:], 0.0), w=[Sst])
            k.op("pool", lambda: nc.gpsimd.memset(Sbf[:], 0.0), w=[Sbf])
            k.op("pool", lambda: nc.gpsimd.memset(halo[:], 0.0), w=[halo])
            NSTR = 3

            def parn(name, shape, dt):
                return [k.sb("%s_%d" % (name, i), shape, dt, esA) for i in range(NSTR)]
            sm2 = parn("sm", [64, 12, 8], F32)
            Rm2 = parn("Rm", [64, 8, 64], F32)
            DT2 = parn("DT", [64, 8, 64], F32)
            DTb2 = parn("DTb", [64, 8, 64], F32)
            Z2 = [parn("Z%d" % i, [64, 8, 64], BF16) for i in range(2)]
            ZT2 = [parn("ZT%d" % i, [64, 8, 64], BF16) for i in range(2)]
            Pm2 = [parn("Pm%d" % i, [64, 8, 64], BF16) for i in range(2)]
            AT2 = parn("AT", [64, 8, 64], BF16)
            kvt2 = parn("kvt", [64, 16, 64], BF16)
            tmpf2 = Rm2
            of2 = DTb2
            zg2 = [TRe(DT2[i], "p h i -> p (h i)") for i in range(NSTR)]
            rbf2 = Z2[0]
            vnew2 = Z2[1]
            kd2 = ZT2[0]
            ybf2 = [TRe(ZT2[1][i], "p h i -> p (h i)") for i in range(NSTR)]
            scan_done = [0]
            smT = k.sb("smT", [64, 7, 8, 8], F32, esA)
            sel63 = k.sb("sel63", [64, 64], F32, esA)
            k.op("pool", lambda: nc.gpsimd.affine_select(out=sel63[:], in_=ones_f[0:64, 0:64], pattern=[[0, 64]],
                                                         compare_op=ALU.is_equal, fill=REG_ZERO, base=-63,
                                                         channel_multiplier=1), r=[ones_f], w=[sel63])
            ygT = k.sb("ygT", [128, 4, 512], BF16, esA)

            def bc(ap2):
                return ap2.unsqueeze(2).to_broadcast([64, 8, 64])

            def a1_stream(Tt):
                for blk in range(4):
                    xb = xt[blk % 2]
                    r0 = Tt * 512 + blk * 128
                    k.dma("sp", xb[:], x_d[r0:r0 + 128, :], w=[xb], own=xb)
                    rms_tm(xb[:], [xb], g_mix, ub[blk % 2][:], [ub[blk % 2]], ub[blk % 2], ssq[blk % 2])
                    yield
                    to_fm(ub[blk % 2], [ub[blk % 2]], uT2[Tt % 2], [uT2[Tt % 2]], blk * 128, 8)
                    yield

            for _ in a1_stream(0):
                pass
            for Tt in range(NT):
                t0 = Tt * 512
                uT = uT2[Tt % 2]
                def ht_stream(c):
                    rn = rn2[c % 3]
                    sq = sq2[c % 3]
                    pp = nps3(c % 3)
                    for kc in range(8):
                        k.op("pe", lambda kc=kc, c=c, pp=pp: nc.tensor.matmul(
                            pp[0:64, :], lhsT=wi[:, kc, c * 64:(c + 1) * 64], rhs=uT[:, kc, :],
                            start=(kc == 0), stop=(kc == 7)), r=[wi.sub("ld"), wi, uT], w=[pp], inc=(kc == 7))
                    yield
                    pr_ = pre2[c % 3]
                    k.op("act", lambda c=c, pr_=pr_: nc.scalar.copy(out=pr_[:, 0:3], in_=halo[:, c, :]),
                         r=[halo], w=[pr_])
                    k.op("act", lambda c=c, pp=pp, pr_=pr_: nc.scalar.copy(out=pr_[:, 3:515], in_=pp[0:64, :]),
                         r=[pp], w=[pr_])
                    k.op("act", lambda c=c, pr_=pr_: nc.scalar.copy(out=halo[:, c, :], in_=pr_[:, 512:515]),
                         r=[pr_], w=[halo])
                    pc = nps3(c % 3)
                    for kk in range(4):
                        k.op("pe", lambda kk=kk, c=c, pc=pc, pr_=pr_: nc.tensor.matmul(
                            pc[0:64, :], lhsT=cdiag[:, c, kk, :], rhs=pr_[:, kk:kk + 512],
                            start=(kk == 0), stop=(kk == 3)), r=[cdiag, pr_], w=[pc], inc=(kk == 3))
                    yield
                    k.op("act", lambda pc=pc: nc.scalar.activation(out=rn[:], in_=pc[0:64, :], func=AF.Exp,
                                                                   scale=-1.0), r=[pc], w=[rn])
                    k.op("act", lambda: nc.scalar.activation(out=rn[:], in_=rn[:], func=AF.Ln, bias=oneb[0:64, :]),
                         r=[rn, oneb], w=[rn])
                    k.op("act", lambda: nc.scalar.activation(out=rn[:], in_=rn[:], func=AF.Exp, scale=-1.0),
                         r=[rn], w=[rn])
                    yield
                    so_ = silt[c % 3] if c < 16 else silv
                    so_ap = silt[c % 3][:, :] if c < 16 else silv[:, c - 16, :]
                    k.op("dve", lambda pc=pc, so_ap=so_ap: nc.vector.tensor_tensor(out=so_ap, in0=pc[0:64, :],
                                                                                   in1=rn[:], op=ALU.mult),
                         r=[pc, rn], w=[so_])
                    yield
                    if c < 16:
                        k.op("dve", lambda so_ap=so_ap: nc.vector.tensor_tensor(out=sq[:], in0=so_ap,
                                                                                in1=so_ap, op=ALU.mult),
                             r=[so_], w=[sq])
                        yield
                        pn = nps3(c % 3)
                        k.op("pe", lambda pn=pn: nc.tensor.matmul(pn[0:64, :], lhsT=ones_b[0:64, 0:64], rhs=sq[:],
                                                                  start=True, stop=True), r=[ones_b, sq], w=[pn])
                        k.op("act", lambda pn=pn: nc.scalar.activation(out=rn[:], in_=pn[0:64, :], func=AF.Ln,
                                                                       bias=epsb[0:64, :]), r=[pn, epsb], w=[rn])
                        k.op("act", lambda: nc.scalar.activation(out=rn[:], in_=rn[:], func=AF.Exp, scale=-0.5),
                             r=[rn], w=[rn])
                        yield
                        sc = 0.125 if c < 8 else 1.0
                        k.op("dve", lambda c=c, sc=sc, so_ap=so_ap: nc.vector.scalar_tensor_tensor(
                            out=qk_n[:, c, :], in0=so_ap, scalar=sc, in1=rn[:], op0=ALU.mult, op1=ALU.mult),
                            r=[so_, rn], w=[qk_n])
                run_interleaved((ht_stream(c) for c in range(24)), 3, stagger=2)
                lat_cols = [(2064, 128), (2192, 128), (2320, 128)]
                plat = []
                for m, (c0, wd) in enumerate(lat_cols):
                    pp = nps()
                    plat.append(pp)
                    for kc in range(8):
                        k.op("pe", lambda kc=kc, c0=c0, pp=pp: nc.tensor.matmul(
                            pp[:, :], lhsT=wi[:, kc, c0:c0 + 128], rhs=uT[:, kc, :],
                            start=(kc == 0), stop=(kc == 7)), r=[wi, uT], w=[pp], inc=(kc == 7))
                    k.op("act", lambda m=m, pp=pp: nc.scalar.activation(out=lsq[:, m, :], in_=pp[:, :],
                                                                        func=AF.Square), r=[pp], w=[lsq])
                pn = nps()
                for m in range(2):
                    k.op("pe", lambda m=m, pn=pn: nc.tensor.matmul(pn[:, :], lhsT=ones_b[:, :], rhs=lsq[:, m, :],
                                                                   start=(m == 0), stop=(m == 1)),
                         r=[ones_b, lsq], w=[pn], inc=(m == 1))
                k.op("act", lambda pn=pn: nc.scalar.activation(out=lrs[:], in_=pn[:, :], func=AF.Ln, scale=1.0 / 256,
                                                               bias=epsb[:, :]), r=[pn, epsb], w=[lrs])
                k.op("act", lambda: nc.scalar.activation(out=lrs[:], in_=lrs[:], func=AF.Exp, scale=-0.5),
                     r=[lrs], w=[lrs])
                for m in range(2):
                    k.op("dve", lambda m=m: nc.vector.scalar_tensor_tensor(
                        out=cq_t[:, m, :], in0=plat[m][:, :], scalar=gq[:, m:m + 1], in1=lrs[:],
                        op0=ALU.mult, op1=ALU.mult), r=[plat[m], gq, lrs], w=[cq_t])
                pn = nps()
                k.op("pe", lambda pn=pn: nc.tensor.matmul(pn[:, :], lhsT=ones_b[:, :], rhs=lsq[:, 2, :],
                                                          start=True, stop=True), r=[ones_b, lsq], w=[pn])
                k.op("act", lambda pn=pn: nc.scalar.activation(out=lrs[:], in_=pn[:, :], func=AF.Ln, scale=1.0 / 128,
                                                               bias=epsb[:, :]), r=[pn, epsb], w=[lrs])
                k.op("act", lambda: nc.scalar.activation(out=lrs[:], in_=lrs[:], func=AF.Exp, scale=-0.5),
                     r=[lrs], w=[lrs])
                k.op("dve", lambda: nc.vector.scalar_tensor_tensor(
                    out=ckv_t[:], in0=plat[2][:, :], scalar=gkv[:, 0:1], in1=lrs[:],
                    op0=ALU.mult, op1=ALU.mult), r=[plat[2], gkv, lrs], w=[ckv_t])
                sincos_tile(t0, posi, rv, rki, rkf, rfr, rcs)
                pks = []
                for m in range(2):
                    pp = nps()
                    pks.append(pp)
                    for kc in range(8):
                        lw = wrot[:, kc, m, :]
                        k.op("pe", lambda kc=kc, pp=pp, lw=lw: nc.tensor.matmul(
                            pp[0:96, :], lhsT=lw, rhs=uT[:, kc, :], start=(kc == 0), stop=(kc == 7)),
                            r=[wi, wrot, uT], w=[pp], inc=(kc == 7))
                k.op("dve", lambda: nc.vector.tensor_tensor(out=rkf[64:96, :], in0=pks[0][64:96, :],
                                                            in1=rcs[64:96, 0, :], op=ALU.mult),
                     r=[pks[0], rcs], w=[rkf])
                k.op("dve", lambda: nc.vector.tensor_tensor(out=rfr[64:96, :], in0=pks[1][64:96, :],
                                                            in1=rcs[64:96, 1, :], op=ALU.mult),
                     r=[pks[1], rcs], w=[rfr])
                k.op("dve", lambda: nc.vector.tensor_tensor(out=kpe_t[64:96, :], in0=rkf[64:96, :],
                                                            in1=rfr[64:96, :], op=ALU.add),
                     r=[rkf, rfr], w=[kpe_t])
                for m in range(2):
                    k.dma("act", cq_d[m, :, t0:t0 + 512], cq_t[:, m, :], r=[cq_t], w=[lat_buf], own=cq_t.sub("st"))
                k.dma("act", ckv_d[:, t0:t0 + 512], ckv_t[:], r=[ckv_t], w=[lat_buf], own=ckv_t.sub("st"))
                k.dma("act", kpe_d[:, t0:t0 + 512], kpe_t[64:96, :], r=[kpe_t], w=[lat_buf], own=kpe_t.sub("st"))
                pa = nps()
                for n in range(8):
                    for kc in range(8):
                        k.op("pe", lambda kc=kc, n=n, pa=pa: nc.tensor.matmul(
                            pa[0:64, n * 16:(n + 1) * 16], lhsT=uT[:, kc, n * 64:(n + 1) * 64],
                            rhs=wi[:, kc, 2048:2064], start=(kc == 0), stop=(kc == 7)),
                            r=[wi, uT], w=[pa], inc=(kc == 7 and n == 7))
                pa3 = pa[0:64, 0:128].rearrange("p (n c) -> p n c", c=16)
                bT, gT, eT, edT, sdT, t1T, glT = [smT[:, j, :, :] for j in range(7)]
                k.op("act", lambda: nc.scalar.activation(out=bT, in_=pa3[:, :, 0:8], func=AF.Exp, scale=-1.0),
                     r=[pa], w=[smT])
                k.op("dve", lambda: nc.vector.tensor_tensor(
                    out=t1T, in0=pa3[:, :, 8:16], in1=dtb[:].unsqueeze(1).to_broadcast([64, 8, 8]), op=ALU.add),
                    r=[pa, dtb], w=[smT])
                k.op("dve", lambda: nc.vector.tensor_scalar(out=bT, in0=bT, scalar1=1.0, scalar2=None, op0=ALU.add),
                     r=[smT], w=[smT])
                k.op("dve", lambda: nc.vector.reciprocal(out=bT, in_=bT), r=[smT], w=[smT])
                k.op("act", lambda: nc.scalar.activation(out=t1T, in_=t1T, func=AF.Exp), r=[smT], w=[smT])
                k.op("act", lambda: nc.scalar.activation(out=t1T, in_=t1T, func=AF.Ln, bias=oneb[0:64, :]),
                     r=[smT, oneb], w=[smT])
                k.op("dve", lambda: nc.vector.tensor_tensor(
                    out=t1T, in0=t1T, in1=nexpA[:].unsqueeze(1).to_broadcast([64, 8, 8]), op=ALU.mult),
                    r=[smT, nexpA], w=[smT])
                pg = nps()
                k.op("pe", lambda: nc.tensor.matmul(pg[0:64, 0:64], lhsT=tri_f[:, :],
                                                    rhs=smT[:, 5, :, :].rearrange("p n h -> p (n h)"),
                                                    start=True, stop=True), r=[tri_f, smT], w=[pg])
                k.op("dve", lambda: nc.vector.tensor_copy(
                    out=gT, in_=pg[0:64, 0:64].rearrange("p (n h) -> p n h", h=8)), r=[pg], w=[smT])
                k.op("act", lambda: nc.scalar.activation(out=eT, in_=gT, func=AF.Exp), r=[smT], w=[smT])
                pl = nps()
                k.op("pe", lambda: nc.tensor.matmul(pl[0:64, 0:64], lhsT=sel63[:, :],
                                                    rhs=smT[:, 1, :, :].rearrange("p n h -> p (n h)"),
                                                    start=True, stop=True), r=[sel63, smT], w=[pl])
                k.op("dve", lambda: nc.vector.tensor_copy(
                    out=glT, in_=pl[0:64, 0:64].rearrange("p (n h) -> p n h", h=8)), r=[pl], w=[smT])
                k.op("act", lambda: nc.scalar.activation(out=sdT, in_=glT, func=AF.Exp), r=[smT], w=[smT])
                k.op("dve", lambda: nc.vector.tensor_tensor(out=edT, in0=glT, in1=gT, op=ALU.subtract),
                     r=[smT], w=[smT])
                k.op("act", lambda: nc.scalar.activation(out=edT, in_=edT, func=AF.Exp), r=[smT], w=[smT])
                def chunk_stream(n):
                    par = n % NSTR
                    gidx = Tt * 8 + n
                    sm = sm2[par]; Rm = Rm2[par]; Dm = Rm; DT = DT2[par]; DTb = DTb2[par]; Xf = DTb
                    Z = [Z2[0][par], Z2[1][par]]; ZT = [ZT2[0][par], ZT2[1][par]]; Pm = [Pm2[0][par], Pm2[1][par]]
                    AT = AT2[par]; kvt = kvt2[par]; kd = kd2[par]; tmpf = tmpf2[par]; rbf = rbf2[par]
                    vnew = vnew2[par]; of = of2[par]; zg = zg2[par]; ybf = ybf2[par]
                    cs = slice(n * 64, n * 64 + 64)
                    bcol = smT[:, 0, n, :]
                    gcol = smT[:, 1, n, :]
                    ecol = smT[:, 2, n, :]
                    edc = smT[:, 3, n, :]
                    sdc = smT[:, 4, n, :]
                    SB = SG = SE = SED = SSD = smT
                    yield
                    k.op("pool", lambda: nc.gpsimd.tensor_tensor(out=Rm[:], in0=ident3[:], in1=bc(gcol), op=ALU.mult),
                         r=[ident3, smT], w=[Rm])
                    pG = nps3(par)
                    yield
                    k.op("pe", lambda pG=pG: nc.tensor.matmul(pG[0:64, :], lhsT=ones_f[0:64, 0:64],
                                                              rhs=Rm[:].rearrange("p h i -> p (h i)"),
                                                              start=True, stop=True), r=[ones_f, Rm], w=[pG])
                    pG3 = pG[0:64, :].rearrange("p (h i) -> p h i", h=8)
                    yield
                    k.op("dve", lambda pG3=pG3: nc.vector.tensor_tensor(out=Dm[:], in0=pG3, in1=bc(gcol),
                                                                        op=ALU.subtract),
                         r=[pG, smT], w=[Dm])
                    yield
                    k.op("pool", lambda: nc.gpsimd.affine_select(
                        out=Dm[:], in_=Dm[:], pattern=[[0, 8], [1, 64]], compare_op=ALU.is_ge, fill=REG_NEG,
                        base=0, channel_multiplier=-1), r=[Dm], w=[Dm])
                    yield
                    k.op("act", lambda: nc.scalar.activation(out=DT[:], in_=Dm[:], func=AF.Exp), r=[Dm], w=[DT])
                    yield
                    k.op("pool", lambda: nc.gpsimd.tensor_tensor(out=DTb[:], in0=DT[:], in1=bc(bcol), op=ALU.mult),
                         r=[DT, smT], w=[DTb])
                    pK = nps3(par)
                    pQ = nps3(par)
                    yield
                    for h in range(8):
                        k.op("pe", lambda h=h, pK=pK: nc.tensor.matmul(
                            pK[0:64, h * 64:(h + 1) * 64], lhsT=qk_n[:, 8 + h, cs], rhs=qk_n[:, 8 + h, cs],
                            start=True, stop=True), r=[qk_n], w=[pK], inc=(h == 7))
                    yield
                    for h in range(8):
                        k.op("pe", lambda h=h, pQ=pQ: nc.tensor.matmul(
                            pQ[0:64, h * 64:(h + 1) * 64], lhsT=qk_n[:, 8 + h, cs], rhs=qk_n[:, h, cs],
                            start=True, stop=True), r=[qk_n], w=[pQ], inc=(h == 7))
                    yield
                    k.op("dve", lambda pK=pK: nc.vector.tensor_tensor(
                        out=Xf[:], in0=pK[0:64, :].rearrange("p (h i) -> p h i", h=8), in1=DTb[:], op=ALU.mult),
                        r=[pK, DTb], w=[Xf])
                    yield
                    k.op("pool", lambda: nc.gpsimd.affine_select(
                        out=Z[0][:], in_=Xf[:], pattern=[[0, 8], [1, 64]], compare_op=ALU.is_gt, fill=REG_ZERO,
                        base=0, channel_multiplier=-1), r=[Xf], w=[Z[0]])
                    yield
                    k.op("dve", lambda pQ=pQ: nc.vector.tensor_tensor(
                        out=AT[:], in0=pQ[0:64, :].rearrange("p (h i) -> p h i", h=8), in1=DT[:], op=ALU.mult),
                        r=[pQ, DT], w=[AT])
                    yield
                    pb = npb()
                    for h in range(8):
                        k.op("pe", lambda h=h, pb=pb: nc.tensor.transpose(
                            out=pb[0:64, h * 64:(h + 1) * 64], in_=qk_n[:, 8 + h, cs], identity=ident_b[0:64, 0:64]),
                            r=[qk_n, ident_b], w=[pb], inc=False)
                    for h in range(8):
                        k.op("pe", lambda h=h, pb=pb: nc.tensor.transpose(
                            out=pb[0:64, (8 + h) * 64:(9 + h) * 64], in_=silv[:, h, cs],
                            identity=ident_b[0:64, 0:64]), r=[silv, ident_b], w=[pb], inc=(h == 7))
                    k.op("act", lambda pb=pb: nc.scalar.copy(
                        out=kvt[:], in_=pb[0:64, :].rearrange("p (c d) -> p c d", c=16)), r=[pb], w=[kvt])
                    yield
                    pb = npb()
                    for h in range(8):
                        k.op("pe", lambda h=h, pb=pb: nc.tensor.transpose(
                            out=pb[0:64, h * 64:(h + 1) * 64], in_=Z[0][:, h, :], identity=ident_b[0:64, 0:64]),
                            r=[Z[0], ident_b], w=[pb], inc=(h == 7))
                    k.op("act", lambda pb=pb: nc.scalar.copy(
                        out=ZT[0][:], in_=pb[0:64, 0:512].rearrange("p (h i) -> p h i", h=8)), r=[pb], w=[ZT[0]])
                    yield
                    k.op("pool", lambda: nc.gpsimd.tensor_tensor(out=Pm[0][:], in0=ident3[:], in1=Z[0][:],
                                                                 op=ALU.subtract), r=[ident3, Z[0]], w=[Pm[0]])
                    cur = 0
                    yield
                    for lev in range(1, 6):
                        nxt = 1 - cur
                        yield
                        pzt = nps3(par)
                        for h in range(8):
                            k.op("pe", lambda h=h, pzt=pzt, cur=cur: nc.tensor.matmul(
                                pzt[0:64, h * 64:(h + 1) * 64], lhsT=Z[cur][:, h, :], rhs=ZT[cur][:, h, :],
                                start=True, stop=True), r=[Z[cur], ZT[cur]], w=[pzt], inc=(h == 7))
                        if lev < 5:
                            pz = nps3(par)
                            for h in range(8):
                                k.op("pe", lambda h=h, pz=pz, cur=cur: nc.tensor.matmul(
                                    pz[0:64, h * 64:(h + 1) * 64], lhsT=ZT[cur][:, h, :], rhs=Z[cur][:, h, :],
                                    start=True, stop=True), r=[Z[cur], ZT[cur]], w=[pz], inc=(h == 7))
                        yield
                        k.op("act", lambda pzt=pzt, nxt=nxt: nc.scalar.copy(
                            out=ZT[nxt][:], in_=pzt[0:64, :].rearrange("p (h i) -> p h i", h=8)),
                            r=[pzt], w=[ZT[nxt]])
                        if lev < 5:
                            k.op("dve", lambda pz=pz, nxt=nxt: nc.vector.tensor_copy(
                                out=Z[nxt][:], in_=pz[0:64, :].rearrange("p (h i) -> p h i", h=8)),
                                r=[pz], w=[Z[nxt]])
                        yield
                        pp = nps3(par)
                        for h in range(8):
                            k.op("pe", lambda h=h, pp=pp, nxt=nxt, cur=cur: nc.tensor.matmul(
                                pp[0:64, h * 64:(h + 1) * 64], lhsT=ZT[nxt][:, h, :], rhs=Pm[cur][:, h, :],
                                start=True, stop=False), r=[ZT[nxt], Pm[cur]], w=[pp], inc=False)
                            k.op("pe", lambda h=h, pp=pp, nxt=nxt, cur=cur: nc.tensor.matmul(
                                pp[0:64, h * 64:(h + 1) * 64], lhsT=ident_b[0:64, 0:64], rhs=Pm[cur][:, h, :],
                                start=False, stop=True), r=[ident_b, Pm[cur]], w=[pp], inc=(h == 7))
                        yield
                        k.op("act", lambda pp=pp, nxt=nxt, cur=cur: nc.scalar.copy(
                            out=Pm[nxt][:], in_=pp[0:64, :].rearrange("p (h i) -> p h i", h=8)),
                            r=[pp], w=[Pm[nxt]])
                        cur = nxt
                    G = Pm[cur]
                    yield
                    k.op("pool", lambda: nc.gpsimd.tensor_tensor(out=kd[:], in0=kvt[:, 0:8, :], in1=bc(edc),
                                                                 op=ALU.mult), r=[kvt, smT], w=[kd])
                    while scan_done[0] < gidx:
                        yield
                    pS = nps3(par)
                    for h in range(8):
                        k.op("pe", lambda h=h, pS=pS: nc.tensor.matmul(
                            pS[0:64, h * 64:(h + 1) * 64], lhsT=qk_n[:, 8 + h, cs], rhs=Sbf[:, h, :],
                            start=True, stop=True), r=[qk_n, Sbf], w=[pS], inc=(h == 7))
                    pO1 = nps3(par)
                    for h in range(8):
                        k.op("pe", lambda h=h, pO1=pO1: nc.tensor.matmul(
                            pO1[0:64, h * 64:(h + 1) * 64], lhsT=qk_n[:, h, cs], rhs=Sbf[:, h, :],
                            start=True, stop=True), r=[qk_n, Sbf], w=[pO1], inc=(h == 7))
                    k.op("dve", lambda pS=pS: nc.vector.tensor_tensor(
                        out=tmpf[:], in0=pS[0:64, :].rearrange("p (h i) -> p h i", h=8), in1=bc(ecol), op=ALU.mult),
                        r=[pS, smT], w=[tmpf])
                    k.op("dve", lambda: nc.vector.tensor_tensor(out=rbf[:], in0=kvt[:, 8:16, :], in1=tmpf[:],
                                                                op=ALU.subtract), r=[kvt, tmpf], w=[rbf])
                    yield
                    pT_ = nps3(par)
                    for h in range(8):
                        k.op("pe", lambda h=h, pT_=pT_: nc.tensor.matmul(
                            pT_[0:64, h * 64:(h + 1) * 64], lhsT=G[:, h, :], rhs=rbf[:, h, :],
                            start=True, stop=True), r=[G, rbf], w=[pT_], inc=(h == 7))
                    k.op("dve", lambda pT_=pT_: nc.vector.tensor_tensor(
                        out=vnew[:], in0=pT_[0:64, :].rearrange("p (h i) -> p h i", h=8), in1=bc(bcol), op=ALU.mult),
                        r=[pT_, smT], w=[vnew])
                    k.op("dve", lambda pO1=pO1: nc.vector.tensor_tensor(
                        out=of[:], in0=pO1[0:64, :].rearrange("p (h i) -> p h i", h=8), in1=bc(ecol), op=ALU.mult),
                        r=[pO1, smT], w=[of])
                    k.op("dve", lambda: nc.vector.tensor_tensor(out=Sst[:], in0=Sst[:], in1=bc(sdc), op=ALU.mult),
                         r=[Sst, smT], w=[Sst])
                    yield
                    pU = nps3(par)
                    for h in range(8):
                        k.op("pe", lambda h=h, pU=pU: nc.tensor.matmul(
                            pU[0:64, h * 64:(h + 1) * 64], lhsT=kd[:, h, :], rhs=vnew[:, h, :],
                            start=True, stop=True), r=[kd, vnew], w=[pU], inc=(h == 7))
                    k.op("dve", lambda pU=pU: nc.vector.tensor_tensor(
                        out=Sbf[:], in0=pU[0:64, :].rearrange("p (h i) -> p h i", h=8), in1=Sst[:], op=ALU.add),
                        r=[pU, Sst], w=[Sbf])
                    scan_done[0] = gidx + 1
                    k.op("dve", lambda pU=pU: nc.vector.tensor_tensor(
                        out=Sst[:], in0=pU[0:64, :].rearrange("p (h i) -> p h i", h=8), in1=Sst[:], op=ALU.add),
                        r=[pU, Sst], w=[Sst])
                    yield
                    pO2 = nps3(par)
                    for h in range(8):
                        k.op("pe", lambda h=h, pO2=pO2: nc.tensor.matmul(
                            pO2[0:64, h * 64:(h + 1) * 64], lhsT=AT[:, h, :], rhs=vnew[:, h, :],
                            start=True, stop=True), r=[AT, vnew], w=[pO2], inc=(h == 7))
                    yield
                    k.op("dve", lambda pO2=pO2: nc.vector.tensor_tensor(
                        out=of[:], in0=pO2[0:64, :].rearrange("p (h i) -> p h i", h=8), in1=of[:], op=ALU.add),
                        r=[pO2, of], w=[of])
                    yield
                    k.op("pool", lambda: nc.gpsimd.tensor_tensor(out=tmpf[:], in0=of[:], in1=of[:], op=ALU.mult),
                         r=[of], w=[tmpf])
                    osq = sm[:, 7, :]
                    yield
                    k.op("dve", lambda: nc.vector.tensor_reduce(out=osq, in_=tmpf[:], axis=AX.X, op=ALU.add),
                         r=[tmpf], w=[sm.sub("osq")])
                    yield
                    k.op("act", lambda: nc.scalar.activation(out=osq, in_=osq, func=AF.Ln, scale=1.0 / 64,
                                                             bias=epsb[0:64, :]),
                         r=[sm.sub("osq"), epsb], w=[sm.sub("osq")])
                    yield
                    k.op("act", lambda: nc.scalar.activation(out=osq, in_=osq, func=AF.Exp, scale=-0.5),
                         r=[sm.sub("osq")], w=[sm.sub("osq")])
                    yield
                    k.op("dve", lambda: nc.vector.tensor_tensor(out=of[:], in0=of[:], in1=bc(osq), op=ALU.mult),
                         r=[of, sm.sub("osq")], w=[of])
                    pz_ = nps3(par)
                    yield
                    for kc in range(8):
                        k.op("pe", lambda kc=kc, pz_=pz_: nc.tensor.matmul(
                            pz_[0:64, :], lhsT=uT[:, kc, cs], rhs=wi[:, kc, 1536:2048],
                            start=(kc == 0), stop=(kc == 7)), r=[wi, uT], w=[pz_], inc=(kc == 7))
                    yield
                    k.op("act", lambda pz_=pz_: nc.scalar.activation(out=zg[:], in_=pz_[0:64, :], func=AF.Exp,
                                                                     scale=-1.0), r=[pz_], w=[zg])
                    yield
                    k.op("act", lambda: nc.scalar.activation(out=zg[:], in_=zg[:], func=AF.Ln, bias=oneb[0:64, :]),
                         r=[zg, oneb], w=[zg])
                    yield
                    k.op("act", lambda: nc.scalar.activation(out=zg[:], in_=zg[:], func=AF.Exp, scale=-1.0),
                         r=[zg], w=[zg])
                    yield
                    k.op("dve", lambda pz_=pz_: nc.vector.tensor_tensor(out=zg[:], in0=pz_[0:64, :], in1=zg[:],
                                                                        op=ALU.mult), r=[pz_, zg], w=[zg])
                    yield
                    k.op("pool", lambda: nc.gpsimd.tensor_tensor(out=zg[:], in0=zg[:], in1=g_gdn[:], op=ALU.mult),
                         r=[zg, g_gdn], w=[zg])
                    yield
                    k.op("dve", lambda: nc.vector.tensor_tensor(
                        out=ybf[:], in0=of[:].rearrange("p h i -> p (h i)"), in1=zg[:], op=ALU.mult),
                        r=[of, zg], w=[ybf])
                    yield
                    pb = npb()
                    for m in range(4):
                        k.op("pe", lambda m=m, pb=pb: nc.tensor.transpose(
                            out=pb[:, m * 64:(m + 1) * 64], in_=ybf[:, m * 128:(m + 1) * 128],
                            identity=ident_b[0:64, 0:64]), r=[ybf, ident_b], w=[pb], inc=(m == 3))
                    k.op("act", lambda pb=pb: nc.scalar.copy(
                        out=ygT[:, :, cs], in_=pb[:, 0:256].rearrange("p (m t) -> p m t", m=4)), r=[pb], w=[ygT])
                run_interleaved((chunk_stream(n) for n in range(8)), NSTR,
                                bg=(a1_stream(Tt + 1) if Tt + 1 < NT else None), stagger=12)
                for m in range(4):
                    k.dma("act", yT_d[m, :, t0:t0 + 512], ygT[:, m, :], r=[ygT], w=[yT_buf], own=ygT.sub("st"))
            k.wait_all("sp", [ygT, cq_t, ckv_t, kpe_t])

        h2_d = nc.dram_tensor("h2_scr", [S, D], F32, kind="Internal").ap()
        h2_buf = Buf("h2_dram")
        SCALE = 96.0 ** -0.5
        k.barrier()
        esB = ExitStack()
        with esB:
            cqT = k.sb("cqT", [128, 2, S], BF16, esB)
            ckvT = k.sb("ckvT", [128, S], BF16, esB)
            kper = k.sb("kper", [96, S], BF16, esB)
            for m in range(2):
                k.dma("sp", cqT[:, m, :], cq_d[m, :, :], r=[lat_buf], w=[cqT], own=cqT)
            k.dma("sp", ckvT[:], ckv_d[:, :], r=[lat_buf], w=[ckvT], own=ckvT)
            k.dma("sp", kper[64:96, :], kpe_d[:, :], r=[lat_buf], w=[kper], own=kper)
            wqb = k.sb("wqb", [128, 2, 768], BF16, esB)
            k.dma("pool", wqb[:], wqb_d[:, :, :], w=[wqb], own=wqb)
            wkvb = k.sb("wkvb", [128, 1024], BF16, esB)
            k.dma("pool", wkvb[:], wkvb_d[:, :], w=[wkvb], own=wkvb)
            wqrot = k.sb("wqrot", [128, 2, 8, 96], BF16, esB)
            wq4 = wqb[:].rearrange("p m (h c) -> p m h c", c=96)
            k.op("pool", lambda: nc.gpsimd.memset(wqrot[:], 0.0), w=[wqrot])
            k.op("act", lambda: nc.scalar.mul(out=wqrot[:, :, :, 64:80], in_=wq4[:, :, :, 80:96], mul=-1.0),
                 r=[wqb], w=[wqrot])
            k.op("act", lambda: nc.scalar.copy(out=wqrot[:, :, :, 80:96], in_=wq4[:, :, :, 64:80]), r=[wqb], w=[wqrot])
            cst = k.sb("cst", [96, 2, S], F32, esB)
            Vt = k.sb("Vt", [128, 32, 8, 65], BF16, esB)
            k.op("pool", lambda: nc.gpsimd.memset(Vt[:, :, :, 64:65], 1.0), w=[Vt])
            wkv3 = wkvb[:].rearrange("p (h c) -> p h c", c=128)
            for kb in range(32):
                pv = nps()
                k.op("pe", lambda kb=kb, pv=pv: nc.tensor.matmul(
                    pv[:, :], lhsT=ckvT[:, kb * 128:(kb + 1) * 128], rhs=wkv3[:, :, 64:128], start=True, stop=True),
                    r=[ckvT, wkvb], w=[pv])
                k.op("act", lambda kb=kb, pv=pv: nc.scalar.copy(
                    out=Vt[:, kb, :, 0:64], in_=pv[:, :].rearrange("p (h c) -> p h c", c=64)), r=[pv], w=[Vt])
            esB0 = ExitStack()
            b_rv = k.sb("b_rv", [96, 512], F32, esB0)
            b_rki = k.sb("b_rki", [96, 512], I32, esB0)
            b_rkf = k.sb("b_rkf", [96, 512], F32, esB0)
            b_rfr = k.sb("b_rfr", [96, 512], F32, esB0)
            b_posi = k.sb("b_posi", [96, 512], I32, esB0)
            for Tt in range(NT):
                sincos_tile(Tt * 512, b_posi, b_rv, b_rki, b_rkf, b_rfr, cst,
                            outf=lambda which, Tt=Tt: cst[64:96, which, Tt * 512:(Tt + 1) * 512])
            k.barrier()
            esB0.close()
            Qh2 = [k.sb("Qh%d" % i, [96, S], BF16, esB) for i in range(2)]
            Kh2 = [k.sb("Kh%d" % i, [96, S], BF16, esB) for i in range(2)]
            for i in range(2):
                k.op("dve", lambda i=i: nc.vector.tensor_copy(out=Kh2[i][64:96, :], in_=kper[64:96, :]),
                     r=[kper], w=[Kh2[i]])
            PT = [k.sb("PT%d" % i, [128, 512], BF16, esB) for i in range(5)]
            osb2 = [k.sb("osb%d" % i, [65, 512], F32, esB) for i in range(2)]
            rec = k.sb("rec", [64, 512], F32, esB)
            yh = k.sb("yh", [64, 512], F32, esB)
            yaT = k.sb("yaT", [128, 4, S], BF16, esB)
            sel = k.sb("sel", [65, 64], F32, esB)
            k.op("pool", lambda: nc.gpsimd.memset(sel[:], 0.0), w=[sel])
            k.op("pool", lambda: nc.gpsimd.memset(sel[64:65, :], 1.0), w=[sel])
            qt1 = k.sb("qt1", [96, 512], F32, esB)
            qt2 = k.sb("qt2", [96, 512], F32, esB)

            def prep_tile(h, Tt):
                Qh, Kh = Qh2[h % 2], Kh2[h % 2]
                ts_ = slice(Tt * 512, (Tt + 1) * 512)
                pk = nps_p()
                k.op("pe", lambda: nc.tensor.matmul(
                    pk[0:64, :], lhsT=wkvb[:, h * 128:h * 128 + 64], rhs=ckvT[:, ts_], start=True, stop=True),
                    r=[wkvb, ckvT], w=[pk])
                k.op("dve", lambda: nc.vector.tensor_copy(out=Kh[0:64, ts_], in_=pk[0:64, :]), r=[pk], w=[Kh])
                pq = nps_p()
                for m in range(2):
                    k.op("pe", lambda m=m: nc.tensor.matmul(
                        pq[0:96, :], lhsT=wqb[:, m, h * 96:(h + 1) * 96], rhs=cqT[:, m, ts_],
                        start=(m == 0), stop=(m == 1)), r=[wqb, cqT], w=[pq], inc=(m == 1))
                k.op("dve", lambda: nc.vector.tensor_copy(out=Qh[0:64, ts_], in_=pq[0:64, :]), r=[pq], w=[Qh])
                k.op("dve", lambda: nc.vector.tensor_tensor(
                    out=qt1[64:96, :], in0=pq[64:96, :], in1=cst[64:96, 0, ts_], op=ALU.mult),
                    r=[pq, cst], w=[qt1])
                pr2 = nps_p()
                for m in range(2):
                    k.op("pe", lambda m=m: nc.tensor.matmul(
                        pr2[0:96, :], lhsT=wqrot[:, m, h, :], rhs=cqT[:, m, ts_],
                        start=(m == 0), stop=(m == 1)), r=[wqrot, cqT], w=[pr2], inc=(m == 1))
                k.op("dve", lambda: nc.vector.tensor_tensor(
                    out=qt2[64:96, :], in0=pr2[64:96, :], in1=cst[64:96, 1, ts_], op=ALU.mult),
                    r=[pr2, cst], w=[qt2])
                k.op("dve", lambda: nc.vector.tensor_tensor(
                    out=Qh[64:96, ts_], in0=qt1[64:96, :], in1=qt2[64:96, :], op=ALU.add),
                    r=[qt1, qt2], w=[Qh])

            sc_rr = [0]
            pr_rr = [0]
            PREPB = [TView(PB[0], F32), TView(PB[1], F32)]

            def nps_b():
                sc_rr[0] = (sc_rr[0] + 1) % 4
                return PS[sc_rr[0]]

            def nps_p():
                pr_rr[0] = (pr_rr[0] + 1) % 2
                return PREPB[pr_rr[0]]

            for Tt in range(NT):
                prep_tile(0, Tt)
            ptr = [0]
            for h in range(8):
                Qh, Kh = Qh2[h % 2], Kh2[h % 2]
                steps = []
                for Qt in range(NT):
                    for kb in range(4 * Qt + 4):
                        steps.append((Qt, kb))
                nst = len(steps)
                info = {}

                def emit_qk(i):
                    Qt, kb = steps[i]
                    d = kb - 4 * Qt
                    c0 = 128 * d if d > 0 else 0
                    qs = slice(Qt * 512 + c0, (Qt + 1) * 512)
                    cs_ = slice(c0, 512)
                    sp_ = nps_b()
                    info[i] = (sp_, cs_, c0, d)
                    k.op("pe", lambda: nc.tensor.matmul(
                        sp_[:, cs_], lhsT=Kh[0:96, kb * 128:(kb + 1) * 128], rhs=Qh[0:96, qs],
                        start=True, stop=True), r=[Kh, Qh], w=[sp_])

                def emit_rest(i):
                    Qt, kb = steps[i]
                    sp_, cs_, c0, d = info.pop(i)
                    nkb = 4 * Qt + 4
                    po = PS[4 + Qt % 2]
                    pt = PT[ptr[0] % 5]
                    ptr[0] += 1
                    k.op("act", lambda: nc.scalar.activation(
                        out=pt[:, cs_], in_=sp_[:, cs_], func=AF.Exp, scale=SCALE), r=[sp_], w=[pt])
                    if d >= 0:
                        k.op("pool", lambda: nc.gpsimd.affine_select(
                            out=pt[:, c0:c0 + 128], in_=pt[:, c0:c0 + 128], pattern=[[1, 128]],
                            compare_op=ALU.is_ge, fill=REG_ZERO, base=0, channel_multiplier=-1),
                            r=[pt], w=[pt])
                    k.op("pe", lambda: nc.tensor.matmul(
                        po[0:65, cs_], lhsT=Vt[:, kb, h, :], rhs=pt[:, cs_], start=(kb == 0),
                        stop=(kb == nkb - 1)), r=[Vt, pt], w=[po], inc=True)
                    if kb == nkb - 1:
                        osb = osb2[Qt % 2]
                        k.op("dve", lambda: nc.vector.tensor_copy(out=osb[:], in_=po[0:65, :]), r=[po], w=[osb])
                        dd("osb", osb, osb[:], [65, 512], F32)
                        return Qt
                    return None

                def finalize(Qt):
                    osb = osb2[Qt % 2]
                    pd = PS[4 + Qt % 2]
                    k.op("pe", lambda: nc.tensor.matmul(pd[0:64, :], lhsT=sel[:, :], rhs=osb[:, :],
                                                        start=True, stop=True), r=[sel, osb], w=[pd])
                    k.op("dve", lambda: nc.vector.reciprocal(out=rec[:], in_=pd[0:64, :]), r=[pd], w=[rec])
                    k.op("dve", lambda: nc.vector.tensor_tensor(out=yh[:], in0=osb[0:64, :], in1=rec[:],
                                                                op=ALU.mult), r=[osb, rec], w=[yh])
                    p0 = (h % 2) * 64
                    k.op("dve", lambda: nc.vector.tensor_copy(
                        out=yaT[p0:p0 + 64, h // 2, Qt * 512:(Qt + 1) * 512], in_=yh[:]), r=[yh], w=[yaT])

                LOOK = 3
                for i in range(min(LOOK, nst)):
                    emit_qk(i)
                pending_fin = []
                prep_next = list(range(NT)) if h < 7 else []
                for i in range(nst):
                    if i + LOOK < nst:
                        emit_qk(i + LOOK)
                    fin = emit_rest(i)
                    pending_fin = [(q, c - 1) for (q, c) in pending_fin]
                    while pending_fin and pending_fin[0][1] <= 0:
                        finalize(pending_fin.pop(0)[0])
                    if fin is not None:
                        pending_fin.append((fin, 2))
                    if prep_next and i % 16 == 8:
                        prep_tile(h + 1, prep_next.pop(0))
                for q, c in pending_fin:
                    finalize(q)
                for Tt in prep_next:
                    prep_tile(h + 1, Tt)
            dd("yaT", yaT, yaT[:], [128, 4, S], BF16)
            ysq = k.sb("ysq", [128, 4, 512], BF16, esB)
            yrs = k.sb("yrs", [128, 512], F32, esB)
            yo = k.sb("yo", [128, 4, 512], BF16, esB)
            for Tt in range(NT):
                ts_ = slice(Tt * 512, (Tt + 1) * 512)
                k.op("dve", lambda ts_=ts_: nc.vector.tensor_tensor(out=ysq[:], in0=yaT[:, :, ts_],
                                                                    in1=yaT[:, :, ts_], op=ALU.mult),
                     r=[yaT], w=[ysq])
                pn = PS[Tt % 4]
                for m in range(4):
                    k.op("pe", lambda m=m, pn=pn: nc.tensor.matmul(pn[:, :], lhsT=ones_b[:, :], rhs=ysq[:, m, :],
                                                                   start=(m == 0), stop=(m == 3)),
                         r=[ones_b, ysq], w=[pn], inc=(m == 3))
                k.op("act", lambda pn=pn: nc.scalar.activation(out=yrs[:], in_=pn[:, :], func=AF.Ln,
                                                               scale=1.0 / 512, bias=epsb[:, :]),
                     r=[pn, epsb], w=[yrs])
                k.op("act", lambda: nc.scalar.activation(out=yrs[:], in_=yrs[:], func=AF.Exp, scale=-0.5),
                     r=[yrs], w=[yrs])
                for m in range(4):
                    k.op("dve", lambda m=m, ts_=ts_: nc.vector.scalar_tensor_tensor(
                        out=yo[:, m, :], in0=yaT[:, m, ts_], scalar=gmo[:, m:m + 1], in1=yrs[:],
                        op0=ALU.mult, op1=ALU.mult), r=[yaT, gmo, yrs], w=[yo])
                dd("yo", yo, yo[:], [128, 4, 512], BF16)
                dd("yrs", yrs, yrs[:], [128, 512], F32)
                if Tt == 1:
                    dd("yo1", yo, yo[:], [128, 4, 512], BF16)
                    dd("yrs1", yrs, yrs[:], [128, 512], F32)
                    dd("ysq1", ysq, ysq[:], [128, 4, 512], BF16)
                for m in range(4):
                    k.dma("act", yT_d[4 + m, :, ts_], yo[:, m, :], r=[yo], w=[yT_buf], own=yo.sub("st"))
            k.wait_all("sp", [yo])

        k.barrier()
        esAB.close()
        esW2 = ExitStack()
        wpp = k.sb("wpp", [128, 2, 1024], BF16, esW2)
        wpg = k.sb("wpg", [128, 8, 1024], BF16, esW2)
        esW1 = ExitStack()
        wup = k.sb("wup", [128, 8, 4096], BF16, esW1)
        wdn = k.sb("wdn", [128, 32, 1024], BF16, esW1)
        g_mlp = k.sb("g_mlp", [128, D], F32, esW1)
        esC = ExitStack()
        with esC:
            wout = k.sb("wout", [128, 8, 1024], BF16, esC)
            for kc in range(8):
                k.dma("pool", wout[:, kc, :], wout_d[:, kc, :], w=[wout], own=wout.sub("ld"))
            for kc in range(8):
                for q4 in range(4):
                    k.dma("pool", wup[:, kc, q4 * 1024:(q4 + 1) * 1024], wup_d[:, kc, q4 * 1024:(q4 + 1) * 1024],
                          w=[wup], own=wup.sub("ld"))
            for fc in range(32):
                k.dma("pool", wdn[:, fc, :], wdn_d[:, fc, :], w=[wdn], own=wdn.sub("ld"))
            k.dma("pool", g_mlp[:], g_mlp_d[:, :], w=[g_mlp], own=g_mlp)
            for kc in range(2):
                k.dma("pool", wpp[:, kc, :], wpp_d[:, kc, :], w=[wpp], own=wpp.sub("ld"))
            for kc in range(8):
                k.dma("pool", wpg[:, kc, :], wpg_d[:, kc, :], w=[wpg], own=wpg.sub("ld"))
            yt2 = [k.sb("yt%d" % i, [128, 8, 512], BF16, esC) for i in range(2)]
            xc = [k.sb("xc%d" % i, [128, D], F32, esC) for i in range(2)]
            hc = [k.sb("hc%d" % i, [128, D], F32, esC) for i in range(2)]

            def c1_block(Tt, blk):
                yt = yt2[Tt % 2]
                r0 = Tt * 512 + blk * 128
                xb = xc[blk % 2]
                hb = hc[blk % 2]
                k.dma("sp", xb[:], x_d[r0:r0 + 128, :], w=[xb], own=xb)
                yield
                for n2 in range(2):
                    pp = nps(blk % 2)
                    for kc in range(8):
                        k.op("pe", lambda kc=kc, pp=pp, n2=n2: nc.tensor.matmul(
                            pp[:, :], lhsT=yt[:, kc, blk * 128:(blk + 1) * 128],
                            rhs=wout[:, kc, n2 * 512:(n2 + 1) * 512], start=(kc == 0), stop=(kc == 7)),
                            r=[yt, wout], w=[pp], inc=(kc == 7))
                    k.op("dve", lambda pp=pp, n2=n2: nc.vector.tensor_tensor(
                        out=hb[:, n2 * 512:(n2 + 1) * 512], in0=pp[:, :], in1=xb[:, n2 * 512:(n2 + 1) * 512],
                        op=ALU.add), r=[pp, xb], w=[hb])
                    yield
                k.dma("act", h1_d[r0:r0 + 128, :], hb[:], r=[hb], w=[h1_buf], own=hb.sub("st"))

            for Tt in range(NT):
                for kc in range(8):
                    k.dma("sp", yt2[Tt % 2][:, kc, :], yT_d[kc, :, Tt * 512:(Tt + 1) * 512], r=[yT_buf],
                          w=[yt2[Tt % 2]], own=yt2[Tt % 2])
                run_interleaved((c1_block(Tt, b) for b in range(4)), 2)
            k.wait_all("sp", hc)

        k.barrier()
        esD = ExitStack()
        with esD:
            ht = [k.sb("ht%d" % i, [128, D], F32, esD) for i in range(2)]
            ho = [k.sb("ho%d" % i, [128, D], F32, esD) for i in range(2)]
            ubm = [k.sb("ubm%d" % i, [128, D], BF16, esD) for i in range(2)]
            uTm = [k.sb("uTm%d" % i, [128, 8, 128], BF16, esD) for i in range(2)]
            ssm = [k.sb("ssm%d" % i, [128, 1], F32, esD) for i in range(2)]
            hT = [k.sb("hT%d" % i, [128, 32, 128], BF16, esD) for i in range(2)]
            rl = [[k.sb("rl%d_%d" % (i, j), [128, 512], BF16, esD) for j in range(2)] for i in range(2)]
            def mlp_block(blk):
                i2 = blk % 2
                r0 = blk * 128
                k.dma("sp", ht[i2][:], h1_d[r0:r0 + 128, :], r=[h1_buf], w=[ht[i2]], own=ht[i2])
                rms_tm(ht[i2][:], [ht[i2]], g_mlp, ubm[i2][:], [ubm[i2]], ubm[i2], ssm[i2])
                dd("ubm", ubm[i2], ubm[i2][:], [128, D], BF16)
                yield
                to_fm(ubm[i2], [ubm[i2]], uTm[i2], [uTm[i2]], 0, 8, p=i2)
                yield
                dd("uTm", uTm[i2], uTm[i2][:], [128, 8, 128], BF16)
                for f4 in range(8):
                    pp = nps(i2)
                    for j in range(4):
                        fc = f4 * 4 + j
                        for kc in range(8):
                            k.op("pe", lambda kc=kc, pp=pp, fc=fc, j=j, i2=i2: nc.tensor.matmul(
                                pp[:, j * 128:(j + 1) * 128], lhsT=wup[:, kc, fc * 128:(fc + 1) * 128],
                                rhs=uTm[i2][:, kc, :], start=(kc == 0), stop=(kc == 7)),
                                r=[wup, uTm[i2]], w=[pp], inc=(kc == 7 and j == 3))
                    rb = rl[i2][f4 % 2]
                    if dbg is not None and 'ppc' in dbg and blk == 0 and f4 == 0:
                        ppc = k.sb("ppc", [128, 512], F32, esD)
                        k.op("dve", lambda pp=pp: nc.vector.tensor_copy(out=ppc[:], in_=pp[:, :]), r=[pp], w=[ppc])
                        dd("ppc", ppc, ppc[:], [128, 512], F32)
                    k.op("act", lambda pp=pp, rb=rb: nc.scalar.activation(out=rb[:], in_=pp[:, :], func=AF.Relu),
                         r=[pp], w=[rb])
                    dd("rl", rb, rb[:], [128, 512], BF16)
                    k.op("pool", lambda rb=rb, f4=f4, i2=i2: nc.gpsimd.tensor_tensor(
                        out=hT[i2][:, f4 * 4:(f4 + 1) * 4, :], in0=rb[:].rearrange("p (j t) -> p j t", j=4),
                        in1=rb[:].rearrange("p (j t) -> p j t", j=4), op=ALU.mult), r=[rb], w=[hT[i2]])
                    yield
                dd("hT", hT[i2], hT[i2][:], [128, 32, 128], BF16)
                dd("wup", wup, wup[:, 0, :], [128, 4096], BF16)
                dd("wdn", wdn, wdn[:, 0, :], [128, 1024], BF16)
                for n2 in range(2):
                    pp = nps(i2)
                    for fc in range(32):
                        k.op("pe", lambda fc=fc, pp=pp, n2=n2, i2=i2: nc.tensor.matmul(
                            pp[:, :], lhsT=hT[i2][:, fc, :], rhs=wdn[:, fc, n2 * 512:(n2 + 1) * 512],
                            start=(fc == 0), stop=(fc == 31)), r=[hT[i2], wdn], w=[pp], inc=(fc == 31))
                    k.op("dve", lambda pp=pp, n2=n2, i2=i2: nc.vector.tensor_tensor(
                        out=ho[i2][:, n2 * 512:(n2 + 1) * 512], in0=pp[:, :], in1=ht[i2][:, n2 * 512:(n2 + 1) * 512],
                        op=ALU.add), r=[pp, ht[i2]], w=[ho[i2]])
                    yield
                k.dma("sp", h2_d[r0:r0 + 128, :], ho[i2][:], r=[ho[i2]], w=[h2_buf], own=ho[i2].sub("st"))
            run_interleaved((mlp_block(b) for b in range(32)), 2, stagger=6)
            k.wait_all("sp", ho)

        k.barrier()
        esW1.close()
        esE = ExitStack()
        with esE:
            gB = {}
            for nm, src in [("post", g_post_d), ("gate", g_gate_d), ("fin", g_fin_d)]:
                gB[nm] = k.sb("gB_" + nm, [128, D], F32, esE)
                k.dma("sp", gB[nm][:], src[:, :], w=[gB[nm]], own=gB[nm])
            h2t = [k.sb("h2t%d" % i, [128, D], F32, esE) for i in range(3)]
            pin = [k.sb("pin%d" % i, [128, 256], F32, esE) for i in range(3)]
            pbf = [k.sb("pbf%d" % i, [128, 256], BF16, esE) for i in range(3)]
            pTt = [k.sb("pTt%d" % i, [128, 2, 128], BF16, esE) for i in range(3)]
            er = [k.sb("er%d" % i, [128, D], F32, esE) for i in range(3)]
            en = [k.sb("en%d" % i, [128, D], F32, esE) for i in range(3)]
            ug = [k.sb("ug%d" % i, [128, D], BF16, esE) for i in range(3)]
            uTg = [k.sb("uTg%d" % i, [128, 8, 128], BF16, esE) for i in range(3)]
            gt = [k.sb("gt%d" % i, [128, D], F32, esE) for i in range(3)]
            h3 = [k.sb("h3%d" % i, [128, D], F32, esE) for i in range(3)]
            fo = [k.sb("fo%d" % i, [128, D], F32, esE) for i in range(3)]
            jk2 = [k.sb("jk%d" % i, [128, D], BF16, esE) for i in range(3)]
            sse = [k.sb("sse%d" % i, [128, 1], F32, esE) for i in range(3)]
            ssg = [k.sb("ssg%d" % i, [128, 1], F32, esE) for i in range(3)]
            ssf = [k.sb("ssf%d" % i, [128, 1], F32, esE) for i in range(3)]
            def ple_block(blk):
                i2 = blk % 3
                r0 = blk * 128
                k.dma("sp", h2t[i2][:], h2_d[r0:r0 + 128, :], r=[h2_buf], w=[h2t[i2]], own=h2t[i2])
                k.dma("sp", pin[i2][:], p_d[r0:r0 + 128, :], w=[pin[i2]], own=pin[i2])
                k.op("dve", lambda i2=i2: nc.vector.tensor_copy(out=pbf[i2][:], in_=pin[i2][:]),
                     r=[pin[i2]], w=[pbf[i2]])
                to_fm(pbf[i2], [pbf[i2]], pTt[i2], [pTt[i2]], 0, 2, p=None)
                yield
                for n2 in range(2):
                    pp = nps()
                    for kc in range(2):
                        k.op("pe", lambda kc=kc, pp=pp, n2=n2, i2=i2: nc.tensor.matmul(
                            pp[:, :], lhsT=pTt[i2][:, kc, :], rhs=wpp[:, kc, n2 * 512:(n2 + 1) * 512],
                            start=(kc == 0), stop=(kc == 1)), r=[pTt[i2], wpp], w=[pp], inc=(kc == 1))
                    k.op("act", lambda pp=pp, n2=n2, i2=i2: nc.scalar.copy(
                        out=er[i2][:, n2 * 512:(n2 + 1) * 512], in_=pp[:, :]), r=[pp], w=[er[i2]])
                yield
                rms_tm(er[i2][:], [er[i2]], gB["post"], en[i2][:], [en[i2]], jk2[i2], sse[i2])
                yield
                rms_tm(h2t[i2][:], [h2t[i2]], gB["gate"], ug[i2][:], [ug[i2]], jk2[i2], ssg[i2])
                yield
                to_fm(ug[i2], [ug[i2]], uTg[i2], [uTg[i2]], 0, 8, p=None)
                yield
                for n2 in range(2):
                    pp = nps()
                    for kc in range(8):
                        k.op("pe", lambda kc=kc, pp=pp, n2=n2, i2=i2: nc.tensor.matmul(
                            pp[:, :], lhsT=uTg[i2][:, kc, :], rhs=wpg[:, kc, n2 * 512:(n2 + 1) * 512],
                            start=(kc == 0), stop=(kc == 7)), r=[uTg[i2], wpg], w=[pp], inc=(kc == 7))
                    gs = gt[i2][:, n2 * 512:(n2 + 1) * 512]
                    k.op("act", lambda pp=pp, gs=gs: nc.scalar.activation(out=gs, in_=pp[:, :], func=AF.Exp,
                                                                          scale=-1.0), r=[pp], w=[gt[i2]])
                yield
                k.op("act", lambda i2=i2: nc.scalar.activation(out=gt[i2][:], in_=gt[i2][:], func=AF.Ln,
                                                               bias=oneb[:, :]), r=[gt[i2], oneb], w=[gt[i2]])
                k.op("act", lambda i2=i2: nc.scalar.activation(out=gt[i2][:], in_=gt[i2][:], func=AF.Exp,
                                                               scale=-1.0), r=[gt[i2]], w=[gt[i2]])
                k.op("pool", lambda i2=i2: nc.gpsimd.tensor_tensor(out=gt[i2][:], in0=gt[i2][:], in1=en[i2][:],
                                                                   op=ALU.mult), r=[gt[i2], en[i2]], w=[gt[i2]])
                k.op("dve", lambda i2=i2: nc.vector.tensor_tensor(out=h3[i2][:], in0=h2t[i2][:], in1=gt[i2][:],
                                                                  op=ALU.add), r=[h2t[i2], gt[i2]], w=[h3[i2]])
                yield
                rms_tm(h3[i2][:], [h3[i2]], gB["fin"], fo[i2][:], [fo[i2]], jk2[i2], ssf[i2])
                k.dma("sp", out_d[r0:r0 + 128, :], fo[i2][:], r=[fo[i2]], w=[], own=fo[i2].sub("st"))
            run_interleaved((ple_block(b) for b in range(32)), 3, stagger=3)
            k.wait_all("sp", fo)
        esW2.close()
        if dbg is not None:
            dsb = Buf("dbgdma")
            k.dma("sp", dbg_y[:, :, :], yT_d[:, :, :], r=[yT_buf], w=[], own=dsb)
            k.dma("sp", dbg_h1[:, :], h1_d[:, :], r=[h1_buf], w=[], own=dsb)
            k.dma("sp", dbg_h2[:, :], h2_d[:, :], r=[h2_buf], w=[], own=dsb)
            k.eng["sp"].wait_ge(dsb.sem, dsb.dtot)
    return nc


def _lay(w, kc):
    K, N = w.shape
    return np.ascontiguousarray(w.reshape(kc, 128, N).transpose(1, 0, 2))


def make_inmaps(inp):
    f = lambda a: np.ascontiguousarray(np.asarray(a, dtype=np.float32))
    bc128 = lambda v: np.ascontiguousarray(np.broadcast_to(f(v).reshape(1, -1), (128, v.size)))
    common = {
        "w_in": _lay(f(inp["w_in"][0]), 8),
        "w_out": _lay(f(inp["w_out"][0]), 8),
        "w_up": _lay(f(inp["w_up"][0]), 8),
        "w_down": _lay(f(inp["w_down"][0]), 32),
        "w_ple_proj": _lay(f(inp["w_ple_proj"][0]), 2),
        "w_ple_gate": _lay(f(inp["w_ple_gate"][0]), 8),
        "w_q_b": _lay(f(inp["w_q_b"][0]), 2),
        "w_kv_b": f(inp["w_kv_b"][0]),
        "g_mix": bc128(inp["mix_norm_w"][0]),
        "g_mlp": bc128(inp["mlp_norm_w"][0]),
        "g_post": bc128(inp["ple_post_norm_w"][0]),
        "g_gate": bc128(inp["ple_gate_norm_w"][0]),
        "g_fin": bc128(inp["final_norm_w"]),
        "g_gdn": np.ascontiguousarray(np.broadcast_to(np.tile(f(inp["gdn_norm_w"][0]), 8).reshape(1, 512), (64, 512))),
        "cw": np.ascontiguousarray(f(inp["conv_w"][0]).T.reshape(24, 64, 4).transpose(1, 0, 2)),
        "alog": np.ascontiguousarray(np.broadcast_to(f(inp["A_log"][0]).reshape(1, 8), (64, 8))),
        "dtb": np.ascontiguousarray(np.broadcast_to(f(inp["dt_bias"][0]).reshape(1, 8), (64, 8))),
        "gq": np.ascontiguousarray(f(inp["q_norm_w"][0]).reshape(2, 128).T),
        "gkv": np.ascontiguousarray(f(inp["kv_norm_w"][0]).reshape(1, 128).T),
        "gmo": np.ascontiguousarray(f(inp["mla_out_norm_w"][0]).reshape(4, 128).T),
    }
    invf = np.zeros((96, 1), np.float32)
    fr = (10000.0 ** (-np.arange(0, 32, 2, dtype=np.float32) / 32)).astype(np.float32)
    invf[64:80, 0] = fr
    invf[80:96, 0] = fr
    common["invf"] = invf
    maps = []
    x = np.asarray(inp["x"], dtype=np.float32)
    p = np.asarray(inp["p"], dtype=np.float32)
    pos = np.asarray(inp["positions"], dtype=np.int32)
    for b in range(8):
        m = dict(common)
        m["x"] = np.ascontiguousarray(x[b])
        m["p"] = np.ascontiguousarray(p[0, b])
        m["pos"] = np.ascontiguousarray(pos[b].reshape(1, S))
        maps.append(m)
    return maps


def kernel(**inputs):
    nc = build_nc()
    maps = make_inmaps(inputs)
    res = run_bass_kernel_spmd(nc, maps, core_ids=list(range(8)))
    return np.stack([np.asarray(r["out"], dtype=np.float32) for r in res.results], axis=0)
```

```python
import numpy as np
import concourse.bass as bass
import concourse.mybir as mybir
from concourse.bass_utils import run_bass_kernel_spmd
from contextlib import ExitStack

F32 = mybir.dt.float32
BF16 = mybir.dt.bfloat16
I32 = mybir.dt.int32
AF = mybir.ActivationFunctionType
ALU = mybir.AluOpType
AX = mybir.AxisListType

S = 4096
D = 1024
NT = 8
EPS = 1e-6
STRICT_SAME = True
RELAX_SAME = False


class Buf:
    __slots__ = ("name", "w", "r", "sem", "dtot", "excl")

    def __init__(self, name):
        self.name = name
        self.excl = False
        self.w = None
        self.r = {}
        self.sem = None
        self.dtot = 0


class T:
    def __init__(self, t, name):
        self.t = t
        self.name = name
        self.b = Buf(name)
        self.subs = {}

    def __getitem__(self, idx):
        return self.t[idx]

    def sub(self, key):
        if key not in self.subs:
            self.subs[key] = Buf(f"{self.name}.{key}")
        return self.subs[key]


class TView(T):
    def __init__(self, base, dt):
        self.t = base.t
        self.name = base.name
        self.b = base.b
        self.subs = base.subs
        self.v = base.t[:, :].bitcast(dt)

    def __getitem__(self, idx):
        return self.v[idx]


class TRe(T):
    def __init__(self, base, pattern, **kw):
        self.t = base.t
        self.name = base.name
        self.b = base.b
        self.subs = base.subs
        self.v = base.t[:].rearrange(pattern, **kw)

    def __getitem__(self, idx):
        return self.v[idx]


class KB:
    def __init__(self, nc, es):
        self.nc = nc
        self.es = es
        self.eng = {"pe": nc.tensor, "act": nc.scalar, "dve": nc.vector, "pool": nc.gpsimd, "sp": nc.sync}
        self.sem = {}
        for n in ["pe", "act", "dve", "pool"]:
            self.sem[n] = es.enter_context(nc.semaphore("c_" + n))
        self.cnt = {n: 0 for n in self.sem}
        self.seen = {n: {} for n in self.eng}
        self.pend = {n: ([], []) for n in self.eng}
        self.nsem = 0
        self.psrr = 0
        self.dbufs = []

    def sb(self, name, shape, dt, es=None):
        es = es or self.es
        return T(es.enter_context(self.nc.sbuf_tensor("s_" + name, list(shape), dt)), name)

    def ps(self, name, shape, dt):
        t = T(self.es.enter_context(self.nc.psum_tensor("p_" + name, list(shape), dt)), name)
        t.b.excl = True
        return t

    def dsem(self, buf):
        if buf.sem is None:
            buf.sem = self.es.enter_context(self.nc.semaphore("d_%d" % self.nsem))
            self.nsem += 1
            self.dbufs.append(buf)
        return buf.sem

    def barrier(self):
        for e in self.eng:
            for n in self.sem:
                if n != e and self.cnt[n] > self.seen[e].get(n, 0):
                    self.eng[e].wait_ge(self.sem[n], self.cnt[n])
                    self.seen[e][n] = self.cnt[n]
            for b in self.dbufs:
                key = "d:" + b.name
                if b.dtot > self.seen[e].get(key, 0):
                    self.eng[e].wait_ge(b.sem, b.dtot)
                    self.seen[e][key] = b.dtot

    @staticmethod
    def _bufs(xs):
        out = []
        for x in xs:
            out.append(x.b if isinstance(x, T) else x)
        return out

    def _wait(self, e, deps):
        for key, (sem, val) in deps.items():
            if key == e and (e == "pe" or not STRICT_SAME):
                continue
            if self.seen[e].get(key, 0) < val:
                self.eng[e].wait_ge(sem, val)
                self.seen[e][key] = val

    @staticmethod
    def _add(deps, ev):
        if ev is None:
            return
        key, sem, val = ev
        if key not in deps or deps[key][1] < val:
            deps[key] = (sem, val)

    def _deps(self, reads, writes, e=None):
        deps = {}
        for b in reads:
            self._add(deps, b.w)
        for b in writes:
            if b.w is not None and b.w[0] != e:
                self._add(deps, b.w)
            for ev in b.r.values():
                if ev[0] != e:
                    self._add(deps, ev)
        return deps

    def op(self, e, fn, r=(), w=(), inc=True):
        reads = self._bufs(r)
        writes = self._bufs(w)
        xr = [b for b in reads if b.excl]
        if xr:
            reads = [b for b in reads if not b.excl]
            writes = writes + [b for b in xr if b not in writes]
        self._wait(e, self._deps(reads, writes, e if RELAX_SAME else None))
        inst = fn()
        if not inc:
            self.pend[e][0].extend(reads)
            self.pend[e][1].extend(writes)
            return inst
        self.cnt[e] += 1
        inst.then_inc(self.sem[e], 1)
        ev = (e, self.sem[e], self.cnt[e])
        pr, pw = self.pend[e]
        for b in reads + pr:
            b.r[e] = ev
        for b in writes + pw:
            b.w = ev
            b.r = {}
        self.pend[e] = ([], [])
        return inst

    def dma(self, q, out, in_, r=(), w=(), own=None, **kw):
        reads = self._bufs(r)
        writes = self._bufs(w)
        ob = own.b if isinstance(own, T) else own
        self._wait(q, self._deps(reads, writes))
        sem = self.dsem(ob)
        ob.dtot += 16
        self.eng[q].dma_start(out=out, in_=in_, **kw).then_inc(sem, 16)
        ev = ("d:" + ob.name, sem, ob.dtot)
        for b in reads:
            b.r["d:" + ob.name] = ev
        for b in writes:
            b.w = ev
            b.r = {}

    def wait_all(self, e, bufs):
        deps = {}
        for b in self._bufs(bufs):
            self._add(deps, b.w)
            for ev in b.r.values():
                self._add(deps, ev)
        self._wait(e, deps)


def build_nc(dbg=None):
    nc = bass.Bass("TRN2", target_bir_lowering=False)

    def din(name, shape, dt=F32):
        return nc.dram_tensor(name, list(shape), dt, kind="ExternalInput").ap()

    x_d = din("x", [S, D])
    p_d = din("p", [S, 256])
    pos_d = din("pos", [1, S], I32)
    win_d = din("w_in", [128, 8, 2480])
    wout_d = din("w_out", [128, 8, 1024])
    wup_d = din("w_up", [128, 8, 4096])
    wdn_d = din("w_down", [128, 32, 1024])
    wpp_d = din("w_ple_proj", [128, 2, 1024])
    wpg_d = din("w_ple_gate", [128, 8, 1024])
    wqb_d = din("w_q_b", [128, 2, 768])
    wkvb_d = din("w_kv_b", [128, 1024])
    g_mix_d = din("g_mix", [128, D])
    g_mlp_d = din("g_mlp", [128, D])
    g_post_d = din("g_post", [128, D])
    g_gate_d = din("g_gate", [128, D])
    g_fin_d = din("g_fin", [128, D])
    g_gdn_d = din("g_gdn", [64, 512])
    cw_d = din("cw", [64, 24, 4])
    alog_d = din("alog", [64, 8])
    dtb_d = din("dtb", [64, 8])
    gq_d = din("gq", [128, 2])
    gkv_d = din("gkv", [128, 1])
    gmo_d = din("gmo", [128, 4])
    invf_d = din("invf", [96, 1])
    out_d = nc.dram_tensor("out", [S, D], F32, kind="ExternalOutput").ap()
    yT_d = nc.dram_tensor("yT_scr", [8, 128, S], BF16, kind="Internal").ap()
    h1_d = nc.dram_tensor("h1_scr", [S, D], F32, kind="Internal").ap()
    if dbg is not None:
        dbg_y = nc.dram_tensor("dbg_y", [8, 128, S], BF16, kind="ExternalOutput").ap()
        dbg_h1 = nc.dram_tensor("dbg_h1", [S, D], F32, kind="ExternalOutput").ap()
        dbg_h2 = nc.dram_tensor("dbg_h2", [S, D], F32, kind="ExternalOutput").ap()

    es = ExitStack()
    with es:
        k = KB(nc, es)
        E = es.enter_context
        yT_buf = Buf("yT_dram")
        h1_buf = Buf("h1_dram")
        ident_b = k.sb("ident_b", [128, 128], BF16)
        ident_f = k.sb("ident_f", [128, 128], F32)
        ones_f = k.sb("ones_f", [128, 128], F32)
        ones_b = k.sb("ones_b", [128, 128], BF16)
        epsb = k.sb("epsb", [128, 1], F32)
        k.op("pool", lambda: nc.gpsimd.memset(epsb[:], EPS), w=[epsb])
        oneb = k.sb("oneb", [128, 1], F32)
        k.op("pool", lambda: nc.gpsimd.memset(oneb[:], 1.0), w=[oneb])
        esAB = ExitStack()
        ident3 = k.sb("ident3", [64, 8, 64], F32, esAB)
        tri_f = k.sb("tri_f", [64, 64], F32, esAB)
        k.op("pool", lambda: nc.gpsimd.memset(ident_f[:], 0.0), w=[ident_f])
        k.op("pool", lambda: nc.gpsimd.affine_select(out=ident_f[:], in_=ident_f[:], pattern=[[-1, 128]],
                                                     compare_op=ALU.not_equal, fill=1.0, base=0,
                                                     channel_multiplier=1), r=[ident_f], w=[ident_f])
        k.op("pool", lambda: nc.gpsimd.tensor_copy(out=ident_b[:], in_=ident_f[:]), r=[ident_f], w=[ident_b])
        k.op("pool", lambda: nc.gpsimd.memset(ones_f[:], 1.0), w=[ones_f])
        k.op("pool", lambda: nc.gpsimd.memset(ones_b[:], 1.0), w=[ones_b])
        k.op("pool", lambda: nc.gpsimd.tensor_copy(
            out=ident3[:], in_=ident_f[0:64, 0:64].unsqueeze(1).to_broadcast([64, 8, 64])), r=[ident_f], w=[ident3])
        k.op("pool", lambda: nc.gpsimd.affine_select(out=tri_f[:], in_=ones_f[0:64, 0:64], pattern=[[1, 64]],
                                                     compare_op=ALU.is_ge, fill=0.0, base=0,
                                                     channel_multiplier=-1), r=[ones_f], w=[tri_f])

        def load_const(name, src, shape, dt=F32, q="sp"):
            t = k.sb(name, shape, dt, esAB)
            k.dma(q, t[:], src, w=[t], own=t)
            return t

        g_gdn = load_const("g_gdn", g_gdn_d[:, :], [64, 512])
        cw = load_const("cw", cw_d[:, :, :], [64, 24, 4])
        alog = load_const("alog", alog_d[:, :], [64, 8])
        dtb = load_const("dtb", dtb_d[:, :], [64, 8])
        gq = load_const("gq", gq_d[:, :], [128, 2])
        gkv = load_const("gkv", gkv_d[:, :], [128, 1])
        gmo = load_const("gmo", gmo_d[:, :], [128, 4])
        invf = load_const("invf", invf_d[:, :], [96, 1])
        nexpA = k.sb("nexpA", [64, 8], F32, esAB)
        k.op("act", lambda: nc.scalar.activation(out=nexpA[:], in_=alog[:], func=AF.Exp), r=[alog], w=[nexpA])
        k.op("dve", lambda: nc.vector.tensor_scalar(out=nexpA[:], in0=nexpA[:], scalar1=-1.0, scalar2=None,
                                                    op0=ALU.mult), r=[nexpA], w=[nexpA])
        REG_NEG = nc.gpsimd.to_reg(-30000.0)
        REG_ZERO = nc.gpsimd.to_reg(0.0)
        dbg_outs = {}

        def dd(name, t, ap, shape, dt):
            if dbg is None or name in dbg_outs or (isinstance(dbg, (set, list, tuple)) and name not in dbg):
                return
            o = nc.dram_tensor("dd_" + name, list(shape), dt, kind="ExternalOutput").ap()
            dbg_outs[name] = o
            k.dma("sp", o, ap, r=[t], w=[], own=Buf("dd_" + name))

        PS = [k.ps("ps%d" % i, [128, 512], F32) for i in range(6)]
        PB = [k.ps("pb%d" % i, [128, 1024], BF16) for i in range(2)]
        rr = {"f": 0, "b": 0}

        srr = {0: 0, 1: 0}

        def nps(p=None):
            if p is None:
                rr["f"] = (rr["f"] + 1) % 6
                return PS[rr["f"]]
            srr[p] = (srr[p] + 1) % 3
            return PS[3 * p + srr[p]]

        srr3 = {0: 0, 1: 0, 2: 0}

        def nps3(p):
            srr3[p] = (srr3[p] + 1) % 2
            return PS[2 * p + srr3[p]]

        def npb(p=None):
            if p is not None:
                return PB[p]
            rr["b"] = (rr["b"] + 1) % 2
            return PB[rr["b"]]

        def run_interleaved(gens, width, bg=None, bg_every=4, stagger=0):
            active = []
            it = iter(gens)
            rnd = 0
            launched = 0
            exhausted = False
            while True:
                while len(active) < width and not exhausted:
                    if launched < width and stagger and launched * stagger > rnd:
                        break
                    g = next(it, None)
                    if g is None:
                        exhausted = True
                        break
                    active.append(g)
                    launched += 1
                if not active and exhausted:
                    break
                for g in list(active):
                    try:
                        next(g)
                    except StopIteration:
                        active.remove(g)
                rnd += 1
                if bg is not None and rnd % bg_every == 0:
                    try:
                        next(bg)
                    except StopIteration:
                        bg = None
            if bg is not None:
                for _ in bg:
                    pass

        def rstd_from_ssq(ssq, n, dim):
            k.op("act", lambda: nc.scalar.activation(out=ssq[0:n, :], in_=ssq[0:n, :], func=AF.Ln,
                                                     scale=1.0 / dim, bias=epsb[0:n, :]), r=[ssq, epsb], w=[ssq])
            k.op("act", lambda: nc.scalar.activation(out=ssq[0:n, :], in_=ssq[0:n, :], func=AF.Exp, scale=-0.5),
                 r=[ssq], w=[ssq])


        def rms_tm(xin, xin_bufs, gB, out, out_bufs, junk, ssq, dim=D):
            k.op("act", lambda: nc.scalar.activation(out=junk[:, 0:dim], in_=xin, func=AF.Square,
                                                     accum_out=ssq[:, 0:1]), r=xin_bufs, w=[junk, ssq])
            rstd_from_ssq(ssq, 128, dim)
            k.op("dve", lambda: nc.vector.scalar_tensor_tensor(out=out, in0=xin, scalar=ssq[:, 0:1], in1=gB[:, 0:dim],
                                                               op0=ALU.mult, op1=ALU.mult),
                 r=list(xin_bufs) + [ssq, gB], w=out_bufs)

        def to_fm(src_bf, src_bufs, dstT, dst_bufs, col0, nkc, p=None):
            pb = npb(p)
            for kc in range(nkc):
                k.op("pe", lambda kc=kc: nc.tensor.transpose(out=pb[:, kc * 128:(kc + 1) * 128],
                                                            in_=src_bf[:, kc * 128:(kc + 1) * 128],
                                                            identity=ident_b[:]),
                     r=list(src_bufs) + [ident_b], w=[pb], inc=(kc == nkc - 1))
            k.op("act", lambda: nc.scalar.copy(
                out=dstT[:, 0:nkc, col0:col0 + 128],
                in_=pb[:, 0:nkc * 128].rearrange("p (k t) -> p k t", k=nkc)), r=[pb], w=dst_bufs)


        TWO_PI = 6.283185307179586

        def sincos_tile(t0, posi, rv, rki, rkf, rfr, rcs, outf=None):
            R = slice(64, 96)
            k.dma("sp", posi[R, :], pos_d[0:1, t0:t0 + 512].to_broadcast([32, 512]), w=[posi], own=posi)
            k.op("dve", lambda: nc.vector.tensor_copy(out=rv[R, :], in_=posi[R, :]), r=[posi], w=[rv])
            k.op("dve", lambda: nc.vector.tensor_scalar(out=rv[R, :], in0=rv[R, :], scalar1=invf[R, 0:1],
                                                        scalar2=1.0 / TWO_PI, op0=ALU.mult, op1=ALU.mult),
                 r=[rv, invf], w=[rv])
            for which in range(2):
                if which == 0:
                    k.op("dve", lambda: nc.vector.tensor_scalar(out=rfr[R, :], in0=rv[R, :], scalar1=0.25,
                                                                scalar2=None, op0=ALU.add), r=[rv], w=[rfr])
                    src = rfr
                else:
                    src = rv
                k.op("dve", lambda src=src: nc.vector.tensor_copy(out=rki[R, :], in_=src[R, :]), r=[src], w=[rki])
                k.op("dve", lambda: nc.vector.tensor_copy(out=rkf[R, :], in_=rki[R, :]), r=[rki], w=[rkf])
                k.op("dve", lambda src=src: nc.vector.tensor_tensor(out=rfr[R, :], in0=src[R, :], in1=rkf[R, :],
                                                                    op=ALU.subtract), r=[src, rkf], w=[rfr])
                k.op("dve", lambda: nc.vector.tensor_scalar(out=rkf[R, :], in0=rfr[R, :], scalar1=0.5, scalar2=None,
                                                            op0=ALU.is_gt), r=[rfr], w=[rkf])
                k.op("dve", lambda: nc.vector.tensor_tensor(out=rfr[R, :], in0=rfr[R, :], in1=rkf[R, :],
                                                            op=ALU.subtract), r=[rfr, rkf], w=[rfr])
                oap = rcs[R, which, :] if outf is None else outf(which)
                k.op("act", lambda oap=oap: nc.scalar.activation(out=oap, in_=rfr[R, :],
                                                                 func=AF.Sin, scale=TWO_PI),
                     r=[rfr], w=[rcs])

        cq_d = nc.dram_tensor("cq_scr", [2, 128, S], BF16, kind="Internal").ap()
        ckv_d = nc.dram_tensor("ckv_scr", [128, S], BF16, kind="Internal").ap()
        kpe_d = nc.dram_tensor("kpe_scr", [32, S], BF16, kind="Internal").ap()
        lat_buf = Buf("lat_dram")
        esA = ExitStack()
        with esA:
            cdiag = k.sb("cdiag", [64, 24, 4, 64], BF16, esA)
            for c in range(24):
                for kk in range(4):
                    k.op("dve", lambda c=c, kk=kk: nc.vector.tensor_scalar(
                        out=cdiag[:, c, kk, :], in0=ident_f[0:64, 0:64], scalar1=cw[:, c, kk:kk + 1], scalar2=None,
                        op0=ALU.mult), r=[ident_f, cw], w=[cdiag], inc=(c == 23 and kk == 3))

            wi = k.sb("wi", [128, 8, 2480], BF16, esA)
            for kc in range(8):
                for hf in range(2):
                    k.dma("pool", wi[:, kc, hf * 1240:(hf + 1) * 1240], win_d[:, kc, hf * 1240:(hf + 1) * 1240],
                          w=[wi], own=wi.sub("ld"))
            g_mix = k.sb("g_mix", [128, D], F32, esA)
            k.dma("sp", g_mix[:], g_mix_d[:, :], w=[g_mix], own=g_mix)
            xt = [k.sb("xt%d" % i, [128, D], F32, esA) for i in range(2)]
            ssq = [k.sb("ssq%d" % i, [128, 1], F32, esA) for i in range(2)]
            ub = [k.sb("ub%d" % i, [128, D], BF16, esA) for i in range(2)]
            uT2 = [k.sb("uT%d" % i, [128, 8, 512], BF16, esA) for i in range(2)]
            pre2 = [k.sb("pre%d" % i, [64, 515], BF16, esA) for i in range(3)]
            halo = k.sb("halo", [64, 24, 3], BF16, esA)
            silt = [k.sb("silt%d" % i, [64, 512], BF16, esA) for i in range(3)]
            silv = k.sb("silv", [64, 8, 512], BF16, esA)
            sq2 = [k.sb("sq%d" % i, [64, 512], BF16, esA) for i in range(3)]
            rn2 = [k.sb("rn%d" % i, [64, 512], F32, esA) for i in range(3)]
            qk_n = k.sb("qk_n", [64, 16, 512], BF16, esA)
            rv = k.sb("rv", [128, 512], F32, esA)
            rki = k.sb("rki", [96, 512], I32, esA)
            posi = k.sb("posi", [96, 512], I32, esA)
            rkf = k.sb("rkf", [96, 512], F32, esA)
            rfr = k.sb("rfr", [96, 512], F32, esA)
            rcs = k.sb("rcs", [96, 2, 512], F32, esA)
            lsq = k.sb("lsq", [128, 3, 512], BF16, esA)
            lrs = rv
            cq_t = k.sb("cq_t", [128, 2, 512], BF16, esA)
            ckv_t = k.sb("ckv_t", [128, 512], BF16, esA)
            kpe_t = k.sb("kpe_t", [96, 512], BF16, esA)
            wrot = k.sb("wrot", [128, 8, 2, 96], BF16, esA)
            k.op("pool", lambda: nc.gpsimd.memset(wrot[:], 0.0), w=[wrot])
            k.op("act", lambda: nc.scalar.copy(out=wrot[:, :, 0, 64:96], in_=wi[:, :, 2448:2480]), r=[wi], w=[wrot])
            k.op("act", lambda: nc.scalar.mul(out=wrot[:, :, 1, 64:80], in_=wi[:, :, 2464:2480], mul=-1.0),
                 r=[wi], w=[wrot])
            k.op("act", lambda: nc.scalar.copy(out=wrot[:, :, 1, 80:96], in_=wi[:, :, 2448:2464]), r=[wi], w=[wrot])
            Sst = k.sb("Sst", [64, 8, 64], F32, esA)
            Sbf = k.sb("Sbf", [64, 8, 64], BF16, esA)
            k.op("pool", lambda: nc.gpsimd.memset(Sst[:], 0.0), w=[Sst])
            k.op("pool", lambda: nc.gpsimd.memset(Sbf[:], 0.0), w=[Sbf])
            k.op("pool", lambda: nc.gpsimd.memset(halo[:], 0.0), w=[halo])
            NSTR = 3

            def parn(name, shape, dt):
                return [k.sb("%s_%d" % (name, i), shape, dt, esA) for i in range(NSTR)]
            sm2 = parn("sm", [64, 12, 8], F32)
            Rm2 = parn("Rm", [64, 8, 64], F32)
            DT2 = parn("DT", [64, 8, 64], F32)
            DTb2 = parn("DTb", [64, 8, 64], F32)
            Z2 = [parn("Z%d" % i, [64, 8, 64], BF16) for i in range(2)]
            ZT2 = [parn("ZT%d" % i, [64, 8, 64], BF16) for i in range(2)]
            Pm2 = [parn("Pm%d" % i, [64, 8, 64], BF16) for i in range(2)]
            AT2 = parn("AT", [64, 8, 64], BF16)
            kvt2 = parn("kvt", [64, 16, 64], BF16)
            tmpf2 = Rm2
            of2 = DTb2
            zg2 = [TRe(DT2[i], "p h i -> p (h i)") for i in range(NSTR)]
            rbf2 = Z2[0]
            vnew2 = Z2[1]
            kd2 = ZT2[0]
            ybf2 = [TRe(ZT2[1][i], "p h i -> p (h i)") for i in range(NSTR)]
            scan_done = [0]
            smT = k.sb("smT", [64, 7, 8, 8], F32, esA)
            sel63 = k.sb("sel63", [64, 64], F32, esA)
            k.op("pool", lambda: nc.gpsimd.affine_select(out=sel63[:], in_=ones_f[0:64, 0:64], pattern=[[0, 64]],
                                                         compare_op=ALU.is_equal, fill=REG_ZERO, base=-63,
                                                         channel_multiplier=1), r=[ones_f], w=[sel63])
            ygT = k.sb("ygT", [128, 4, 512], BF16, esA)

            def bc(ap2):
                return ap2.unsqueeze(2).to_broadcast([64, 8, 64])

            def a1_stream(Tt):
                for blk in range(4):
                    xb = xt[blk % 2]
                    r0 = Tt * 512 + blk * 128
                    k.dma("sp", xb[:], x_d[r0:r0 + 128, :], w=[xb], own=xb)
                    rms_tm(xb[:], [xb], g_mix, ub[blk % 2][:], [ub[blk % 2]], ub[blk % 2], ssq[blk % 2])
                    yield
                    to_fm(ub[blk % 2], [ub[blk % 2]], uT2[Tt % 2], [uT2[Tt % 2]], blk * 128, 8)
                    yield

            for _ in a1_stream(0):
                pass
            for Tt in range(NT):
                t0 = Tt * 512
                uT = uT2[Tt % 2]
                def ht_stream(c):
                    rn = rn2[c % 3]
                    sq = sq2[c % 3]
                    pp = nps3(c % 3)
                    for kc in range(8):
                        k.op("pe", lambda kc=kc, c=c, pp=pp: nc.tensor.matmul(
                            pp[0:64, :], lhsT=wi[:, kc, c * 64:(c + 1) * 64], rhs=uT[:, kc, :],
                            start=(kc == 0), stop=(kc == 7)), r=[wi.sub("ld"), wi, uT], w=[pp], inc=(kc == 7))
                    yield
                    pr_ = pre2[c % 3]
                    k.op("act", lambda c=c, pr_=pr_: nc.scalar.copy(out=pr_[:, 0:3], in_=halo[:, c, :]),
                         r=[halo], w=[pr_])
                    k.op("act", lambda c=c, pp=pp, pr_=pr_: nc.scalar.copy(out=pr_[:, 3:515], in_=pp[0:64, :]),
                         r=[pp], w=[pr_])
                    k.op("act", lambda c=c, pr_=pr_: nc.scalar.copy(out=halo[:, c, :], in_=pr_[:, 512:515]),
                         r=[pr_], w=[halo])
                    pc = nps3(c % 3)
                    for kk in range(4):
                        k.op("pe", lambda kk=kk, c=c, pc=pc, pr_=pr_: nc.tensor.matmul(
                            pc[0:64, :], lhsT=cdiag[:, c, kk, :], rhs=pr_[:, kk:kk + 512],
                            start=(kk == 0), stop=(kk == 3)), r=[cdiag, pr_], w=[pc], inc=(kk == 3))
                    yield
                    k.op("act", lambda pc=pc: nc.scalar.activation(out=rn[:], in_=pc[0:64, :], func=AF.Exp,
                                                                   scale=-1.0), r=[pc], w=[rn])
                    k.op("act", lambda: nc.scalar.activation(out=rn[:], in_=rn[:], func=AF.Ln, bias=oneb[0:64, :]),
                         r=[rn, oneb], w=[rn])
                    k.op("act", lambda: nc.scalar.activation(out=rn[:], in_=rn[:], func=AF.Exp, scale=-1.0),
                         r=[rn], w=[rn])
                    yield
                    so_ = silt[c % 3] if c < 16 else silv
                    so_ap = silt[c % 3][:, :] if c < 16 else silv[:, c - 16, :]
                    k.op("dve", lambda pc=pc, so_ap=so_ap: nc.vector.tensor_tensor(out=so_ap, in0=pc[0:64, :],
                                                                                   in1=rn[:], op=ALU.mult),
                         r=[pc, rn], w=[so_])
                    yield
                    if c < 16:
                        k.op("dve", lambda so_ap=so_ap: nc.vector.tensor_tensor(out=sq[:], in0=so_ap,
                                                                                in1=so_ap, op=ALU.mult),
                             r=[so_], w=[sq])
                        yield
                        pn = nps3(c % 3)
                        k.op("pe", lambda pn=pn: nc.tensor.matmul(pn[0:64, :], lhsT=ones_b[0:64, 0:64], rhs=sq[:],
                                                                  start=True, stop=True), r=[ones_b, sq], w=[pn])
                        k.op("act", lambda pn=pn: nc.scalar.activation(out=rn[:], in_=pn[0:64, :], func=AF.Ln,
                                                                       bias=epsb[0:64, :]), r=[pn, epsb], w=[rn])
                        k.op("act", lambda: nc.scalar.activation(out=rn[:], in_=rn[:], func=AF.Exp, scale=-0.5),
                             r=[rn], w=[rn])
                        yield
                        sc = 0.125 if c < 8 else 1.0
                        k.op("dve", lambda c=c, sc=sc, so_ap=so_ap: nc.vector.scalar_tensor_tensor(
                            out=qk_n[:, c, :], in0=so_ap, scalar=sc, in1=rn[:], op0=ALU.mult, op1=ALU.mult),
                            r=[so_, rn], w=[qk_n])
                run_interleaved((ht_stream(c) for c in range(24)), 3, stagger=2)
                lat_cols = [(2064, 128), (2192, 128), (2320, 128)]
                plat = []
                for m, (c0, wd) in enumerate(lat_cols):
                    pp = nps()
                    plat.append(pp)
                    for kc in range(8):
                        k.op("pe", lambda kc=kc, c0=c0, pp=pp: nc.tensor.matmul(
                            pp[:, :], lhsT=wi[:, kc, c0:c0 + 128], rhs=uT[:, kc, :],
                            start=(kc == 0), stop=(kc == 7)), r=[wi, uT], w=[pp], inc=(kc == 7))
                    k.op("act", lambda m=m, pp=pp: nc.scalar.activation(out=lsq[:, m, :], in_=pp[:, :],
                                                                        func=AF.Square), r=[pp], w=[lsq])
                pn = nps()
                for m in range(2):
                    k.op("pe", lambda m=m, pn=pn: nc.tensor.matmul(pn[:, :], lhsT=ones_b[:, :], rhs=lsq[:, m, :],
                                                                   start=(m == 0), stop=(m == 1)),
                         r=[ones_b, lsq], w=[pn], inc=(m == 1))
                k.op("act", lambda pn=pn: nc.scalar.activation(out=lrs[:], in_=pn[:, :], func=AF.Ln, scale=1.0 / 256,
                                                               bias=epsb[:, :]), r=[pn, epsb], w=[lrs])
                k.op("act", lambda: nc.scalar.activation(out=lrs[:], in_=lrs[:], func=AF.Exp, scale=-0.5),
                     r=[lrs], w=[lrs])
                for m in range(2):
                    k.op("dve", lambda m=m: nc.vector.scalar_tensor_tensor(
                        out=cq_t[:, m, :], in0=plat[m][:, :], scalar=gq[:, m:m + 1], in1=lrs[:],
                        op0=ALU.mult, op1=ALU.mult), r=[plat[m], gq, lrs], w=[cq_t])
                pn = nps()
                k.op("pe", lambda pn=pn: nc.tensor.matmul(pn[:, :], lhsT=ones_b[:, :], rhs=lsq[:, 2, :],
                                                          start=True, stop=True), r=[ones_b, lsq], w=[pn])
                k.op("act", lambda pn=pn: nc.scalar.activation(out=lrs[:], in_=pn[:, :], func=AF.Ln, scale=1.0 / 128,
                                                               bias=epsb[:, :]), r=[pn, epsb], w=[lrs])
                k.op("act", lambda: nc.scalar.activation(out=lrs[:], in_=lrs[:], func=AF.Exp, scale=-0.5),
                     r=[lrs], w=[lrs])
                k.op("dve", lambda: nc.vector.scalar_tensor_tensor(
                    out=ckv_t[:], in0=plat[2][:, :], scalar=gkv[:, 0:1], in1=lrs[:],
                    op0=ALU.mult, op1=ALU.mult), r=[plat[2], gkv, lrs], w=[ckv_t])
                sincos_tile(t0, posi, rv, rki, rkf, rfr, rcs)
                pks = []
                for m in range(2):
                    pp = nps()
                    pks.append(pp)
                    for kc in range(8):
                        lw = wrot[:, kc, m, :]
                        k.op("pe", lambda kc=kc, pp=pp, lw=lw: nc.tensor.matmul(
                            pp[0:96, :], lhsT=lw, rhs=uT[:, kc, :], start=(kc == 0), stop=(kc == 7)),
                            r=[wi, wrot, uT], w=[pp], inc=(kc == 7))
                k.op("dve", lambda: nc.vector.tensor_tensor(out=rkf[64:96, :], in0=pks[0][64:96, :],
                                                            in1=rcs[64:96, 0, :], op=ALU.mult),
                     r=[pks[0], rcs], w=[rkf])
                k.op("dve", lambda: nc.vector.tensor_tensor(out=rfr[64:96, :], in0=pks[1][64:96, :],
                                                            in1=rcs[64:96, 1, :], op=ALU.mult),
                     r=[pks[1], rcs], w=[rfr])
                k.op("dve", lambda: nc.vector.tensor_tensor(out=kpe_t[64:96, :], in0=rkf[64:96, :],
                                                            in1=rfr[64:96, :], op=ALU.add),
                     r=[rkf, rfr], w=[kpe_t])
                for m in range(2):
                    k.dma("act", cq_d[m, :, t0:t0 + 512], cq_t[:, m, :], r=[cq_t], w=[lat_buf], own=cq_t.sub("st"))
                k.dma("act", ckv_d[:, t0:t0 + 512], ckv_t[:], r=[ckv_t], w=[lat_buf], own=ckv_t.sub("st"))
                k.dma("act", kpe_d[:, t0:t0 + 512], kpe_t[64:96, :], r=[kpe_t], w=[lat_buf], own=kpe_t.sub("st"))
                pa = nps()
                for n in range(8):
                    for kc in range(8):
                        k.op("pe", lambda kc=kc, n=n, pa=pa: nc.tensor.matmul(
                            pa[0:64, n * 16:(n + 1) * 16], lhsT=uT[:, kc, n * 64:(n + 1) * 64],
                            rhs=wi[:, kc, 2048:2064], start=(kc == 0), stop=(kc == 7)),
                            r=[wi, uT], w=[pa], inc=(kc == 7 and n == 7))
                pa3 = pa[0:64, 0:128].rearrange("p (n c) -> p n c", c=16)
                bT, gT, eT, edT, sdT, t1T, glT = [smT[:, j, :, :] for j in range(7)]
                k.op("act", lambda: nc.scalar.activation(out=bT, in_=pa3[:, :, 0:8], func=AF.Exp, scale=-1.0),
                     r=[pa], w=[smT])
                k.op("dve", lambda: nc.vector.tensor_tensor(
                    out=t1T, in0=pa3[:, :, 8:16], in1=dtb[:].unsqueeze(1).to_broadcast([64, 8, 8]), op=ALU.add),
                    r=[pa, dtb], w=[smT])
                k.op("dve", lambda: nc.vector.tensor_scalar(out=bT, in0=bT, scalar1=1.0, scalar2=None, op0=ALU.add),
                     r=[smT], w=[smT])
                k.op("dve", lambda: nc.vector.reciprocal(out=bT, in_=bT), r=[smT], w=[smT])
                k.op("act", lambda: nc.scalar.activation(out=t1T, in_=t1T, func=AF.Exp), r=[smT], w=[smT])
                k.op("act", lambda: nc.scalar.activation(out=t1T, in_=t1T, func=AF.Ln, bias=oneb[0:64, :]),
                     r=[smT, oneb], w=[smT])
                k.op("dve", lambda: nc.vector.tensor_tensor(
                    out=t1T, in0=t1T, in1=nexpA[:].unsqueeze(1).to_broadcast([64, 8, 8]), op=ALU.mult),
                    r=[smT, nexpA], w=[smT])
                pg = nps()
                k.op("pe", lambda: nc.tensor.matmul(pg[0:64, 0:64], lhsT=tri_f[:, :],
                                                    rhs=smT[:, 5, :, :].rearrange("p n h -> p (n h)"),
                                                    start=True, stop=True), r=[tri_f, smT], w=[pg])
                k.op("dve", lambda: nc.vector.tensor_copy(
                    out=gT, in_=pg[0:64, 0:64].rearrange("p (n h) -> p n h", h=8)), r=[pg], w=[smT])
                k.op("act", lambda: nc.scalar.activation(out=eT, in_=gT, func=AF.Exp), r=[smT], w=[smT])
                pl = nps()
                k.op("pe", lambda: nc.tensor.matmul(pl[0:64, 0:64], lhsT=sel63[:, :],
                                                    rhs=smT[:, 1, :, :].rearrange("p n h -> p (n h)"),
                                                    start=True, stop=True), r=[sel63, smT], w=[pl])
                k.op("dve", lambda: nc.vector.tensor_copy(
                    out=glT, in_=pl[0:64, 0:64].rearrange("p (n h) -> p n h", h=8)), r=[pl], w=[smT])
                k.op("act", lambda: nc.scalar.activation(out=sdT, in_=glT, func=AF.Exp), r=[smT], w=[smT])
                k.op("dve", lambda: nc.vector.tensor_tensor(out=edT, in0=glT, in1=gT, op=ALU.subtract),
                     r=[smT], w=[smT])
                k.op("act", lambda: nc.scalar.activation(out=edT, in_=edT, func=AF.Exp), r=[smT], w=[smT])
                def chunk_stream(n):
                    par = n % NSTR
                    gidx = Tt * 8 + n
                    sm = sm2[par]; Rm = Rm2[par]; Dm = Rm; DT = DT2[par]; DTb = DTb2[par]; Xf = DTb
                    Z = [Z2[0][par], Z2[1][par]]; ZT = [ZT2[0][par], ZT2[1][par]]; Pm = [Pm2[0][par], Pm2[1][par]]
                    AT = AT2[par]; kvt = kvt2[par]; kd = kd2[par]; tmpf = tmpf2[par]; rbf = rbf2[par]
                    vnew = vnew2[par]; of = of2[par]; zg = zg2[par]; ybf = ybf2[par]
                    cs = slice(n * 64, n * 64 + 64)
                    bcol = smT[:, 0, n, :]
                    gcol = smT[:, 1, n, :]
                    ecol = smT[:, 2, n, :]
                    edc = smT[:, 3, n, :]
                    sdc = smT[:, 4, n, :]
                    SB = SG = SE = SED = SSD = smT
                    yield
                    k.op("pool", lambda: nc.gpsimd.tensor_tensor(out=Rm[:], in0=ident3[:], in1=bc(gcol), op=ALU.mult),
                         r=[ident3, smT], w=[Rm])
                    pG = nps3(par)
                    yield
                    k.op("pe", lambda pG=pG: nc.tensor.matmul(pG[0:64, :], lhsT=ones_f[0:64, 0:64],
                                                              rhs=Rm[:].rearrange("p h i -> p (h i)"),
                                                              start=True, stop=True), r=[ones_f, Rm], w=[pG])
                    pG3 = pG[0:64, :].rearrange("p (h i) -> p h i", h=8)
                    yield
                    k.op("dve", lambda pG3=pG3: nc.vector.tensor_tensor(out=Dm[:], in0=pG3, in1=bc(gcol),
                                                                        op=ALU.subtract),
                         r=[pG, smT], w=[Dm])
                    yield
                    k.op("pool", lambda: nc.gpsimd.affine_select(
                        out=Dm[:], in_=Dm[:], pattern=[[0, 8], [1, 64]], compare_op=ALU.is_ge, fill=REG_NEG,
                        base=0, channel_multiplier=-1), r=[Dm], w=[Dm])
                    yield
                    k.op("act", lambda: nc.scalar.activation(out=DT[:], in_=Dm[:], func=AF.Exp), r=[Dm], w=[DT])
                    yield
                    k.op("pool", lambda: nc.gpsimd.tensor_tensor(out=DTb[:], in0=DT[:], in1=bc(bcol), op=ALU.mult),
                         r=[DT, smT], w=[DTb])
                    pK = nps3(par)
                    pQ = nps3(par)
                    yield
                    for h in range(8):
                        k.op("pe", lambda h=h, pK=pK: nc.tensor.matmul(
                            pK[0:64, h * 64:(h + 1) * 64], lhsT=qk_n[:, 8 + h, cs], rhs=qk_n[:, 8 + h, cs],
                            start=True, stop=True), r=[qk_n], w=[pK], inc=(h == 7))
                    yield
                    for h in range(8):
                        k.op("pe", lambda h=h, pQ=pQ: nc.tensor.matmul(
                            pQ[0:64, h * 64:(h + 1) * 64], lhsT=qk_n[:, 8 + h, cs], rhs=qk_n[:, h, cs],
                            start=True, stop=True), r=[qk_n], w=[pQ], inc=(h == 7))
                    yield
                    k.op("dve", lambda pK=pK: nc.vector.tensor_tensor(
                        out=Xf[:], in0=pK[0:64, :].rearrange("p (h i) -> p h i", h=8), in1=DTb[:], op=ALU.mult),
                        r=[pK, DTb], w=[Xf])
                    yield
                    k.op("pool", lambda: nc.gpsimd.affine_select(
                        out=Z[0][:], in_=Xf[:], pattern=[[0, 8], [1, 64]], compare_op=ALU.is_gt, fill=REG_ZERO,
                        base=0, channel_multiplier=-1), r=[Xf], w=[Z[0]])
                    yield
                    k.op("dve", lambda pQ=pQ: nc.vector.tensor_tensor(
                        out=AT[:], in0=pQ[0:64, :].rearrange("p (h i) -> p h i", h=8), in1=DT[:], op=ALU.mult),
                        r=[pQ, DT], w=[AT])
                    yield
                    pb = npb()
                    for h in range(8):
                        k.op("pe", lambda h=h, pb=pb: nc.tensor.transpose(
                            out=pb[0:64, h * 64:(h + 1) * 64], in_=qk_n[:, 8 + h, cs], identity=ident_b[0:64, 0:64]),
                            r=[qk_n, ident_b], w=[pb], inc=False)
                    for h in range(8):
                        k.op("pe", lambda h=h, pb=pb: nc.tensor.transpose(
                            out=pb[0:64, (8 + h) * 64:(9 + h) * 64], in_=silv[:, h, cs],
                            identity=ident_b[0:64, 0:64]), r=[silv, ident_b], w=[pb], inc=(h == 7))
                    k.op("act", lambda pb=pb: nc.scalar.copy(
                        out=kvt[:], in_=pb[0:64, :].rearrange("p (c d) -> p c d", c=16)), r=[pb], w=[kvt])
                    yield
                    pb = npb()
                    for h in range(8):
                        k.op("pe", lambda h=h, pb=pb: nc.tensor.transpose(
                            out=pb[0:64, h * 64:(h + 1) * 64], in_=Z[0][:, h, :], identity=ident_b[0:64, 0:64]),
                            r=[Z[0], ident_b], w=[pb], inc=(h == 7))
                    k.op("act", lambda pb=pb: nc.scalar.copy(
                        out=ZT[0][:], in_=pb[0:64, 0:512].rearrange("p (h i) -> p h i", h=8)), r=[pb], w=[ZT[0]])
                    yield
                    k.op("pool", lambda: nc.gpsimd.tensor_tensor(out=Pm[0][:], in0=ident3[:], in1=Z[0][:],
                                                                 op=ALU.subtract), r=[ident3, Z[0]], w=[Pm[0]])
                    cur = 0
                    yield
                    for lev in range(1, 6):
                        nxt = 1 - cur
                        yield
                        pzt = nps3(par)
                        for h in range(8):
                            k.op("pe", lambda h=h, pzt=pzt, cur=cur: nc.tensor.matmul(
                                pzt[0:64, h * 64:(h + 1) * 64], lhsT=Z[cur][:, h, :], rhs=ZT[cur][:, h, :],
                                start=True, stop=True), r=[Z[cur], ZT[cur]], w=[pzt], inc=(h == 7))
                        if lev < 5:
                            pz = nps3(par)
                            for h in range(8):
                                k.op("pe", lambda h=h, pz=pz, cur=cur: nc.tensor.matmul(
                                    pz[0:64, h * 64:(h + 1) * 64], lhsT=ZT[cur][:, h, :], rhs=Z[cur][:, h, :],
                                    start=True, stop=True), r=[Z[cur], ZT[cur]], w=[pz], inc=(h == 7))
                        yield
                        k.op("act", lambda pzt=pzt, nxt=nxt: nc.scalar.copy(
                            out=ZT[nxt][:], in_=pzt[0:64, :].rearrange("p (h i) -> p h i", h=8)),
                            r=[pzt], w=[ZT[nxt]])
                        if lev < 5:
                            k.op("dve", lambda pz=pz, nxt=nxt: nc.vector.tensor_copy(
                                out=Z[nxt][:], in_=pz[0:64, :].rearrange("p (h i) -> p h i", h=8)),
                                r=[pz], w=[Z[nxt]])
                        yield
                        pp = nps3(par)
                        for h in range(8):
                            k.op("pe", lambda h=h, pp=pp, nxt=nxt, cur=cur: nc.tensor.matmul(
                                pp[0:64, h * 64:(h + 1) * 64], lhsT=ZT[nxt][:, h, :], rhs=Pm[cur][:, h, :],
                                start=True, stop=False), r=[ZT[nxt], Pm[cur]], w=[pp], inc=False)
                            k.op("pe", lambda h=h, pp=pp, nxt=nxt, cur=cur: nc.tensor.matmul(
                                pp[0:64, h * 64:(h + 1) * 64], lhsT=ident_b[0:64, 0:64], rhs=Pm[cur][:, h, :],
                                start=False, stop=True), r=[ident_b, Pm[cur]], w=[pp], inc=(h == 7))
                        yield
                        k.op("act", lambda pp=pp, nxt=nxt, cur=cur: nc.scalar.copy(
                            out=Pm[nxt][:], in_=pp[0:64, :].rearrange("p (h i) -> p h i", h=8)),
                            r=[pp], w=[Pm[nxt]])
                        cur = nxt
                    G = Pm[cur]
                    yield
                    k.op("pool", lambda: nc.gpsimd.tensor_tensor(out=kd[:], in0=kvt[:, 0:8, :], in1=bc(edc),
                                                                 op=ALU.mult), r=[kvt, smT], w=[kd])
                    while scan_done[0] < gidx:
                        yield
                    pS = nps3(par)
                    for h in range(8):
                        k.op("pe", lambda h=h, pS=pS: nc.tensor.matmul(
                            pS[0:64, h * 64:(h + 1) * 64], lhsT=qk_n[:, 8 + h, cs], rhs=Sbf[:, h, :],
                            start=True, stop=True), r=[qk_n, Sbf], w=[pS], inc=(h == 7))
                    pO1 = nps3(par)
                    for h in range(8):
                        k.op("pe", lambda h=h, pO1=pO1: nc.tensor.matmul(
                            pO1[0:64, h * 64:(h + 1) * 64], lhsT=qk_n[:, h, cs], rhs=Sbf[:, h, :],
                            start=True, stop=True), r=[qk_n, Sbf], w=[pO1], inc=(h == 7))
                    k.op("dve", lambda pS=pS: nc.vector.tensor_tensor(
                        out=tmpf[:], in0=pS[0:64, :].rearrange("p (h i) -> p h i", h=8), in1=bc(ecol), op=ALU.mult),
                        r=[pS, smT], w=[tmpf])
                    k.op("dve", lambda: nc.vector.tensor_tensor(out=rbf[:], in0=kvt[:, 8:16, :], in1=tmpf[:],
                                                                op=ALU.subtract), r=[kvt, tmpf], w=[rbf])
                    yield
                    pT_ = nps3(par)
                    for h in range(8):
                        k.op("pe", lambda h=h, pT_=pT_: nc.tensor.matmul(
                            pT_[0:64, h * 64:(h + 1) * 64], lhsT=G[:, h, :], rhs=rbf[:, h, :],
                            start=True, stop=True), r=[G, rbf], w=[pT_], inc=(h == 7))
                    k.op("dve", lambda pT_=pT_: nc.vector.tensor_tensor(
                        out=vnew[:], in0=pT_[0:64, :].rearrange("p (h i) -> p h i", h=8), in1=bc(bcol), op=ALU.mult),
                        r=[pT_, smT], w=[vnew])
                    k.op("dve", lambda pO1=pO1: nc.vector.tensor_tensor(
                        out=of[:], in0=pO1[0:64, :].rearrange("p (h i) -> p h i", h=8), in1=bc(ecol), op=ALU.mult),
                        r=[pO1, smT], w=[of])
                    k.op("dve", lambda: nc.vector.tensor_tensor(out=Sst[:], in0=Sst[:], in1=bc(sdc), op=ALU.mult),
                         r=[Sst, smT], w=[Sst])
                    yield
                    pU = nps3(par)
                    for h in range(8):
                        k.op("pe", lambda h=h, pU=pU: nc.tensor.matmul(
                            pU[0:64, h * 64:(h + 1) * 64], lhsT=kd[:, h, :], rhs=vnew[:, h, :],
                            start=True, stop=True), r=[kd, vnew], w=[pU], inc=(h == 7))
                    k.op("dve", lambda pU=pU: nc.vector.tensor_tensor(
                        out=Sbf[:], in0=pU[0:64, :].rearrange("p (h i) -> p h i", h=8), in1=Sst[:], op=ALU.add),
                        r=[pU, Sst], w=[Sbf])
                    scan_done[0] = gidx + 1
                    k.op("dve", lambda pU=pU: nc.vector.tensor_tensor(
                        out=Sst[:], in0=pU[0:64, :].rearrange("p (h i) -> p h i", h=8), in1=Sst[:], op=ALU.add),
                        r=[pU, Sst], w=[Sst])
                    yield
                    pO2 = nps3(par)
                    for h in range(8):
                        k.op("pe", lambda h=h, pO2=pO2: nc.tensor.matmul(
                            pO2[0:64, h * 64:(h + 1) * 64], lhsT=AT[:, h, :], rhs=vnew[:, h, :],
                            start=True, stop=True), r=[AT, vnew], w=[pO2], inc=(h == 7))
                    yield
                    k.op("dve", lambda pO2=pO2: nc.vector.tensor_tensor(
                        out=of[:], in0=pO2[0:64, :].rearrange("p (h i) -> p h i", h=8), in1=of[:], op=ALU.add),
                        r=[pO2, of], w=[of])
                    yield
                    k.op("pool", lambda: nc.gpsimd.tensor_tensor(out=tmpf[:], in0=of[:], in1=of[:], op=ALU.mult),
                         r=[of], w=[tmpf])
                    osq = sm[:, 7, :]
                    yield
                    k.op("dve", lambda: nc.vector.tensor_reduce(out=osq, in_=tmpf[:], axis=AX.X, op=ALU.add),
                         r=[tmpf], w=[sm.sub("osq")])
                    yield
                    k.op("act", lambda: nc.scalar.activation(out=osq, in_=osq, func=AF.Ln, scale=1.0 / 64,
                                                             bias=epsb[0:64, :]),
                         r=[sm.sub("osq"), epsb], w=[sm.sub("osq")])
                    yield
                    k.op("act", lambda: nc.scalar.activation(out=osq, in_=osq, func=AF.Exp, scale=-0.5),
                         r=[sm.sub("osq")], w=[sm.sub("osq")])
                    yield
                    k.op("dve", lambda: nc.vector.tensor_tensor(out=of[:], in0=of[:], in1=bc(osq), op=ALU.mult),
                         r=[of, sm.sub("osq")], w=[of])
                    pz_ = nps3(par)
                    yield
                    for kc in range(8):
                        k.op("pe", lambda kc=kc, pz_=pz_: nc.tensor.matmul(
                            pz_[0:64, :], lhsT=uT[:, kc, cs], rhs=wi[:, kc, 1536:2048],
                            start=(kc == 0), stop=(kc == 7)), r=[wi, uT], w=[pz_], inc=(kc == 7))
                    yield
                    k.op("act", lambda pz_=pz_: nc.scalar.activation(out=zg[:], in_=pz_[0:64, :], func=AF.Exp,
                                                                     scale=-1.0), r=[pz_], w=[zg])
                    yield
                    k.op("act", lambda: nc.scalar.activation(out=zg[:], in_=zg[:], func=AF.Ln, bias=oneb[0:64, :]),
                         r=[zg, oneb], w=[zg])
                    yield
                    k.op("act", lambda: nc.scalar.activation(out=zg[:], in_=zg[:], func=AF.Exp, scale=-1.0),
                         r=[zg], w=[zg])
                    yield
                    k.op("dve", lambda pz_=pz_: nc.vector.tensor_tensor(out=zg[:], in0=pz_[0:64, :], in1=zg[:],
                                                                        op=ALU.mult), r=[pz_, zg], w=[zg])
                    yield
                    k.op("pool", lambda: nc.gpsimd.tensor_tensor(out=zg[:], in0=zg[:], in1=g_gdn[:], op=ALU.mult),
                         r=[zg, g_gdn], w=[zg])
                    yield
                    k.op("dve", lambda: nc.vector.tensor_tensor(
                        out=ybf[:], in0=of[:].rearrange("p h i -> p (h i)"), in1=zg[:], op=ALU.mult),
                        r=[of, zg], w=[ybf])
                    yield
                    pb = npb()
                    for m in range(4):
                        k.op("pe", lambda m=m, pb=pb: nc.tensor.transpose(
                            out=pb[:, m * 64:(m + 1) * 64], in_=ybf[:, m * 128:(m + 1) * 128],
                            identity=ident_b[0:64, 0:64]), r=[ybf, ident_b], w=[pb], inc=(m == 3))
                    k.op("act", lambda pb=pb: nc.scalar.copy(
                        out=ygT[:, :, cs], in_=pb[:, 0:256].rearrange("p (m t) -> p m t", m=4)), r=[pb], w=[ygT])
                run_interleaved((chunk_stream(n) for n in range(8)), NSTR,
                                bg=(a1_stream(Tt + 1) if Tt + 1 < NT else None), stagger=12)
                for m in range(4):
                    k.dma("act", yT_d[m, :, t0:t0 + 512], ygT[:, m, :], r=[ygT], w=[yT_buf], own=ygT.sub("st"))
            k.wait_all("sp", [ygT, cq_t, ckv_t, kpe_t])

        h2_d = nc.dram_tensor("h2_scr", [S, D], F32, kind="Internal").ap()
        h2_buf = Buf("h2_dram")
        SCALE = 96.0 ** -0.5
        k.barrier()
        esB = ExitStack()
        with esB:
            cqT = k.sb("cqT", [128, 2, S], BF16, esB)
            ckvT = k.sb("ckvT", [128, S], BF16, esB)
            kper = k.sb("kper", [96, S], BF16, esB)
            for m in range(2):
                k.dma("sp", cqT[:, m, :], cq_d[m, :, :], r=[lat_buf], w=[cqT], own=cqT)
            k.dma("sp", ckvT[:], ckv_d[:, :], r=[lat_buf], w=[ckvT], own=ckvT)
            k.dma("sp", kper[64:96, :], kpe_d[:, :], r=[lat_buf], w=[kper], own=kper)
            wqb = k.sb("wqb", [128, 2, 768], BF16, esB)
            k.dma("pool", wqb[:], wqb_d[:, :, :], w=[wqb], own=wqb)
            wkvb = k.sb("wkvb", [128, 1024], BF16, esB)
            k.dma("pool", wkvb[:], wkvb_d[:, :], w=[wkvb], own=wkvb)
            wqrot = k.sb("wqrot", [128, 2, 8, 96], BF16, esB)
            wq4 = wqb[:].rearrange("p m (h c) -> p m h c", c=96)
            k.op("pool", lambda: nc.gpsimd.memset(wqrot[:], 0.0), w=[wqrot])
            k.op("act", lambda: nc.scalar.mul(out=wqrot[:, :, :, 64:80], in_=wq4[:, :, :, 80:96], mul=-1.0),
                 r=[wqb], w=[wqrot])
            k.op("act", lambda: nc.scalar.copy(out=wqrot[:, :, :, 80:96], in_=wq4[:, :, :, 64:80]), r=[wqb], w=[wqrot])
            cst = k.sb("cst", [96, 2, S], F32, esB)
            Vt = k.sb("Vt", [128, 32, 8, 65], BF16, esB)
            k.op("pool", lambda: nc.gpsimd.memset(Vt[:, :, :, 64:65], 1.0), w=[Vt])
            wkv3 = wkvb[:].rearrange("p (h c) -> p h c", c=128)
            for kb in range(32):
                pv = nps()
                k.op("pe", lambda kb=kb, pv=pv: nc.tensor.matmul(
                    pv[:, :], lhsT=ckvT[:, kb * 128:(kb + 1) * 128], rhs=wkv3[:, :, 64:128], start=True, stop=True),
                    r=[ckvT, wkvb], w=[pv])
                k.op("act", lambda kb=kb, pv=pv: nc.scalar.copy(
                    out=Vt[:, kb, :, 0:64], in_=pv[:, :].rearrange("p (h c) -> p h c", c=64)), r=[pv], w=[Vt])
            esB0 = ExitStack()
            b_rv = k.sb("b_rv", [96, 512], F32, esB0)
            b_rki = k.sb("b_rki", [96, 512], I32, esB0)
            b_rkf = k.sb("b_rkf", [96, 512], F32, esB0)
            b_rfr = k.sb("b_rfr", [96, 512], F32, esB0)
            b_posi = k.sb("b_posi", [96, 512], I32, esB0)
            for Tt in range(NT):
                sincos_tile(Tt * 512, b_posi, b_rv, b_rki, b_rkf, b_rfr, cst,
                            outf=lambda which, Tt=Tt: cst[64:96, which, Tt * 512:(Tt + 1) * 512])
            k.barrier()
            esB0.close()
            Qh2 = [k.sb("Qh%d" % i, [96, S], BF16, esB) for i in range(2)]
            Kh2 = [k.sb("Kh%d" % i, [96, S], BF16, esB) for i in range(2)]
            for i in range(2):
                k.op("dve", lambda i=i: nc.vector.tensor_copy(out=Kh2[i][64:96, :], in_=kper[64:96, :]),
                     r=[kper], w=[Kh2[i]])
            PT = [k.sb("PT%d" % i, [128, 512], BF16, esB) for i in range(5)]
            osb2 = [k.sb("osb%d" % i, [65, 512], F32, esB) for i in range(2)]
            rec = k.sb("rec", [64, 512], F32, esB)
            yh = k.sb("yh", [64, 512], F32, esB)
            yaT = k.sb("yaT", [128, 4, S], BF16, esB)
            sel = k.sb("sel", [65, 64], F32, esB)
            k.op("pool", lambda: nc.gpsimd.memset(sel[:], 0.0), w=[sel])
            k.op("pool", lambda: nc.gpsimd.memset(sel[64:65, :], 1.0), w=[sel])
            qt1 = k.sb("qt1", [96, 512], F32, esB)
            qt2 = k.sb("qt2", [96, 512], F32, esB)

            def prep_tile(h, Tt):
                Qh, Kh = Qh2[h % 2], Kh2[h % 2]
                ts_ = slice(Tt * 512, (Tt + 1) * 512)
                pk = nps_p()
                k.op("pe", lambda: nc.tensor.matmul(
                    pk[0:64, :], lhsT=wkvb[:, h * 128:h * 128 + 64], rhs=ckvT[:, ts_], start=True, stop=True),
                    r=[wkvb, ckvT], w=[pk])
                k.op("dve", lambda: nc.vector.tensor_copy(out=Kh[0:64, ts_], in_=pk[0:64, :]), r=[pk], w=[Kh])
                pq = nps_p()
                for m in range(2):
                    k.op("pe", lambda m=m: nc.tensor.matmul(
                        pq[0:96, :], lhsT=wqb[:, m, h * 96:(h + 1) * 96], rhs=cqT[:, m, ts_],
                        start=(m == 0), stop=(m == 1)), r=[wqb, cqT], w=[pq], inc=(m == 1))
                k.op("dve", lambda: nc.vector.tensor_copy(out=Qh[0:64, ts_], in_=pq[0:64, :]), r=[pq], w=[Qh])
                k.op("dve", lambda: nc.vector.tensor_tensor(
                    out=qt1[64:96, :], in0=pq[64:96, :], in1=cst[64:96, 0, ts_], op=ALU.mult),
                    r=[pq, cst], w=[qt1])
                pr2 = nps_p()
                for m in range(2):
                    k.op("pe", lambda m=m: nc.tensor.matmul(
                        pr2[0:96, :], lhsT=wqrot[:, m, h, :], rhs=cqT[:, m, ts_],
                        start=(m == 0), stop=(m == 1)), r=[wqrot, cqT], w=[pr2], inc=(m == 1))
                k.op("dve", lambda: nc.vector.tensor_tensor(
                    out=qt2[64:96, :], in0=pr2[64:96, :], in1=cst[64:96, 1, ts_], op=ALU.mult),
                    r=[pr2, cst], w=[qt2])
                k.op("dve", lambda: nc.vector.tensor_tensor(
                    out=Qh[64:96, ts_], in0=qt1[64:96, :], in1=qt2[64:96, :], op=ALU.add),
                    r=[qt1, qt2], w=[Qh])

            sc_rr = [0]
            pr_rr = [0]
            PREPB = [TView(PB[0], F32), TView(PB[1], F32)]

            def nps_b():
                sc_rr[0] = (sc_rr[0] + 1) % 4
                return PS[sc_rr[0]]

            def nps_p():
                pr_rr[0] = (pr_rr[0] + 1) % 2
                return PREPB[pr_rr[0]]

            for Tt in range(NT):
                prep_tile(0, Tt)
            ptr = [0]
            for h in range(8):
                Qh, Kh = Qh2[h % 2], Kh2[h % 2]
                steps = []
                for Qt in range(NT):
                    for kb in range(4 * Qt + 4):
                        steps.append((Qt, kb))
                nst = len(steps)
                info = {}

                def emit_qk(i):
                    Qt, kb = steps[i]
                    d = kb - 4 * Qt
                    c0 = 128 * d if d > 0 else 0
                    qs = slice(Qt * 512 + c0, (Qt + 1) * 512)
                    cs_ = slice(c0, 512)
                    sp_ = nps_b()
                    info[i] = (sp_, cs_, c0, d)
                    k.op("pe", lambda: nc.tensor.matmul(
                        sp_[:, cs_], lhsT=Kh[0:96, kb * 128:(kb + 1) * 128], rhs=Qh[0:96, qs],
                        start=True, stop=True), r=[Kh, Qh], w=[sp_])

                def emit_rest(i):
                    Qt, kb = steps[i]
                    sp_, cs_, c0, d = info.pop(i)
                    nkb = 4 * Qt + 4
                    po = PS[4 + Qt % 2]
                    pt = PT[ptr[0] % 5]
                    ptr[0] += 1
                    k.op("act", lambda: nc.scalar.activation(
                        out=pt[:, cs_], in_=sp_[:, cs_], func=AF.Exp, scale=SCALE), r=[sp_], w=[pt])
                    if d >= 0:
                        k.op("pool", lambda: nc.gpsimd.affine_select(
                            out=pt[:, c0:c0 + 128], in_=pt[:, c0:c0 + 128], pattern=[[1, 128]],
                            compare_op=ALU.is_ge, fill=REG_ZERO, base=0, channel_multiplier=-1),
                            r=[pt], w=[pt])
                    k.op("pe", lambda: nc.tensor.matmul(
                        po[0:65, cs_], lhsT=Vt[:, kb, h, :], rhs=pt[:, cs_], start=(kb == 0),
                        stop=(kb == nkb - 1)), r=[Vt, pt], w=[po], inc=True)
                    if kb == nkb - 1:
                        osb = osb2[Qt % 2]
                        k.op("dve", lambda: nc.vector.tensor_copy(out=osb[:], in_=po[0:65, :]), r=[po], w=[osb])
                        dd("osb", osb, osb[:], [65, 512], F32)
                        return Qt
                    return None

                def finalize(Qt):
                    osb = osb2[Qt % 2]
                    pd = PS[4 + Qt % 2]
                    k.op("pe", lambda: nc.tensor.matmul(pd[0:64, :], lhsT=sel[:, :], rhs=osb[:, :],
                                                        start=True, stop=True), r=[sel, osb], w=[pd])
                    k.op("dve", lambda: nc.vector.reciprocal(out=rec[:], in_=pd[0:64, :]), r=[pd], w=[rec])
                    k.op("dve", lambda: nc.vector.tensor_tensor(out=yh[:], in0=osb[0:64, :], in1=rec[:],
                                                                op=ALU.mult), r=[osb, rec], w=[yh])
                    p0 = (h % 2) * 64
                    k.op("dve", lambda: nc.vector.tensor_copy(
                        out=yaT[p0:p0 + 64, h // 2, Qt * 512:(Qt + 1) * 512], in_=yh[:]), r=[yh], w=[yaT])

                LOOK = 3
                for i in range(min(LOOK, nst)):
                    emit_qk(i)
                pending_fin = []
                prep_next = list(range(NT)) if h < 7 else []
                for i in range(nst):
                    if i + LOOK < nst:
                        emit_qk(i + LOOK)
                    fin = emit_rest(i)
                    pending_fin = [(q, c - 1) for (q, c) in pending_fin]
                    while pending_fin and pending_fin[0][1] <= 0:
                        finalize(pending_fin.pop(0)[0])
                    if fin is not None:
                        pending_fin.append((fin, 2))
                    if prep_next and i % 16 == 8:
                        prep_tile(h + 1, prep_next.pop(0))
                for q, c in pending_fin:
                    finalize(q)
                for Tt in prep_next:
                    prep_tile(h + 1, Tt)
            dd("yaT", yaT, yaT[:], [128, 4, S], BF16)
            ysq = k.sb("ysq", [128, 4, 512], BF16, esB)
            yrs = k.sb("yrs", [128, 512], F32, esB)
            yo = k.sb("yo", [128, 4, 512], BF16, esB)
            for Tt in range(NT):
                ts_ = slice(Tt * 512, (Tt + 1) * 512)
                k.op("dve", lambda ts_=ts_: nc.vector.tensor_tensor(out=ysq[:], in0=yaT[:, :, ts_],
                                                                    in1=yaT[:, :, ts_], op=ALU.mult),
                     r=[yaT], w=[ysq])
                pn = PS[Tt % 4]
                for m in range(4):
                    k.op("pe", lambda m=m, pn=pn: nc.tensor.matmul(pn[:, :], lhsT=ones_b[:, :], rhs=ysq[:, m, :],
                                                                   start=(m == 0), stop=(m == 3)),
                         r=[ones_b, ysq], w=[pn], inc=(m == 3))
                k.op("act", lambda pn=pn: nc.scalar.activation(out=yrs[:], in_=pn[:, :], func=AF.Ln,
                                                               scale=1.0 / 512, bias=epsb[:, :]),
                     r=[pn, epsb], w=[yrs])
                k.op("act", lambda: nc.scalar.activation(out=yrs[:], in_=yrs[:], func=AF.Exp, scale=-0.5),
                     r=[yrs], w=[yrs])
                for m in range(4):
                    k.op("dve", lambda m=m, ts_=ts_: nc.vector.scalar_tensor_tensor(
                        out=yo[:, m, :], in0=yaT[:, m, ts_], scalar=gmo[:, m:m + 1], in1=yrs[:],
                        op0=ALU.mult, op1=ALU.mult), r=[yaT, gmo, yrs], w=[yo])
                dd("yo", yo, yo[:], [128, 4, 512], BF16)
                dd("yrs", yrs, yrs[:], [128, 512], F32)
                if Tt == 1:
                    dd("yo1", yo, yo[:], [128, 4, 512], BF16)
                    dd("yrs1", yrs, yrs[:], [128, 512], F32)
                    dd("ysq1", ysq, ysq[:], [128, 4, 512], BF16)
                for m in range(4):
                    k.dma("act", yT_d[4 + m, :, ts_], yo[:, m, :], r=[yo], w=[yT_buf], own=yo.sub("st"))
            k.wait_all("sp", [yo])

        k.barrier()
        esAB.close()
        esW2 = ExitStack()
        wpp = k.sb("wpp", [128, 2, 1024], BF16, esW2)
        wpg = k.sb("wpg", [128, 8, 1024], BF16, esW2)
        esW1 = ExitStack()
        wup = k.sb("wup", [128, 8, 4096], BF16, esW1)
        wdn = k.sb("wdn", [128, 32, 1024], BF16, esW1)
        g_mlp = k.sb("g_mlp", [128, D], F32, esW1)
        esC = ExitStack()
        with esC:
            wout = k.sb("wout", [128, 8, 1024], BF16, esC)
            for kc in range(8):
                k.dma("pool", wout[:, kc, :], wout_d[:, kc, :], w=[wout], own=wout.sub("ld"))
            for kc in range(8):
                for q4 in range(4):
                    k.dma("pool", wup[:, kc, q4 * 1024:(q4 + 1) * 1024], wup_d[:, kc, q4 * 1024:(q4 + 1) * 1024],
                          w=[wup], own=wup.sub("ld"))
            for fc in range(32):
                k.dma("pool", wdn[:, fc, :], wdn_d[:, fc, :], w=[wdn], own=wdn.sub("ld"))
            k.dma("pool", g_mlp[:], g_mlp_d[:, :], w=[g_mlp], own=g_mlp)
            for kc in range(2):
                k.dma("pool", wpp[:, kc, :], wpp_d[:, kc, :], w=[wpp], own=wpp.sub("ld"))
            for kc in range(8):
                k.dma("pool", wpg[:, kc, :], wpg_d[:, kc, :], w=[wpg], own=wpg.sub("ld"))
            yt2 = [k.sb("yt%d" % i, [128, 8, 512], BF16, esC) for i in range(2)]
            xc = [k.sb("xc%d" % i, [128, D], F32, esC) for i in range(2)]
            hc = [k.sb("hc%d" % i, [128, D], F32, esC) for i in range(2)]

            def c1_block(Tt, blk):
                yt = yt2[Tt % 2]
                r0 = Tt * 512 + blk * 128
                xb = xc[blk % 2]
                hb = hc[blk % 2]
                k.dma("sp", xb[:], x_d[r0:r0 + 128, :], w=[xb], own=xb)
                yield
                for n2 in range(2):
                    pp = nps(blk % 2)
                    for kc in range(8):
                        k.op("pe", lambda kc=kc, pp=pp, n2=n2: nc.tensor.matmul(
                            pp[:, :], lhsT=yt[:, kc, blk * 128:(blk + 1) * 128],
                            rhs=wout[:, kc, n2 * 512:(n2 + 1) * 512], start=(kc == 0), stop=(kc == 7)),
                            r=[yt, wout], w=[pp], inc=(kc == 7))
                    k.op("dve", lambda pp=pp, n2=n2: nc.vector.tensor_tensor(
                        out=hb[:, n2 * 512:(n2 + 1) * 512], in0=pp[:, :], in1=xb[:, n2 * 512:(n2 + 1) * 512],
                        op=ALU.add), r=[pp, xb], w=[hb])
                    yield
                k.dma("act", h1_d[r0:r0 + 128, :], hb[:], r=[hb], w=[h1_buf], own=hb.sub("st"))

            for Tt in range(NT):
                for kc in range(8):
                    k.dma("sp", yt2[Tt % 2][:, kc, :], yT_d[kc, :, Tt * 512:(Tt + 1) * 512], r=[yT_buf],
                          w=[yt2[Tt % 2]], own=yt2[Tt % 2])
                run_interleaved((c1_block(Tt, b) for b in range(4)), 2)
            k.wait_all("sp", hc)

        k.barrier()
        esD = ExitStack()
        with esD:
            ht = [k.sb("ht%d" % i, [128, D], F32, esD) for i in range(2)]
            ho = [k.sb("ho%d" % i, [128, D], F32, esD) for i in range(2)]
            ubm = [k.sb("ubm%d" % i, [128, D], BF16, esD) for i in range(2)]
            uTm = [k.sb("uTm%d" % i, [128, 8, 128], BF16, esD) for i in range(2)]
            ssm = [k.sb("ssm%d" % i, [128, 1], F32, esD) for i in range(2)]
            hT = [k.sb("hT%d" % i, [128, 32, 128], BF16, esD) for i in range(2)]
            rl = [[k.sb("rl%d_%d" % (i, j), [128, 512], BF16, esD) for j in range(2)] for i in range(2)]
            def mlp_block(blk):
                i2 = blk % 2
                r0 = blk * 128
                k.dma("sp", ht[i2][:], h1_d[r0:r0 + 128, :], r=[h1_buf], w=[ht[i2]], own=ht[i2])
                rms_tm(ht[i2][:], [ht[i2]], g_mlp, ubm[i2][:], [ubm[i2]], ubm[i2], ssm[i2])
                dd("ubm", ubm[i2], ubm[i2][:], [128, D], BF16)
                yield
                to_fm(ubm[i2], [ubm[i2]], uTm[i2], [uTm[i2]], 0, 8, p=i2)
                yield
                dd("uTm", uTm[i2], uTm[i2][:], [128, 8, 128], BF16)
                for f4 in range(8):
                    pp = nps(i2)
                    for j in range(4):
                        fc = f4 * 4 + j
                        for kc in range(8):
                            k.op("pe", lambda kc=kc, pp=pp, fc=fc, j=j, i2=i2: nc.tensor.matmul(
                                pp[:, j * 128:(j + 1) * 128], lhsT=wup[:, kc, fc * 128:(fc + 1) * 128],
                                rhs=uTm[i2][:, kc, :], start=(kc == 0), stop=(kc == 7)),
                                r=[wup, uTm[i2]], w=[pp], inc=(kc == 7 and j == 3))
                    rb = rl[i2][f4 % 2]
                    if dbg is not None and 'ppc' in dbg and blk == 0 and f4 == 0:
                        ppc = k.sb("ppc", [128, 512], F32, esD)
                        k.op("dve", lambda pp=pp: nc.vector.tensor_copy(out=ppc[:], in_=pp[:, :]), r=[pp], w=[ppc])
                        dd("ppc", ppc, ppc[:], [128, 512], F32)
                    k.op("act", lambda pp=pp, rb=rb: nc.scalar.activation(out=rb[:], in_=pp[:, :], func=AF.Relu),
                         r=[pp], w=[rb])
                    dd("rl", rb, rb[:], [128, 512], BF16)
                    k.op("pool", lambda rb=rb, f4=f4, i2=i2: nc.gpsimd.tensor_tensor(
                        out=hT[i2][:, f4 * 4:(f4 + 1) * 4, :], in0=rb[:].rearrange("p (j t) -> p j t", j=4),
                        in1=rb[:].rearrange("p (j t) -> p j t", j=4), op=ALU.mult), r=[rb], w=[hT[i2]])
                    yield
                dd("hT", hT[i2], hT[i2][:], [128, 32, 128], BF16)
                dd("wup", wup, wup[:, 0, :], [128, 4096], BF16)
                dd("wdn", wdn, wdn[:, 0, :], [128, 1024], BF16)
                for n2 in range(2):
                    pp = nps(i2)
                    for fc in range(32):
                        k.op("pe", lambda fc=fc, pp=pp, n2=n2, i2=i2: nc.tensor.matmul(
                            pp[:, :], lhsT=hT[i2][:, fc, :], rhs=wdn[:, fc, n2 * 512:(n2 + 1) * 512],
                            start=(fc == 0), stop=(fc == 31)), r=[hT[i2], wdn], w=[pp], inc=(fc == 31))
                    k.op("dve", lambda pp=pp, n2=n2, i2=i2: nc.vector.tensor_tensor(
                        out=ho[i2][:, n2 * 512:(n2 + 1) * 512], in0=pp[:, :], in1=ht[i2][:, n2 * 512:(n2 + 1) * 512],
                        op=ALU.add), r=[pp, ht[i2]], w=[ho[i2]])
                    yield
                k.dma("sp", h2_d[r0:r0 + 128, :], ho[i2][:], r=[ho[i2]], w=[h2_buf], own=ho[i2].sub("st"))
            run_interleaved((mlp_block(b) for b in range(32)), 2, stagger=6)
            k.wait_all("sp", ho)

        k.barrier()
        esW1.close()
        esE = ExitStack()
        with esE:
            gB = {}
            for nm, src in [("post", g_post_d), ("gate", g_gate_d), ("fin", g_fin_d)]:
                gB[nm] = k.sb("gB_" + nm, [128, D], F32, esE)
                k.dma("sp", gB[nm][:], src[:, :], w=[gB[nm]], own=gB[nm])
            h2t = [k.sb("h2t%d" % i, [128, D], F32, esE) for i in range(4)]
            pin = [k.sb("pin%d" % i, [128, 256], F32, esE) for i in range(4)]
            pbf = [k.sb("pbf%d" % i, [128, 256], BF16, esE) for i in range(4)]
            pTt = [k.sb("pTt%d" % i, [128, 2, 128], BF16, esE) for i in range(4)]
            er = [k.sb("er%d" % i, [128, D], F32, esE) for i in range(4)]
            en = [k.sb("en%d" % i, [128, D], F32, esE) for i in range(4)]
            ug = [k.sb("ug%d" % i, [128, D], BF16, esE) for i in range(4)]
            uTg = [k.sb("uTg%d" % i, [128, 8, 128], BF16, esE) for i in range(4)]
            gt = [k.sb("gt%d" % i, [128, D], F32, esE) for i in range(4)]
            h3 = [k.sb("h3%d" % i, [128, D], F32, esE) for i in range(4)]
            fo = [k.sb("fo%d" % i, [128, D], F32, esE) for i in range(4)]
            jk2 = [k.sb("jk%d" % i, [128, D], BF16, esE) for i in range(4)]
            sse = [k.sb("sse%d" % i, [128, 1], F32, esE) for i in range(4)]
            ssg = [k.sb("ssg%d" % i, [128, 1], F32, esE) for i in range(4)]
            ssf = [k.sb("ssf%d" % i, [128, 1], F32, esE) for i in range(4)]
            def ple_block(blk):
                i2 = blk % 4
                r0 = blk * 128
                k.dma("sp", h2t[i2][:], h2_d[r0:r0 + 128, :], r=[h2_buf], w=[h2t[i2]], own=h2t[i2])
                k.dma("sp", pin[i2][:], p_d[r0:r0 + 128, :], w=[pin[i2]], own=pin[i2])
                k.op("dve", lambda i2=i2: nc.vector.tensor_copy(out=pbf[i2][:], in_=pin[i2][:]),
                     r=[pin[i2]], w=[pbf[i2]])
                to_fm(pbf[i2], [pbf[i2]], pTt[i2], [pTt[i2]], 0, 2, p=None)
                yield
                for n2 in range(2):
                    pp = nps()
                    for kc in range(2):
                        k.op("pe", lambda kc=kc, pp=pp, n2=n2, i2=i2: nc.tensor.matmul(
                            pp[:, :], lhsT=pTt[i2][:, kc, :], rhs=wpp[:, kc, n2 * 512:(n2 + 1) * 512],
                            start=(kc == 0), stop=(kc == 1)), r=[pTt[i2], wpp], w=[pp], inc=(kc == 1))
                    k.op("act", lambda pp=pp, n2=n2, i2=i2: nc.scalar.copy(
                        out=er[i2][:, n2 * 512:(n2 + 1) * 512], in_=pp[:, :]), r=[pp], w=[er[i2]])
                yield
                rms_tm(er[i2][:], [er[i2]], gB["post"], en[i2][:], [en[i2]], jk2[i2], sse[i2])
                yield
                rms_tm(h2t[i2][:], [h2t[i2]], gB["gate"], ug[i2][:], [ug[i2]], jk2[i2], ssg[i2])
                yield
                to_fm(ug[i2], [ug[i2]], uTg[i2], [uTg[i2]], 0, 8, p=None)
                yield
                for n2 in range(2):
                    pp = nps()
                    for kc in range(8):
                        k.op("pe", lambda kc=kc, pp=pp, n2=n2, i2=i2: nc.tensor.matmul(
                            pp[:, :], lhsT=uTg[i2][:, kc, :], rhs=wpg[:, kc, n2 * 512:(n2 + 1) * 512],
                            start=(kc == 0), stop=(kc == 7)), r=[uTg[i2], wpg], w=[pp], inc=(kc == 7))
                    gs = gt[i2][:, n2 * 512:(n2 + 1) * 512]
                    k.op("act", lambda pp=pp, gs=gs: nc.scalar.activation(out=gs, in_=pp[:, :], func=AF.Exp,
                                                                          scale=-1.0), r=[pp], w=[gt[i2]])
                yield
                k.op("act", lambda i2=i2: nc.scalar.activation(out=gt[i2][:], in_=gt[i2][:], func=AF.Ln,
                                                               bias=oneb[:, :]), r=[gt[i2], oneb], w=[gt[i2]])
                k.op("act", lambda i2=i2: nc.scalar.activation(out=gt[i2][:], in_=gt[i2][:], func=AF.Exp,
                                                               scale=-1.0), r=[gt[i2]], w=[gt[i2]])
                k.op("pool", lambda i2=i2: nc.gpsimd.tensor_tensor(out=gt[i2][:], in0=gt[i2][:], in1=en[i2][:],
                                                                   op=ALU.mult), r=[gt[i2], en[i2]], w=[gt[i2]])
                k.op("dve", lambda i2=i2: nc.vector.tensor_tensor(out=h3[i2][:], in0=h2t[i2][:], in1=gt[i2][:],
                                                                  op=ALU.add), r=[h2t[i2], gt[i2]], w=[h3[i2]])
                yield
                rms_tm(h3[i2][:], [h3[i2]], gB["fin"], fo[i2][:], [fo[i2]], jk2[i2], ssf[i2])
                k.dma("sp", out_d[r0:r0 + 128, :], fo[i2][:], r=[fo[i2]], w=[], own=fo[i2].sub("st"))
            run_interleaved((ple_block(b) for b in range(32)), 4, stagger=2)
            k.wait_all("sp", fo)
        esW2.close()
        if dbg is not None:
            dsb = Buf("dbgdma")
            k.dma("sp", dbg_y[:, :, :], yT_d[:, :, :], r=[yT_buf], w=[], own=dsb)
            k.dma("sp", dbg_h1[:, :], h1_d[:, :], r=[h1_buf], w=[], own=dsb)
            k.dma("sp", dbg_h2[:, :], h2_d[:, :], r=[h2_buf], w=[], own=dsb)
            k.eng["sp"].wait_ge(dsb.sem, dsb.dtot)
    return nc


def _lay(w, kc):
    K, N = w.shape
    return np.ascontiguousarray(w.reshape(kc, 128, N).transpose(1, 0, 2))


def make_inmaps(inp):
    f = lambda a: np.ascontiguousarray(np.asarray(a, dtype=np.float32))
    bc128 = lambda v: np.ascontiguousarray(np.broadcast_to(f(v).reshape(1, -1), (128, v.size)))
    common = {
        "w_in": _lay(f(inp["w_in"][0]), 8),
        "w_out": _lay(f(inp["w_out"][0]), 8),
        "w_up": _lay(f(inp["w_up"][0]), 8),
        "w_down": _lay(f(inp["w_down"][0]), 32),
        "w_ple_proj": _lay(f(inp["w_ple_proj"][0]), 2),
        "w_ple_gate": _lay(f(inp["w_ple_gate"][0]), 8),
        "w_q_b": _lay(f(inp["w_q_b"][0]), 2),
        "w_kv_b": f(inp["w_kv_b"][0]),
        "g_mix": bc128(inp["mix_norm_w"][0]),
        "g_mlp": bc128(inp["mlp_norm_w"][0]),
        "g_post": bc128(inp["ple_post_norm_w"][0]),
        "g_gate": bc128(inp["ple_gate_norm_w"][0]),
        "g_fin": bc128(inp["final_norm_w"]),
        "g_gdn": np.ascontiguousarray(np.broadcast_to(np.tile(f(inp["gdn_norm_w"][0]), 8).reshape(1, 512), (64, 512))),
        "cw": np.ascontiguousarray(f(inp["conv_w"][0]).T.reshape(24, 64, 4).transpose(1, 0, 2)),
        "alog": np.ascontiguousarray(np.broadcast_to(f(inp["A_log"][0]).reshape(1, 8), (64, 8))),
        "dtb": np.ascontiguousarray(np.broadcast_to(f(inp["dt_bias"][0]).reshape(1, 8), (64, 8))),
        "gq": np.ascontiguousarray(f(inp["q_norm_w"][0]).reshape(2, 128).T),
        "gkv": np.ascontiguousarray(f(inp["kv_norm_w"][0]).reshape(1, 128).T),
        "gmo": np.ascontiguousarray(f(inp["mla_out_norm_w"][0]).reshape(4, 128).T),
    }
    invf = np.zeros((96, 1), np.float32)
    fr = (10000.0 ** (-np.arange(0, 32, 2, dtype=np.float32) / 32)).astype(np.float32)
    invf[64:80, 0] = fr
    invf[80:96, 0] = fr
    common["invf"] = invf
    maps = []
    x = np.asarray(inp["x"], dtype=np.float32)
    p = np.asarray(inp["p"], dtype=np.float32)
    pos = np.asarray(inp["positions"], dtype=np.int32)
    for b in range(8):
        m = dict(common)
        m["x"] = np.ascontiguousarray(x[b])
        m["p"] = np.ascontiguousarray(p[0, b])
        m["pos"] = np.ascontiguousarray(pos[b].reshape(1, S))
        maps.append(m)
    return maps


def kernel(**inputs):
    nc = build_nc()
    maps = make_inmaps(inputs)
    res = run_bass_kernel_spmd(nc, maps, core_ids=list(range(8)))
    return np.stack([np.asarray(r["out"], dtype=np.float32) for r in res.results], axis=0)
```

```python
import numpy as np
import concourse.bass as bass
import concourse.mybir as mybir
from concourse.bass_utils import run_bass_kernel_spmd
from contextlib import ExitStack

F32 = mybir.dt.float32
BF16 = mybir.dt.bfloat16
I32 = mybir.dt.int32
AF = mybir.ActivationFunctionType
ALU = mybir.AluOpType
AX = mybir.AxisListType

S = 4096
D = 1024
NT = 8
EPS = 1e-6
STRICT_SAME = True
RELAX_SAME = False


class Buf:
    __slots__ = ("name", "w", "r", "sem", "dtot", "excl")

    def __init__(self, name):
        self.name = name
        self.excl = False
        self.w = None
        self.r = {}
        self.sem = None
        self.dtot = 0


class T:
    def __init__(self, t, name):
        self.t = t
        self.name = name
        self.b = Buf(name)
        self.subs = {}

    def __getitem__(self, idx):
        return self.t[idx]

    def sub(self, key):
        if key not in self.subs:
            self.subs[key] = Buf(f"{self.name}.{key}")
        return self.subs[key]


class TView(T):
    def __init__(self, base, dt):
        self.t = base.t
        self.name = base.name
        self.b = base.b
        self.subs = base.subs
        self.v = base.t[:, :].bitcast(dt)

    def __getitem__(self, idx):
        return self.v[idx]


class TRe(T):
    def __init__(self, base, pattern, **kw):
        self.t = base.t
        self.name = base.name
        self.b = base.b
        self.subs = base.subs
        self.v = base.t[:].rearrange(pattern, **kw)

    def __getitem__(self, idx):
        return self.v[idx]


class KB:
    def __init__(self, nc, es):
        self.nc = nc
        self.es = es
        self.eng = {"pe": nc.tensor, "act": nc.scalar, "dve": nc.vector, "pool": nc.gpsimd, "sp": nc.sync}
        self.sem = {}
        for n in ["pe", "act", "dve", "pool"]:
            self.sem[n] = es.enter_context(nc.semaphore("c_" + n))
        self.cnt = {n: 0 for n in self.sem}
        self.seen = {n: {} for n in self.eng}
        self.pend = {n: ([], []) for n in self.eng}
        self.nsem = 0
        self.psrr = 0
        self.dbufs = []

    def sb(self, name, shape, dt, es=None):
        es = es or self.es
        return T(es.enter_context(self.nc.sbuf_tensor("s_" + name, list(shape), dt)), name)

    def ps(self, name, shape, dt):
        t = T(self.es.enter_context(self.nc.psum_tensor("p_" + name, list(shape), dt)), name)
        t.b.excl = True
        return t

    def dsem(self, buf):
        if buf.sem is None:
            buf.sem = self.es.enter_context(self.nc.semaphore("d_%d" % self.nsem))
            self.nsem += 1
            self.dbufs.append(buf)
        return buf.sem

    def barrier(self):
        for e in self.eng:
            for n in self.sem:
                if n != e and self.cnt[n] > self.seen[e].get(n, 0):
                    self.eng[e].wait_ge(self.sem[n], self.cnt[n])
                    self.seen[e][n] = self.cnt[n]
            for b in self.dbufs:
                key = "d:" + b.name
                if b.dtot > self.seen[e].get(key, 0):
                    self.eng[e].wait_ge(b.sem, b.dtot)
                    self.seen[e][key] = b.dtot

    @staticmethod
    def _bufs(xs):
        out = []
        for x in xs:
            out.append(x.b if isinstance(x, T) else x)
        return out

    def _wait(self, e, deps):
        for key, (sem, val) in deps.items():
            if key == e and (e == "pe" or not STRICT_SAME):
                continue
            if self.seen[e].get(key, 0) < val:
                self.eng[e].wait_ge(sem, val)
                self.seen[e][key] = val

    @staticmethod
    def _add(deps, ev):
        if ev is None:
            return
        key, sem, val = ev
        if key not in deps or deps[key][1] < val:
            deps[key] = (sem, val)

    def _deps(self, reads, writes, e=None):
        deps = {}
        for b in reads:
            self._add(deps, b.w)
        for b in writes:
            if b.w is not None and b.w[0] != e:
                self._add(deps, b.w)
            for ev in b.r.values():
                if ev[0] != e:
                    self._add(deps, ev)
        return deps

    def op(self, e, fn, r=(), w=(), inc=True):
        reads = self._bufs(r)
        writes = self._bufs(w)
        xr = [b for b in reads if b.excl]
        if xr:
            reads = [b for b in reads if not b.excl]
            writes = writes + [b for b in xr if b not in writes]
        self._wait(e, self._deps(reads, writes, e if RELAX_SAME else None))
        inst = fn()
        if not inc:
            self.pend[e][0].extend(reads)
            self.pend[e][1].extend(writes)
            return inst
        self.cnt[e] += 1
        inst.then_inc(self.sem[e], 1)
        ev = (e, self.sem[e], self.cnt[e])
        pr, pw = self.pend[e]
        for b in reads + pr:
            b.r[e] = ev
        for b in writes + pw:
            b.w = ev
            b.r = {}
        self.pend[e] = ([], [])
        return inst

    def dma(self, q, out, in_, r=(), w=(), own=None, **kw):
        reads = self._bufs(r)
        writes = self._bufs(w)
        ob = own.b if isinstance(own, T) else own
        self._wait(q, self._deps(reads, writes))
        sem = self.dsem(ob)
        ob.dtot += 16
        self.eng[q].dma_start(out=out, in_=in_, **kw).then_inc(sem, 16)
        ev = ("d:" + ob.name, sem, ob.dtot)
        for b in reads:
            b.r["d:" + ob.name] = ev
        for b in writes:
            b.w = ev
            b.r = {}

    def wait_all(self, e, bufs):
        deps = {}
        for b in self._bufs(bufs):
            self._add(deps, b.w)
            for ev in b.r.values():
                self._add(deps, ev)
        self._wait(e, deps)


def build_nc(dbg=None):
    nc = bass.Bass("TRN2", target_bir_lowering=False)

    def din(name, shape, dt=F32):
        return nc.dram_tensor(name, list(shape), dt, kind="ExternalInput").ap()

    x_d = din("x", [S, D])
    p_d = din("p", [S, 256])
    pos_d = din("pos", [1, S], I32)
    win_d = din("w_in", [128, 8, 2480])
    wout_d = din("w_out", [128, 8, 1024])
    wup_d = din("w_up", [128, 8, 4096])
    wdn_d = din("w_down", [128, 32, 1024])
    wpp_d = din("w_ple_proj", [128, 2, 1024])
    wpg_d = din("w_ple_gate", [128, 8, 1024])
    wqb_d = din("w_q_b", [128, 2, 768])
    wkvb_d = din("w_kv_b", [128, 1024])
    g_mix_d = din("g_mix", [128, D])
    g_mlp_d = din("g_mlp", [128, D])
    g_post_d = din("g_post", [128, D])
    g_gate_d = din("g_gate", [128, D])
    g_fin_d = din("g_fin", [128, D])
    g_gdn_d = din("g_gdn", [64, 512])
    cw_d = din("cw", [64, 24, 4])
    alog_d = din("alog", [64, 8])
    dtb_d = din("dtb", [64, 8])
    gq_d = din("gq", [128, 2])
    gkv_d = din("gkv", [128, 1])
    gmo_d = din("gmo", [128, 4])
    invf_d = din("invf", [96, 1])
    out_d = nc.dram_tensor("out", [S, D], F32, kind="ExternalOutput").ap()
    yT_d = nc.dram_tensor("yT_scr", [8, 128, S], BF16, kind="Internal").ap()
    h1_d = nc.dram_tensor("h1_scr", [S, D], F32, kind="Internal").ap()
    if dbg is not None:
        dbg_y = nc.dram_tensor("dbg_y", [8, 128, S], BF16, kind="ExternalOutput").ap()
        dbg_h1 = nc.dram_tensor("dbg_h1", [S, D], F32, kind="ExternalOutput").ap()
        dbg_h2 = nc.dram_tensor("dbg_h2", [S, D], F32, kind="ExternalOutput").ap()

    es = ExitStack()
    with es:
        k = KB(nc, es)
        E = es.enter_context
        yT_buf = Buf("yT_dram")
        h1_buf = Buf("h1_dram")
        ident_b = k.sb("ident_b", [128, 128], BF16)
        ident_f = k.sb("ident_f", [128, 128], F32)
        ones_f = k.sb("ones_f", [128, 128], F32)
        ones_b = k.sb("ones_b", [128, 128], BF16)
        epsb = k.sb("epsb", [128, 1], F32)
        k.op("pool", lambda: nc.gpsimd.memset(epsb[:], EPS), w=[epsb])
        oneb = k.sb("oneb", [128, 1], F32)
        k.op("pool", lambda: nc.gpsimd.memset(oneb[:], 1.0), w=[oneb])
        esAB = ExitStack()
        ident3 = k.sb("ident3", [64, 8, 64], F32, esAB)
        tri_f = k.sb("tri_f", [64, 64], F32, esAB)
        k.op("pool", lambda: nc.gpsimd.memset(ident_f[:], 0.0), w=[ident_f])
        k.op("pool", lambda: nc.gpsimd.affine_select(out=ident_f[:], in_=ident_f[:], pattern=[[-1, 128]],
                                                     compare_op=ALU.not_equal, fill=1.0, base=0,
                                                     channel_multiplier=1), r=[ident_f], w=[ident_f])
        k.op("pool", lambda: nc.gpsimd.tensor_copy(out=ident_b[:], in_=ident_f[:]), r=[ident_f], w=[ident_b])
        k.op("pool", lambda: nc.gpsimd.memset(ones_f[:], 1.0), w=[ones_f])
        k.op("pool", lambda: nc.gpsimd.memset(ones_b[:], 1.0), w=[ones_b])
        k.op("pool", lambda: nc.gpsimd.tensor_copy(
            out=ident3[:], in_=ident_f[0:64, 0:64].unsqueeze(1).to_broadcast([64, 8, 64])), r=[ident_f], w=[ident3])
        k.op("pool", lambda: nc.gpsimd.affine_select(out=tri_f[:], in_=ones_f[0:64, 0:64], pattern=[[1, 64]],
                                                     compare_op=ALU.is_ge, fill=0.0, base=0,
                                                     channel_multiplier=-1), r=[ones_f], w=[tri_f])

        def load_const(name, src, shape, dt=F32, q="sp"):
            t = k.sb(name, shape, dt, esAB)
            k.dma(q, t[:], src, w=[t], own=t)
            return t

        g_gdn = load_const("g_gdn", g_gdn_d[:, :], [64, 512])
        cw = load_const("cw", cw_d[:, :, :], [64, 24, 4])
        alog = load_const("alog", alog_d[:, :], [64, 8])
        dtb = load_const("dtb", dtb_d[:, :], [64, 8])
        gq = load_const("gq", gq_d[:, :], [128, 2])
        gkv = load_const("gkv", gkv_d[:, :], [128, 1])
        gmo = load_const("gmo", gmo_d[:, :], [128, 4])
        invf = load_const("invf", invf_d[:, :], [96, 1])
        nexpA = k.sb("nexpA", [64, 8], F32, esAB)
        k.op("act", lambda: nc.scalar.activation(out=nexpA[:], in_=alog[:], func=AF.Exp), r=[alog], w=[nexpA])
        k.op("dve", lambda: nc.vector.tensor_scalar(out=nexpA[:], in0=nexpA[:], scalar1=-1.0, scalar2=None,
                                                    op0=ALU.mult), r=[nexpA], w=[nexpA])
        REG_NEG = nc.gpsimd.to_reg(-30000.0)
        REG_ZERO = nc.gpsimd.to_reg(0.0)
        dbg_outs = {}

        def dd(name, t, ap, shape, dt):
            if dbg is None or name in dbg_outs or (isinstance(dbg, (set, list, tuple)) and name not in dbg):
                return
            o = nc.dram_tensor("dd_" + name, list(shape), dt, kind="ExternalOutput").ap()
            dbg_outs[name] = o
            k.dma("sp", o, ap, r=[t], w=[], own=Buf("dd_" + name))

        PS = [k.ps("ps%d" % i, [128, 512], F32) for i in range(6)]
        PB = [k.ps("pb%d" % i, [128, 1024], BF16) for i in range(2)]
        rr = {"f": 0, "b": 0}

        srr = {0: 0, 1: 0}

        def nps(p=None):
            if p is None:
                rr["f"] = (rr["f"] + 1) % 6
                return PS[rr["f"]]
            srr[p] = (srr[p] + 1) % 3
            return PS[3 * p + srr[p]]

        srr3 = {0: 0, 1: 0, 2: 0}

        def nps3(p):
            srr3[p] = (srr3[p] + 1) % 2
            return PS[2 * p + srr3[p]]

        def npb(p=None):
            if p is not None:
                return PB[p]
            rr["b"] = (rr["b"] + 1) % 2
            return PB[rr["b"]]

        def run_interleaved(gens, width, bg=None, bg_every=4, stagger=0):
            active = []
            it = iter(gens)
            rnd = 0
            launched = 0
            exhausted = False
            while True:
                while len(active) < width and not exhausted:
                    if launched < width and stagger and launched * stagger > rnd:
                        break
                    g = next(it, None)
                    if g is None:
                        exhausted = True
                        break
                    active.append(g)
                    launched += 1
                if not active and exhausted:
                    break
                for g in list(active):
                    try:
                        next(g)
                    except StopIteration:
                        active.remove(g)
                rnd += 1
                if bg is not None and rnd % bg_every == 0:
                    try:
                        next(bg)
                    except StopIteration:
                        bg = None
            if bg is not None:
                for _ in bg:
                    pass

        def rstd_from_ssq(ssq, n, dim):
            k.op("act", lambda: nc.scalar.activation(out=ssq[0:n, :], in_=ssq[0:n, :], func=AF.Ln,
                                                     scale=1.0 / dim, bias=epsb[0:n, :]), r=[ssq, epsb], w=[ssq])
            k.op("act", lambda: nc.scalar.activation(out=ssq[0:n, :], in_=ssq[0:n, :], func=AF.Exp, scale=-0.5),
                 r=[ssq], w=[ssq])


        def rms_tm(xin, xin_bufs, gB, out, out_bufs, junk, ssq, dim=D):
            k.op("act", lambda: nc.scalar.activation(out=junk[:, 0:dim], in_=xin, func=AF.Square,
                                                     accum_out=ssq[:, 0:1]), r=xin_bufs, w=[junk, ssq])
            rstd_from_ssq(ssq, 128, dim)
            k.op("dve", lambda: nc.vector.scalar_tensor_tensor(out=out, in0=xin, scalar=ssq[:, 0:1], in1=gB[:, 0:dim],
                                                               op0=ALU.mult, op1=ALU.mult),
                 r=list(xin_bufs) + [ssq, gB], w=out_bufs)

        def to_fm(src_bf, src_bufs, dstT, dst_bufs, col0, nkc, p=None):
            pb = npb(p)
            for kc in range(nkc):
                k.op("pe", lambda kc=kc: nc.tensor.transpose(out=pb[:, kc * 128:(kc + 1) * 128],
                                                            in_=src_bf[:, kc * 128:(kc + 1) * 128],
                                                            identity=ident_b[:]),
                     r=list(src_bufs) + [ident_b], w=[pb], inc=(kc == nkc - 1))
            k.op("act", lambda: nc.scalar.copy(
                out=dstT[:, 0:nkc, col0:col0 + 128],
                in_=pb[:, 0:nkc * 128].rearrange("p (k t) -> p k t", k=nkc)), r=[pb], w=dst_bufs)


        TWO_PI = 6.283185307179586

        def sincos_tile(t0, posi, rv, rki, rkf, rfr, rcs, outf=None):
            R = slice(64, 96)
            k.dma("sp", posi[R, :], pos_d[0:1, t0:t0 + 512].to_broadcast([32, 512]), w=[posi], own=posi)
            k.op("dve", lambda: nc.vector.tensor_copy(out=rv[R, :], in_=posi[R, :]), r=[posi], w=[rv])
            k.op("dve", lambda: nc.vector.tensor_scalar(out=rv[R, :], in0=rv[R, :], scalar1=invf[R, 0:1],
                                                        scalar2=1.0 / TWO_PI, op0=ALU.mult, op1=ALU.mult),
                 r=[rv, invf], w=[rv])
            for which in range(2):
                if which == 0:
                    k.op("dve", lambda: nc.vector.tensor_scalar(out=rfr[R, :], in0=rv[R, :], scalar1=0.25,
                                                                scalar2=None, op0=ALU.add), r=[rv], w=[rfr])
                    src = rfr
                else:
                    src = rv
                k.op("dve", lambda src=src: nc.vector.tensor_copy(out=rki[R, :], in_=src[R, :]), r=[src], w=[rki])
                k.op("dve", lambda: nc.vector.tensor_copy(out=rkf[R, :], in_=rki[R, :]), r=[rki], w=[rkf])
                k.op("dve", lambda src=src: nc.vector.tensor_tensor(out=rfr[R, :], in0=src[R, :], in1=rkf[R, :],
                                                                    op=ALU.subtract), r=[src, rkf], w=[rfr])
                k.op("dve", lambda: nc.vector.tensor_scalar(out=rkf[R, :], in0=rfr[R, :], scalar1=0.5, scalar2=None,
                                                            op0=ALU.is_gt), r=[rfr], w=[rkf])
                k.op("dve", lambda: nc.vector.tensor_tensor(out=rfr[R, :], in0=rfr[R, :], in1=rkf[R, :],
                                                            op=ALU.subtract), r=[rfr, rkf], w=[rfr])
                oap = rcs[R, which, :] if outf is None else outf(which)
                k.op("act", lambda oap=oap: nc.scalar.activation(out=oap, in_=rfr[R, :],
                                                                 func=AF.Sin, scale=TWO_PI),
                     r=[rfr], w=[rcs])

        cq_d = nc.dram_tensor("cq_scr", [2, 128, S], BF16, kind="Internal").ap()
        ckv_d = nc.dram_tensor("ckv_scr", [128, S], BF16, kind="Internal").ap()
        kpe_d = nc.dram_tensor("kpe_scr", [32, S], BF16, kind="Internal").ap()
        lat_buf = Buf("lat_dram")
        esA = ExitStack()
        with esA:
            cdiag = k.sb("cdiag", [64, 24, 4, 64], BF16, esA)
            for c in range(24):
                for kk in range(4):
                    k.op("dve", lambda c=c, kk=kk: nc.vector.tensor_scalar(
                        out=cdiag[:, c, kk, :], in0=ident_f[0:64, 0:64], scalar1=cw[:, c, kk:kk + 1], scalar2=None,
                        op0=ALU.mult), r=[ident_f, cw], w=[cdiag], inc=(c == 23 and kk == 3))

            wi = k.sb("wi", [128, 8, 2480], BF16, esA)
            for kc in range(8):
                for hf in range(2):
                    k.dma("pool", wi[:, kc, hf * 1240:(hf + 1) * 1240], win_d[:, kc, hf * 1240:(hf + 1) * 1240],
                          w=[wi], own=wi.sub("ld"))
            g_mix = k.sb("g_mix", [128, D], F32, esA)
            k.dma("sp", g_mix[:], g_mix_d[:, :], w=[g_mix], own=g_mix)
            xt = [k.sb("xt%d" % i, [128, D], F32, esA) for i in range(2)]
            ssq = [k.sb("ssq%d" % i, [128, 1], F32, esA) for i in range(2)]
            ub = [k.sb("ub%d" % i, [128, D], BF16, esA) for i in range(2)]
            uT2 = [k.sb("uT%d" % i, [128, 8, 512], BF16, esA) for i in range(2)]
            pre2 = [k.sb("pre%d" % i, [64, 515], BF16, esA) for i in range(3)]
            halo = k.sb("halo", [64, 24, 3], BF16, esA)
            silt = [k.sb("silt%d" % i, [64, 512], BF16, esA) for i in range(3)]
            silv = k.sb("silv", [64, 8, 512], BF16, esA)
            sq2 = [k.sb("sq%d" % i, [64, 512], BF16, esA) for i in range(3)]
            rn2 = [k.sb("rn%d" % i, [64, 512], F32, esA) for i in range(3)]
            qk_n = k.sb("qk_n", [64, 16, 512], BF16, esA)
            rv = k.sb("rv", [128, 512], F32, esA)
            rki = k.sb("rki", [96, 512], I32, esA)
            posi = k.sb("posi", [96, 512], I32, esA)
            rkf = k.sb("rkf", [96, 512], F32, esA)
            rfr = k.sb("rfr", [96, 512], F32, esA)
            rcs = k.sb("rcs", [96, 2, 512], F32, esA)
            lsq = k.sb("lsq", [128, 3, 512], BF16, esA)
            lrs = rv
            cq_t = k.sb("cq_t", [128, 2, 512], BF16, esA)
            ckv_t = k.sb("ckv_t", [128, 512], BF16, esA)
            kpe_t = k.sb("kpe_t", [96, 512], BF16, esA)
            wrot = k.sb("wrot", [128, 8, 2, 96], BF16, esA)
            k.op("pool", lambda: nc.gpsimd.memset(wrot[:], 0.0), w=[wrot])
            k.op("act", lambda: nc.scalar.copy(out=wrot[:, :, 0, 64:96], in_=wi[:, :, 2448:2480]), r=[wi], w=[wrot])
            k.op("act", lambda: nc.scalar.mul(out=wrot[:, :, 1, 64:80], in_=wi[:, :, 2464:2480], mul=-1.0),
                 r=[wi], w=[wrot])
            k.op("act", lambda: nc.scalar.copy(out=wrot[:, :, 1, 80:96], in_=wi[:, :, 2448:2464]), r=[wi], w=[wrot])
            Sst = k.sb("Sst", [64, 8, 64], F32, esA)
            Sbf = k.sb("Sbf", [64, 8, 64], BF16, esA)
            k.op("pool", lambda: nc.gpsimd.memset(Sst[:], 0.0), w=[Sst])
            k.op("pool", lambda: nc.gpsimd.memset(Sbf[:], 0.0), w=[Sbf])
            k.op("pool", lambda: nc.gpsimd.memset(halo[:], 0.0), w=[halo])
            NSTR = 3

            def parn(name, shape, dt):
                return [k.sb("%s_%d" % (name, i), shape, dt, esA) for i in range(NSTR)]
            sm2 = parn("sm", [64, 12, 8], F32)
            Rm2 = parn("Rm", [64, 8, 64], F32)
            DT2 = parn("DT", [64, 8, 64], F32)
            DTb2 = parn("DTb", [64, 8, 64], F32)
            Z2 = [parn("Z%d" % i, [64, 8, 64], BF16) for i in range(2)]
            ZT2 = [parn("ZT%d" % i, [64, 8, 64], BF16) for i in range(2)]
            Pm2 = [parn("Pm%d" % i, [64, 8, 64], BF16) for i in range(2)]
            AT2 = parn("AT", [64, 8, 64], BF16)
            kvt2 = parn("kvt", [64, 16, 64], BF16)
            tmpf2 = Rm2
            of2 = DTb2
            zg2 = [TRe(DT2[i], "p h i -> p (h i)") for i in range(NSTR)]
            rbf2 = Z2[0]
            vnew2 = Z2[1]
            kd2 = ZT2[0]
            ybf2 = [TRe(ZT2[1][i], "p h i -> p (h i)") for i in range(NSTR)]
            scan_done = [0]
            smT = k.sb("smT", [64, 7, 8, 8], F32, esA)
            sel63 = k.sb("sel63", [64, 64], F32, esA)
            k.op("pool", lambda: nc.gpsimd.affine_select(out=sel63[:], in_=ones_f[0:64, 0:64], pattern=[[0, 64]],
                                                         compare_op=ALU.is_equal, fill=REG_ZERO, base=-63,
                                                         channel_multiplier=1), r=[ones_f], w=[sel63])
            ygT = k.sb("ygT", [128, 4, 512], BF16, esA)

            def bc(ap2):
                return ap2.unsqueeze(2).to_broadcast([64, 8, 64])

            def a1_stream(Tt):
                for blk in range(4):
                    xb = xt[blk % 2]
                    r0 = Tt * 512 + blk * 128
                    k.dma("sp", xb[:], x_d[r0:r0 + 128, :], w=[xb], own=xb)
                    rms_tm(xb[:], [xb], g_mix, ub[blk % 2][:], [ub[blk % 2]], ub[blk % 2], ssq[blk % 2])
                    yield
                    to_fm(ub[blk % 2], [ub[blk % 2]], uT2[Tt % 2], [uT2[Tt % 2]], blk * 128, 8)
                    yield

            for _ in a1_stream(0):
                pass
            for Tt in range(NT):
                t0 = Tt * 512
                uT = uT2[Tt % 2]
                def ht_stream(c):
                    rn = rn2[c % 3]
                    sq = sq2[c % 3]
                    pp = nps3(c % 3)
                    for kc in range(8):
                        k.op("pe", lambda kc=kc, c=c, pp=pp: nc.tensor.matmul(
                            pp[0:64, :], lhsT=wi[:, kc, c * 64:(c + 1) * 64], rhs=uT[:, kc, :],
                            start=(kc == 0), stop=(kc == 7)), r=[wi.sub("ld"), wi, uT], w=[pp], inc=(kc == 7))
                    yield
                    pr_ = pre2[c % 3]
                    k.op("act", lambda c=c, pr_=pr_: nc.scalar.copy(out=pr_[:, 0:3], in_=halo[:, c, :]),
                         r=[halo], w=[pr_])
                    k.op("act", lambda c=c, pp=pp, pr_=pr_: nc.scalar.copy(out=pr_[:, 3:515], in_=pp[0:64, :]),
                         r=[pp], w=[pr_])
                    k.op("act", lambda c=c, pr_=pr_: nc.scalar.copy(out=halo[:, c, :], in_=pr_[:, 512:515]),
                         r=[pr_], w=[halo])
                    pc = nps3(c % 3)
                    for kk in range(4):
                        k.op("pe", lambda kk=kk, c=c, pc=pc, pr_=pr_: nc.tensor.matmul(
                            pc[0:64, :], lhsT=cdiag[:, c, kk, :], rhs=pr_[:, kk:kk + 512],
                            start=(kk == 0), stop=(kk == 3)), r=[cdiag, pr_], w=[pc], inc=(kk == 3))
                    yield
                    k.op("act", lambda pc=pc: nc.scalar.activation(out=rn[:], in_=pc[0:64, :], func=AF.Exp,
                                                                   scale=-1.0), r=[pc], w=[rn])
                    k.op("act", lambda: nc.scalar.activation(out=rn[:], in_=rn[:], func=AF.Ln, bias=oneb[0:64, :]),
                         r=[rn, oneb], w=[rn])
                    k.op("act", lambda: nc.scalar.activation(out=rn[:], in_=rn[:], func=AF.Exp, scale=-1.0),
                         r=[rn], w=[rn])
                    yield
                    so_ = silt[c % 3] if c < 16 else silv
                    so_ap = silt[c % 3][:, :] if c < 16 else silv[:, c - 16, :]
                    k.op("dve", lambda pc=pc, so_ap=so_ap: nc.vector.tensor_tensor(out=so_ap, in0=pc[0:64, :],
                                                                                   in1=rn[:], op=ALU.mult),
                         r=[pc, rn], w=[so_])
                    yield
                    if c < 16:
                        k.op("dve", lambda so_ap=so_ap: nc.vector.tensor_tensor(out=sq[:], in0=so_ap,
                                                                                in1=so_ap, op=ALU.mult),
                             r=[so_], w=[sq])
                        yield
                        pn = nps3(c % 3)
                        k.op("pe", lambda pn=pn: nc.tensor.matmul(pn[0:64, :], lhsT=ones_b[0:64, 0:64], rhs=sq[:],
                                                                  start=True, stop=True), r=[ones_b, sq], w=[pn])
                        k.op("act", lambda pn=pn: nc.scalar.activation(out=rn[:], in_=pn[0:64, :], func=AF.Ln,
                                                                       bias=epsb[0:64, :]), r=[pn, epsb], w=[rn])
                        k.op("act", lambda: nc.scalar.activation(out=rn[:], in_=rn[:], func=AF.Exp, scale=-0.5),
                             r=[rn], w=[rn])
                        yield
                        sc = 0.125 if c < 8 else 1.0
                        k.op("dve", lambda c=c, sc=sc, so_ap=so_ap: nc.vector.scalar_tensor_tensor(
                            out=qk_n[:, c, :], in0=so_ap, scalar=sc, in1=rn[:], op0=ALU.mult, op1=ALU.mult),
                            r=[so_, rn], w=[qk_n])
                run_interleaved((ht_stream(c) for c in range(24)), 3, stagger=2)
                lat_cols = [(2064, 128), (2192, 128), (2320, 128)]
                plat = []
                for m, (c0, wd) in enumerate(lat_cols):
                    pp = nps()
                    plat.append(pp)
                    for kc in range(8):
                        k.op("pe", lambda kc=kc, c0=c0, pp=pp: nc.tensor.matmul(
                            pp[:, :], lhsT=wi[:, kc, c0:c0 + 128], rhs=uT[:, kc, :],
                            start=(kc == 0), stop=(kc == 7)), r=[wi, uT], w=[pp], inc=(kc == 7))
                    k.op("act", lambda m=m, pp=pp: nc.scalar.activation(out=lsq[:, m, :], in_=pp[:, :],
                                                                        func=AF.Square), r=[pp], w=[lsq])
                pn = nps()
                for m in range(2):
                    k.op("pe", lambda m=m, pn=pn: nc.tensor.matmul(pn[:, :], lhsT=ones_b[:, :], rhs=lsq[:, m, :],
                                                                   start=(m == 0), stop=(m == 1)),
                         r=[ones_b, lsq], w=[pn], inc=(m == 1))
                k.op("act", lambda pn=pn: nc.scalar.activation(out=lrs[:], in_=pn[:, :], func=AF.Ln, scale=1.0 / 256,
                                                               bias=epsb[:, :]), r=[pn, epsb], w=[lrs])
                k.op("act", lambda: nc.scalar.activation(out=lrs[:], in_=lrs[:], func=AF.Exp, scale=-0.5),
                     r=[lrs], w=[lrs])
                for m in range(2):
                    k.op("dve", lambda m=m: nc.vector.scalar_tensor_tensor(
                        out=cq_t[:, m, :], in0=plat[m][:, :], scalar=gq[:, m:m + 1], in1=lrs[:],
                        op0=ALU.mult, op1=ALU.mult), r=[plat[m], gq, lrs], w=[cq_t])
                pn = nps()
                k.op("pe", lambda pn=pn: nc.tensor.matmul(pn[:, :], lhsT=ones_b[:, :], rhs=lsq[:, 2, :],
                                                          start=True, stop=True), r=[ones_b, lsq], w=[pn])
                k.op("act", lambda pn=pn: nc.scalar.activation(out=lrs[:], in_=pn[:, :], func=AF.Ln, scale=1.0 / 128,
                                                               bias=epsb[:, :]), r=[pn, epsb], w=[lrs])
                k.op("act", lambda: nc.scalar.activation(out=lrs[:], in_=lrs[:], func=AF.Exp, scale=-0.5),
                     r=[lrs], w=[lrs])
                k.op("dve", lambda: nc.vector.scalar_tensor_tensor(
                    out=ckv_t[:], in0=plat[2][:, :], scalar=gkv[:, 0:1], in1=lrs[:],
                    op0=ALU.mult, op1=ALU.mult), r=[plat[2], gkv, lrs], w=[ckv_t])
                sincos_tile(t0, posi, rv, rki, rkf, rfr, rcs)
                pks = []
                for m in range(2):
                    pp = nps()
                    pks.append(pp)
                    for kc in range(8):
                        lw = wrot[:, kc, m, :]
                        k.op("pe", lambda kc=kc, pp=pp, lw=lw: nc.tensor.matmul(
                            pp[0:96, :], lhsT=lw, rhs=uT[:, kc, :], start=(kc == 0), stop=(kc == 7)),
                            r=[wi, wrot, uT], w=[pp], inc=(kc == 7))
                k.op("dve", lambda: nc.vector.tensor_tensor(out=rkf[64:96, :], in0=pks[0][64:96, :],
                                                            in1=rcs[64:96, 0, :], op=ALU.mult),
                     r=[pks[0], rcs], w=[rkf])
                k.op("dve", lambda: nc.vector.tensor_tensor(out=rfr[64:96, :], in0=pks[1][64:96, :],
                                                            in1=rcs[64:96, 1, :], op=ALU.mult),
                     r=[pks[1], rcs], w=[rfr])
                k.op("dve", lambda: nc.vector.tensor_tensor(out=kpe_t[64:96, :], in0=rkf[64:96, :],
                                                            in1=rfr[64:96, :], op=ALU.add),
                     r=[rkf, rfr], w=[kpe_t])
                for m in range(2):
                    k.dma("act", cq_d[m, :, t0:t0 + 512], cq_t[:, m, :], r=[cq_t], w=[lat_buf], own=cq_t.sub("st"))
                k.dma("act", ckv_d[:, t0:t0 + 512], ckv_t[:], r=[ckv_t], w=[lat_buf], own=ckv_t.sub("st"))
                k.dma("act", kpe_d[:, t0:t0 + 512], kpe_t[64:96, :], r=[kpe_t], w=[lat_buf], own=kpe_t.sub("st"))
                pa = nps()
                for n in range(8):
                    for kc in range(8):
                        k.op("pe", lambda kc=kc, n=n, pa=pa: nc.tensor.matmul(
                            pa[0:64, n * 16:(n + 1) * 16], lhsT=uT[:, kc, n * 64:(n + 1) * 64],
                            rhs=wi[:, kc, 2048:2064], start=(kc == 0), stop=(kc == 7)),
                            r=[wi, uT], w=[pa], inc=(kc == 7 and n == 7))
                pa3 = pa[0:64, 0:128].rearrange("p (n c) -> p n c", c=16)
                bT, gT, eT, edT, sdT, t1T, glT = [smT[:, j, :, :] for j in range(7)]
                k.op("act", lambda: nc.scalar.activation(out=bT, in_=pa3[:, :, 0:8], func=AF.Exp, scale=-1.0),
                     r=[pa], w=[smT])
                k.op("dve", lambda: nc.vector.tensor_tensor(
                    out=t1T, in0=pa3[:, :, 8:16], in1=dtb[:].unsqueeze(1).to_broadcast([64, 8, 8]), op=ALU.add),
                    r=[pa, dtb], w=[smT])
                k.op("dve", lambda: nc.vector.tensor_scalar(out=bT, in0=bT, scalar1=1.0, scalar2=None, op0=ALU.add),
                     r=[smT], w=[smT])
                k.op("dve", lambda: nc.vector.reciprocal(out=bT, in_=bT), r=[smT], w=[smT])
                k.op("act", lambda: nc.scalar.activation(out=t1T, in_=t1T, func=AF.Exp), r=[smT], w=[smT])
                k.op("act", lambda: nc.scalar.activation(out=t1T, in_=t1T, func=AF.Ln, bias=oneb[0:64, :]),
                     r=[smT, oneb], w=[smT])
                k.op("dve", lambda: nc.vector.tensor_tensor(
                    out=t1T, in0=t1T, in1=nexpA[:].unsqueeze(1).to_broadcast([64, 8, 8]), op=ALU.mult),
                    r=[smT, nexpA], w=[smT])
                pg = nps()
                k.op("pe", lambda: nc.tensor.matmul(pg[0:64, 0:64], lhsT=tri_f[:, :],
                                                    rhs=smT[:, 5, :, :].rearrange("p n h -> p (n h)"),
                                                    start=True, stop=True), r=[tri_f, smT], w=[pg])
                k.op("dve", lambda: nc.vector.tensor_copy(
                    out=gT, in_=pg[0:64, 0:64].rearrange("p (n h) -> p n h", h=8)), r=[pg], w=[smT])
                k.op("act", lambda: nc.scalar.activation(out=eT, in_=gT, func=AF.Exp), r=[smT], w=[smT])
                pl = nps()
                k.op("pe", lambda: nc.tensor.matmul(pl[0:64, 0:64], lhsT=sel63[:, :],
                                                    rhs=smT[:, 1, :, :].rearrange("p n h -> p (n h)"),
                                                    start=True, stop=True), r=[sel63, smT], w=[pl])
                k.op("dve", lambda: nc.vector.tensor_copy(
                    out=glT, in_=pl[0:64, 0:64].rearrange("p (n h) -> p n h", h=8)), r=[pl], w=[smT])
                k.op("act", lambda: nc.scalar.activation(out=sdT, in_=glT, func=AF.Exp), r=[smT], w=[smT])
                k.op("dve", lambda: nc.vector.tensor_tensor(out=edT, in0=glT, in1=gT, op=ALU.subtract),
                     r=[smT], w=[smT])
                k.op("act", lambda: nc.scalar.activation(out=edT, in_=edT, func=AF.Exp), r=[smT], w=[smT])
                def chunk_stream(n):
                    par = n % NSTR
                    gidx = Tt * 8 + n
                    sm = sm2[par]; Rm = Rm2[par]; Dm = Rm; DT = DT2[par]; DTb = DTb2[par]; Xf = DTb
                    Z = [Z2[0][par], Z2[1][par]]; ZT = [ZT2[0][par], ZT2[1][par]]; Pm = [Pm2[0][par], Pm2[1][par]]
                    AT = AT2[par]; kvt = kvt2[par]; kd = kd2[par]; tmpf = tmpf2[par]; rbf = rbf2[par]
                    vnew = vnew2[par]; of = of2[par]; zg = zg2[par]; ybf = ybf2[par]
                    cs = slice(n * 64, n * 64 + 64)
                    bcol = smT[:, 0, n, :]
                    gcol = smT[:, 1, n, :]
                    ecol = smT[:, 2, n, :]
                    edc = smT[:, 3, n, :]
                    sdc = smT[:, 4, n, :]
                    SB = SG = SE = SED = SSD = smT
                    yield
                    k.op("pool", lambda: nc.gpsimd.tensor_tensor(out=Rm[:], in0=ident3[:], in1=bc(gcol), op=ALU.mult),
                         r=[ident3, smT], w=[Rm])
                    pG = nps3(par)
                    yield
                    k.op("pe", lambda pG=pG: nc.tensor.matmul(pG[0:64, :], lhsT=ones_f[0:64, 0:64],
                                                              rhs=Rm[:].rearrange("p h i -> p (h i)"),
                                                              start=True, stop=True), r=[ones_f, Rm], w=[pG])
                    pG3 = pG[0:64, :].rearrange("p (h i) -> p h i", h=8)
                    yield
                    k.op("dve", lambda pG3=pG3: nc.vector.tensor_tensor(out=Dm[:], in0=pG3, in1=bc(gcol),
                                                                        op=ALU.subtract),
                         r=[pG, smT], w=[Dm])
                    yield
                    k.op("pool", lambda: nc.gpsimd.affine_select(
                        out=Dm[:], in_=Dm[:], pattern=[[0, 8], [1, 64]], compare_op=ALU.is_ge, fill=REG_NEG,
                        base=0, channel_multiplier=-1), r=[Dm], w=[Dm])
                    yield
                    k.op("act", lambda: nc.scalar.activation(out=DT[:], in_=Dm[:], func=AF.Exp), r=[Dm], w=[DT])
                    yield
                    k.op("pool", lambda: nc.gpsimd.tensor_tensor(out=DTb[:], in0=DT[:], in1=bc(bcol), op=ALU.mult),
                         r=[DT, smT], w=[DTb])
                    pK = nps3(par)
                    pQ = nps3(par)
                    yield
                    for h in range(8):
                        k.op("pe", lambda h=h, pK=pK: nc.tensor.matmul(
                            pK[0:64, h * 64:(h + 1) * 64], lhsT=qk_n[:, 8 + h, cs], rhs=qk_n[:, 8 + h, cs],
                            start=True, stop=True), r=[qk_n], w=[pK], inc=(h == 7))
                    yield
                    for h in range(8):
                        k.op("pe", lambda h=h, pQ=pQ: nc.tensor.matmul(
                            pQ[0:64, h * 64:(h + 1) * 64], lhsT=qk_n[:, 8 + h, cs], rhs=qk_n[:, h, cs],
                            start=True, stop=True), r=[qk_n], w=[pQ], inc=(h == 7))
                    yield
                    k.op("dve", lambda pK=pK: nc.vector.tensor_tensor(
                        out=Xf[:], in0=pK[0:64, :].rearrange("p (h i) -> p h i", h=8), in1=DTb[:], op=ALU.mult),
                        r=[pK, DTb], w=[Xf])
                    yield
                    k.op("pool", lambda: nc.gpsimd.affine_select(
                        out=Z[0][:], in_=Xf[:], pattern=[[0, 8], [1, 64]], compare_op=ALU.is_gt, fill=REG_ZERO,
                        base=0, channel_multiplier=-1), r=[Xf], w=[Z[0]])
                    yield
                    k.op("dve", lambda pQ=pQ: nc.vector.tensor_tensor(
                        out=AT[:], in0=pQ[0:64, :].rearrange("p (h i) -> p h i", h=8), in1=DT[:], op=ALU.mult),
                        r=[pQ, DT], w=[AT])
                    yield
                    pb = npb()
                    for h in range(8):
                        k.op("pe", lambda h=h, pb=pb: nc.tensor.transpose(
                            out=pb[0:64, h * 64:(h + 1) * 64], in_=qk_n[:, 8 + h, cs], identity=ident_b[0:64, 0:64]),
                            r=[qk_n, ident_b], w=[pb], inc=False)
                    for h in range(8):
                        k.op("pe", lambda h=h, pb=pb: nc.tensor.transpose(
                            out=pb[0:64, (8 + h) * 64:(9 + h) * 64], in_=silv[:, h, cs],
                            identity=ident_b[0:64, 0:64]), r=[silv, ident_b], w=[pb], inc=(h == 7))
                    k.op("act", lambda pb=pb: nc.scalar.copy(
                        out=kvt[:], in_=pb[0:64, :].rearrange("p (c d) -> p c d", c=16)), r=[pb], w=[kvt])
                    yield
                    pb = npb()
                    for h in range(8):
                        k.op("pe", lambda h=h, pb=pb: nc.tensor.transpose(
                            out=pb[0:64, h * 64:(h + 1) * 64], in_=Z[0][:, h, :], identity=ident_b[0:64, 0:64]),
                            r=[Z[0], ident_b], w=[pb], inc=(h == 7))
                    k.op("act", lambda pb=pb: nc.scalar.copy(
                        out=ZT[0][:], in_=pb[0:64, 0:512].rearrange("p (h i) -> p h i", h=8)), r=[pb], w=[ZT[0]])
                    yield
                    k.op("pool", lambda: nc.gpsimd.tensor_tensor(out=Pm[0][:], in0=ident3[:], in1=Z[0][:],
                                                                 op=ALU.subtract), r=[ident3, Z[0]], w=[Pm[0]])
                    cur = 0
                    yield
                    for lev in range(1, 6):
                        nxt = 1 - cur
                        yield
                        pzt = nps3(par)
                        for h in range(8):
                            k.op("pe", lambda h=h, pzt=pzt, cur=cur: nc.tensor.matmul(
                                pzt[0:64, h * 64:(h + 1) * 64], lhsT=Z[cur][:, h, :], rhs=ZT[cur][:, h, :],
                                start=True, stop=True), r=[Z[cur], ZT[cur]], w=[pzt], inc=(h == 7))
                        if lev < 5:
                            pz = nps3(par)
                            for h in range(8):
                                k.op("pe", lambda h=h, pz=pz, cur=cur: nc.tensor.matmul(
                                    pz[0:64, h * 64:(h + 1) * 64], lhsT=ZT[cur][:, h, :], rhs=Z[cur][:, h, :],
                                    start=True, stop=True), r=[Z[cur], ZT[cur]], w=[pz], inc=(h == 7))
                        yield
                        k.op("act", lambda pzt=pzt, nxt=nxt: nc.scalar.copy(
                            out=ZT[nxt][:], in_=pzt[0:64, :].rearrange("p (h i) -> p h i", h=8)),
                            r=[pzt], w=[ZT[nxt]])
                        if lev < 5:
                            k.op("dve", lambda pz=pz, nxt=nxt: nc.vector.tensor_copy(
                                out=Z[nxt][:], in_=pz[0:64, :].rearrange("p (h i) -> p h i", h=8)),
                                r=[pz], w=[Z[nxt]])
                        yield
                        pp = nps3(par)
                        for h in range(8):
                            k.op("pe", lambda h=h, pp=pp, nxt=nxt, cur=cur: nc.tensor.matmul(
                                pp[0:64, h * 64:(h + 1) * 64], lhsT=ZT[nxt][:, h, :], rhs=Pm[cur][:, h, :],
                                start=True, stop=False), r=[ZT[nxt], Pm[cur]], w=[pp], inc=False)
                            k.op("pe", lambda h=h, pp=pp, nxt=nxt, cur=cur: nc.tensor.matmul(
                                pp[0:64, h * 64:(h + 1) * 64], lhsT=ident_b[0:64, 0:64], rhs=Pm[cur][:, h, :],
                                start=False, stop=True), r=[ident_b, Pm[cur]], w=[pp], inc=(h == 7))
                        yield
                        k.op("act", lambda pp=pp, nxt=nxt, cur=cur: nc.scalar.copy(
                            out=Pm[nxt][:], in_=pp[0:64, :].rearrange("p (h i) -> p h i", h=8)),
                            r=[pp], w=[Pm[nxt]])
                        cur = nxt
                    G = Pm[cur]
                    yield
                    k.op("pool", lambda: nc.gpsimd.tensor_tensor(out=kd[:], in0=kvt[:, 0:8, :], in1=bc(edc),
                                                                 op=ALU.mult), r=[kvt, smT], w=[kd])
                    while scan_done[0] < gidx:
                        yield
                    pS = nps3(par)
                    for h in range(8):
                        k.op("pe", lambda h=h, pS=pS: nc.tensor.matmul(
                            pS[0:64, h * 64:(h + 1) * 64], lhsT=qk_n[:, 8 + h, cs], rhs=Sbf[:, h, :],
                            start=True, stop=True), r=[qk_n, Sbf], w=[pS], inc=(h == 7))
                    pO1 = nps3(par)
                    for h in range(8):
                        k.op("pe", lambda h=h, pO1=pO1: nc.tensor.matmul(
                            pO1[0:64, h * 64:(h + 1) * 64], lhsT=qk_n[:, h, cs], rhs=Sbf[:, h, :],
                            start=True, stop=True), r=[qk_n, Sbf], w=[pO1], inc=(h == 7))
                    k.op("dve", lambda pS=pS: nc.vector.tensor_tensor(
                        out=tmpf[:], in0=pS[0:64, :].rearrange("p (h i) -> p h i", h=8), in1=bc(ecol), op=ALU.mult),
                        r=[pS, smT], w=[tmpf])
                    k.op("dve", lambda: nc.vector.tensor_tensor(out=rbf[:], in0=kvt[:, 8:16, :], in1=tmpf[:],
                                                                op=ALU.subtract), r=[kvt, tmpf], w=[rbf])
                    yield
                    pT_ = nps3(par)
                    for h in range(8):
                        k.op("pe", lambda h=h, pT_=pT_: nc.tensor.matmul(
                            pT_[0:64, h * 64:(h + 1) * 64], lhsT=G[:, h, :], rhs=rbf[:, h, :],
                            start=True, stop=True), r=[G, rbf], w=[pT_], inc=(h == 7))
                    k.op("dve", lambda pT_=pT_: nc.vector.tensor_tensor(
                        out=vnew[:], in0=pT_[0:64, :].rearrange("p (h i) -> p h i", h=8), in1=bc(bcol), op=ALU.mult),
                        r=[pT_, smT], w=[vnew])
                    k.op("dve", lambda pO1=pO1: nc.vector.tensor_tensor(
                        out=of[:], in0=pO1[0:64, :].rearrange("p (h i) -> p h i", h=8), in1=bc(ecol), op=ALU.mult),
                        r=[pO1, smT], w=[of])
                    k.op("dve", lambda: nc.vector.tensor_tensor(out=Sst[:], in0=Sst[:], in1=bc(sdc), op=ALU.mult),
                         r=[Sst, smT], w=[Sst])
                    yield
                    pU = nps3(par)
                    for h in range(8):
                        k.op("pe", lambda h=h, pU=pU: nc.tensor.matmul(
                            pU[0:64, h * 64:(h + 1) * 64], lhsT=kd[:, h, :], rhs=vnew[:, h, :],
                            start=True, stop=True), r=[kd, vnew], w=[pU], inc=(h == 7))
                    k.op("dve", lambda pU=pU: nc.vector.tensor_tensor(
                        out=Sbf[:], in0=pU[0:64, :].rearrange("p (h i) -> p h i", h=8), in1=Sst[:], op=ALU.add),
                        r=[pU, Sst], w=[Sbf])
                    scan_done[0] = gidx + 1
                    k.op("dve", lambda pU=pU: nc.vector.tensor_tensor(
                        out=Sst[:], in0=pU[0:64, :].rearrange("p (h i) -> p h i", h=8), in1=Sst[:], op=ALU.add),
                        r=[pU, Sst], w=[Sst])
                    yield
                    pO2 = nps3(par)
                    for h in range(8):
                        k.op("pe", lambda h=h, pO2=pO2: nc.tensor.matmul(
                            pO2[0:64, h * 64:(h + 1) * 64], lhsT=AT[:, h, :], rhs=vnew[:, h, :],
                            start=True, stop=True), r=[AT, vnew], w=[pO2], inc=(h == 7))
                    yield
                    k.op("dve", lambda pO2=pO2: nc.vector.tensor_tensor(
                        out=of[:], in0=pO2[0:64, :].rearrange("p (h i) -> p h i", h=8), in1=of[:], op=ALU.add),
                        r=[pO2, of], w=[of])
                    yield
                    k.op("pool", lambda: nc.gpsimd.tensor_tensor(out=tmpf[:], in0=of[:], in1=of[:], op=ALU.mult),
                         r=[of], w=[tmpf])
                    osq = sm[:, 7, :]
                    yield
                    k.op("dve", lambda: nc.vector.tensor_reduce(out=osq, in_=tmpf[:], axis=AX.X, op=ALU.add),
                         r=[tmpf], w=[sm.sub("osq")])
                    yield
                    k.op("act", lambda: nc.scalar.activation(out=osq, in_=osq, func=AF.Ln, scale=1.0 / 64,
                                                             bias=epsb[0:64, :]),
                         r=[sm.sub("osq"), epsb], w=[sm.sub("osq")])
                    yield
                    k.op("act", lambda: nc.scalar.activation(out=osq, in_=osq, func=AF.Exp, scale=-0.5),
                         r=[sm.sub("osq")], w=[sm.sub("osq")])
                    yield
                    k.op("dve", lambda: nc.vector.tensor_tensor(out=of[:], in0=of[:], in1=bc(osq), op=ALU.mult),
                         r=[of, sm.sub("osq")], w=[of])
                    pz_ = nps3(par)
                    yield
                    for kc in range(8):
                        k.op("pe", lambda kc=kc, pz_=pz_: nc.tensor.matmul(
                            pz_[0:64, :], lhsT=uT[:, kc, cs], rhs=wi[:, kc, 1536:2048],
                            start=(kc == 0), stop=(kc == 7)), r=[wi, uT], w=[pz_], inc=(kc == 7))
                    yield
                    k.op("act", lambda pz_=pz_: nc.scalar.activation(out=zg[:], in_=pz_[0:64, :], func=AF.Exp,
                                                                     scale=-1.0), r=[pz_], w=[zg])
                    yield
                    k.op("act", lambda: nc.scalar.activation(out=zg[:], in_=zg[:], func=AF.Ln, bias=oneb[0:64, :]),
                         r=[zg, oneb], w=[zg])
                    yield
                    k.op("act", lambda: nc.scalar.activation(out=zg[:], in_=zg[:], func=AF.Exp, scale=-1.0),
                         r=[zg], w=[zg])
                    yield
                    k.op("dve", lambda pz_=pz_: nc.vector.tensor_tensor(out=zg[:], in0=pz_[0:64, :], in1=zg[:],
                                                                        op=ALU.mult), r=[pz_, zg], w=[zg])
                    yield
                    k.op("pool", lambda: nc.gpsimd.tensor_tensor(out=zg[:], in0=zg[:], in1=g_gdn[:], op=ALU.mult),
                         r=[zg, g_gdn], w=[zg])
                    yield
                    k.op("dve", lambda: nc.vector.tensor_tensor(
                        out=ybf[:], in0=of[:].rearrange("p h i -> p (h i)"), in1=zg[:], op=ALU.mult),
                        r=[of, zg], w=[ybf])
                    yield
                    pb = npb()
                    for m in range(4):
                        k.op("pe", lambda m=m, pb=pb: nc.tensor.transpose(
                            out=pb[:, m * 64:(m + 1) * 64], in_=ybf[:, m * 128:(m + 1) * 128],
                            identity=ident_b[0:64, 0:64]), r=[ybf, ident_b], w=[pb], inc=(m == 3))
                    k.op("act", lambda pb=pb: nc.scalar.copy(
                        out=ygT[:, :, cs], in_=pb[:, 0:256].rearrange("p (m t) -> p m t", m=4)), r=[pb], w=[ygT])
                run_interleaved((chunk_stream(n) for n in range(8)), NSTR,
                                bg=(a1_stream(Tt + 1) if Tt + 1 < NT else None), stagger=12)
                for m in range(4):
                    k.dma("act", yT_d[m, :, t0:t0 + 512], ygT[:, m, :], r=[ygT], w=[yT_buf], own=ygT.sub("st"))
            k.wait_all("sp", [ygT, cq_t, ckv_t, kpe_t])

        h2_d = nc.dram_tensor("h2_scr", [S, D], F32, kind="Internal").ap()
        h2_buf = Buf("h2_dram")
        SCALE = 96.0 ** -0.5
        k.barrier()
        esB = ExitStack()
        with esB:
            cqT = k.sb("cqT", [128, 2, S], BF16, esB)
            ckvT = k.sb("ckvT", [128, S], BF16, esB)
            kper = k.sb("kper", [96, S], BF16, esB)
            for m in range(2):
                k.dma("sp", cqT[:, m, :], cq_d[m, :, :], r=[lat_buf], w=[cqT], own=cqT)
            k.dma("sp", ckvT[:], ckv_d[:, :], r=[lat_buf], w=[ckvT], own=ckvT)
            k.dma("sp", kper[64:96, :], kpe_d[:, :], r=[lat_buf], w=[kper], own=kper)
            wqb = k.sb("wqb", [128, 2, 768], BF16, esB)
            k.dma("pool", wqb[:], wqb_d[:, :, :], w=[wqb], own=wqb)
            wkvb = k.sb("wkvb", [128, 1024], BF16, esB)
            k.dma("pool", wkvb[:], wkvb_d[:, :], w=[wkvb], own=wkvb)
            wqrot = k.sb("wqrot", [128, 2, 8, 96], BF16, esB)
            wq4 = wqb[:].rearrange("p m (h c) -> p m h c", c=96)
            k.op("pool", lambda: nc.gpsimd.memset(wqrot[:], 0.0), w=[wqrot])
            k.op("act", lambda: nc.scalar.mul(out=wqrot[:, :, :, 64:80], in_=wq4[:, :, :, 80:96], mul=-1.0),
                 r=[wqb], w=[wqrot])
            k.op("act", lambda: nc.scalar.copy(out=wqrot[:, :, :, 80:96], in_=wq4[:, :, :, 64:80]), r=[wqb], w=[wqrot])
            cst = k.sb("cst", [96, 2, S], F32, esB)
            Vt = k.sb("Vt", [128, 32, 8, 65], BF16, esB)
            k.op("pool", lambda: nc.gpsimd.memset(Vt[:, :, :, 64:65], 1.0), w=[Vt])
            wkv3 = wkvb[:].rearrange("p (h c) -> p h c", c=128)
            for kb in range(32):
                pv = nps()
                k.op("pe", lambda kb=kb, pv=pv: nc.tensor.matmul(
                    pv[:, :], lhsT=ckvT[:, kb * 128:(kb + 1) * 128], rhs=wkv3[:, :, 64:128], start=True, stop=True),
                    r=[ckvT, wkvb], w=[pv])
                k.op("act", lambda kb=kb, pv=pv: nc.scalar.copy(
                    out=Vt[:, kb, :, 0:64], in_=pv[:, :].rearrange("p (h c) -> p h c", c=64)), r=[pv], w=[Vt])
            esB0 = ExitStack()
            b_rv = k.sb("b_rv", [96, 512], F32, esB0)
            b_rki = k.sb("b_rki", [96, 512], I32, esB0)
            b_rkf = k.sb("b_rkf", [96, 512], F32, esB0)
            b_rfr = k.sb("b_rfr", [96, 512], F32, esB0)
            b_posi = k.sb("b_posi", [96, 512], I32, esB0)
            for Tt in range(NT):
                sincos_tile(Tt * 512, b_posi, b_rv, b_rki, b_rkf, b_rfr, cst,
                            outf=lambda which, Tt=Tt: cst[64:96, which, Tt * 512:(Tt + 1) * 512])
            k.barrier()
            esB0.close()
            Qh2 = [k.sb("Qh%d" % i, [96, S], BF16, esB) for i in range(2)]
            Kh2 = [k.sb("Kh%d" % i, [96, S], BF16, esB) for i in range(2)]
            for i in range(2):
                k.op("dve", lambda i=i: nc.vector.tensor_copy(out=Kh2[i][64:96, :], in_=kper[64:96, :]),
                     r=[kper], w=[Kh2[i]])
            PT = [k.sb("PT%d" % i, [128, 512], BF16, esB) for i in range(5)]
            osb2 = [k.sb("osb%d" % i, [65, 512], F32, esB) for i in range(2)]
            rec = k.sb("rec", [64, 512], F32, esB)
            yh = k.sb("yh", [64, 512], F32, esB)
            yaT = k.sb("yaT", [128, 4, S], BF16, esB)
            sel = k.sb("sel", [65, 64], F32, esB)
            k.op("pool", lambda: nc.gpsimd.memset(sel[:], 0.0), w=[sel])
            k.op("pool", lambda: nc.gpsimd.memset(sel[64:65, :], 1.0), w=[sel])
            qt1 = k.sb("qt1", [96, 512], F32, esB)
            qt2 = k.sb("qt2", [96, 512], F32, esB)

            def prep_tile(h, Tt):
                Qh, Kh = Qh2[h % 2], Kh2[h % 2]
                ts_ = slice(Tt * 512, (Tt + 1) * 512)
                pk = nps_p()
                k.op("pe", lambda: nc.tensor.matmul(
                    pk[0:64, :], lhsT=wkvb[:, h * 128:h * 128 + 64], rhs=ckvT[:, ts_], start=True, stop=True),
                    r=[wkvb, ckvT], w=[pk])
                k.op("dve", lambda: nc.vector.tensor_copy(out=Kh[0:64, ts_], in_=pk[0:64, :]), r=[pk], w=[Kh])
                pq = nps_p()
                for m in range(2):
                    k.op("pe", lambda m=m: nc.tensor.matmul(
                        pq[0:96, :], lhsT=wqb[:, m, h * 96:(h + 1) * 96], rhs=cqT[:, m, ts_],
                        start=(m == 0), stop=(m == 1)), r=[wqb, cqT], w=[pq], inc=(m == 1))
                k.op("dve", lambda: nc.vector.tensor_copy(out=Qh[0:64, ts_], in_=pq[0:64, :]), r=[pq], w=[Qh])
                k.op("dve", lambda: nc.vector.tensor_tensor(
                    out=qt1[64:96, :], in0=pq[64:96, :], in1=cst[64:96, 0, ts_], op=ALU.mult),
                    r=[pq, cst], w=[qt1])
                pr2 = nps_p()
                for m in range(2):
                    k.op("pe", lambda m=m: nc.tensor.matmul(
                        pr2[0:96, :], lhsT=wqrot[:, m, h, :], rhs=cqT[:, m, ts_],
                        start=(m == 0), stop=(m == 1)), r=[wqrot, cqT], w=[pr2], inc=(m == 1))
                k.op("dve", lambda: nc.vector.tensor_tensor(
                    out=qt2[64:96, :], in0=pr2[64:96, :], in1=cst[64:96, 1, ts_], op=ALU.mult),
                    r=[pr2, cst], w=[qt2])
                k.op("dve", lambda: nc.vector.tensor_tensor(
                    out=Qh[64:96, ts_], in0=qt1[64:96, :], in1=qt2[64:96, :], op=ALU.add),
                    r=[qt1, qt2], w=[Qh])

            sc_rr = [0]
            pr_rr = [0]
            PREPB = [TView(PB[0], F32), TView(PB[1], F32)]

            def nps_b():
                sc_rr[0] = (sc_rr[0] + 1) % 4
                return PS[sc_rr[0]]

            def nps_p():
                pr_rr[0] = (pr_rr[0] + 1) % 2
                return PREPB[pr_rr[0]]

            for Tt in range(NT):
                prep_tile(0, Tt)
            ptr = [0]
            for h in range(8):
                Qh, Kh = Qh2[h % 2], Kh2[h % 2]
                steps = []
                for Qt in range(NT):
                    for kb in range(4 * Qt + 4):
                        steps.append((Qt, kb))
                nst = len(steps)
                info = {}

                def emit_qk(i):
                    Qt, kb = steps[i]
                    d = kb - 4 * Qt
                    c0 = 128 * d if d > 0 else 0
                    qs = slice(Qt * 512 + c0, (Qt + 1) * 512)
                    cs_ = slice(c0, 512)
                    sp_ = nps_b()
                    info[i] = (sp_, cs_, c0, d)
                    k.op("pe", lambda: nc.tensor.matmul(
                        sp_[:, cs_], lhsT=Kh[0:96, kb * 128:(kb + 1) * 128], rhs=Qh[0:96, qs],
                        start=True, stop=True), r=[Kh, Qh], w=[sp_])

                def emit_rest(i):
                    Qt, kb = steps[i]
                    sp_, cs_, c0, d = info.pop(i)
                    nkb = 4 * Qt + 4
                    po = PS[4 + Qt % 2]
                    pt = PT[ptr[0] % 5]
                    ptr[0] += 1
                    k.op("act", lambda: nc.scalar.activation(
                        out=pt[:, cs_], in_=sp_[:, cs_], func=AF.Exp, scale=SCALE), r=[sp_], w=[pt])
                    if d >= 0:
                        k.op("pool", lambda: nc.gpsimd.affine_select(
                            out=pt[:, c0:c0 + 128], in_=pt[:, c0:c0 + 128], pattern=[[1, 128]],
                            compare_op=ALU.is_ge, fill=REG_ZERO, base=0, channel_multiplier=-1),
                            r=[pt], w=[pt])
                    k.op("pe", lambda: nc.tensor.matmul(
                        po[0:65, cs_], lhsT=Vt[:, kb, h, :], rhs=pt[:, cs_], start=(kb == 0),
                        stop=(kb == nkb - 1)), r=[Vt, pt], w=[po], inc=True)
                    if kb == nkb - 1:
                        osb = osb2[Qt % 2]
                        k.op("dve", lambda: nc.vector.tensor_copy(out=osb[:], in_=po[0:65, :]), r=[po], w=[osb])
                        dd("osb", osb, osb[:], [65, 512], F32)
                        return Qt
                    return None

                def finalize(Qt):
                    osb = osb2[Qt % 2]
                    pd = PS[4 + Qt % 2]
                    k.op("pe", lambda: nc.tensor.matmul(pd[0:64, :], lhsT=sel[:, :], rhs=osb[:, :],
                                                        start=True, stop=True), r=[sel, osb], w=[pd])
                    k.op("dve", lambda: nc.vector.reciprocal(out=rec[:], in_=pd[0:64, :]), r=[pd], w=[rec])
                    k.op("dve", lambda: nc.vector.tensor_tensor(out=yh[:], in0=osb[0:64, :], in1=rec[:],
                                                                op=ALU.mult), r=[osb, rec], w=[yh])
                    p0 = (h % 2) * 64
                    k.op("dve", lambda: nc.vector.tensor_copy(
                        out=yaT[p0:p0 + 64, h // 2, Qt * 512:(Qt + 1) * 512], in_=yh[:]), r=[yh], w=[yaT])

                LOOK = 3
                for i in range(min(LOOK, nst)):
                    emit_qk(i)
                pending_fin = []
                prep_next = list(range(NT)) if h < 7 else []
                for i in range(nst):
                    if i + LOOK < nst:
                        emit_qk(i + LOOK)
                    fin = emit_rest(i)
                    pending_fin = [(q, c - 1) for (q, c) in pending_fin]
                    while pending_fin and pending_fin[0][1] <= 0:
                        finalize(pending_fin.pop(0)[0])
                    if fin is not None:
                        pending_fin.append((fin, 2))
                    if prep_next and i % 16 == 8:
                        prep_tile(h + 1, prep_next.pop(0))
                for q, c in pending_fin:
                    finalize(q)
                for Tt in prep_next:
                    prep_tile(h + 1, Tt)
            dd("yaT", yaT, yaT[:], [128, 4, S], BF16)
            ysq = k.sb("ysq", [128, 4, 512], BF16, esB)
            yrs = k.sb("yrs", [128, 512], F32, esB)
            yo = k.sb("yo", [128, 4, 512], BF16, esB)
            for Tt in range(NT):
                ts_ = slice(Tt * 512, (Tt + 1) * 512)
                k.op("dve", lambda ts_=ts_: nc.vector.tensor_tensor(out=ysq[:], in0=yaT[:, :, ts_],
                                                                    in1=yaT[:, :, ts_], op=ALU.mult),
                     r=[yaT], w=[ysq])
                pn = PS[Tt % 4]
                for m in range(4):
                    k.op("pe", lambda m=m, pn=pn: nc.tensor.matmul(pn[:, :], lhsT=ones_b[:, :], rhs=ysq[:, m, :],
                                                                   start=(m == 0), stop=(m == 3)),
                         r=[ones_b, ysq], w=[pn], inc=(m == 3))
                k.op("act", lambda pn=pn: nc.scalar.activation(out=yrs[:], in_=pn[:, :], func=AF.Ln,
                                                               scale=1.0 / 512, bias=epsb[:, :]),
                     r=[pn, epsb], w=[yrs])
                k.op("act", lambda: nc.scalar.activation(out=yrs[:], in_=yrs[:], func=AF.Exp, scale=-0.5),
                     r=[yrs], w=[yrs])
                for m in range(4):
                    k.op("dve", lambda m=m, ts_=ts_: nc.vector.scalar_tensor_tensor(
                        out=yo[:, m, :], in0=yaT[:, m, ts_], scalar=gmo[:, m:m + 1], in1=yrs[:],
                        op0=ALU.mult, op1=ALU.mult), r=[yaT, gmo, yrs], w=[yo])
                dd("yo", yo, yo[:], [128, 4, 512], BF16)
                dd("yrs", yrs, yrs[:], [128, 512], F32)
                if Tt == 1:
                    dd("yo1", yo, yo[:], [128, 4, 512], BF16)
                    dd("yrs1", yrs, yrs[:], [128, 512], F32)
                    dd("ysq1", ysq, ysq[:], [128, 4, 512], BF16)
                for m in range(4):
                    k.dma("act", yT_d[4 + m, :, ts_], yo[:, m, :], r=[yo], w=[yT_buf], own=yo.sub("st"))
            k.wait_all("sp", [yo])

        k.barrier()
        esAB.close()
        esW2 = ExitStack()
        wpp = k.sb("wpp", [128, 2, 1024], BF16, esW2)
        wpg = k.sb("wpg", [128, 8, 1024], BF16, esW2)
        esW1 = ExitStack()
        wup = k.sb("wup", [128, 8, 4096], BF16, esW1)
        wdn = k.sb("wdn", [128, 32, 1024], BF16, esW1)
        g_mlp = k.sb("g_mlp", [128, D], F32, esW1)
        esC = ExitStack()
        with esC:
            wout = k.sb("wout", [128, 8, 1024], BF16, esC)
            for kc in range(8):
                k.dma("pool", wout[:, kc, :], wout_d[:, kc, :], w=[wout], own=wout.sub("ld"))
            for kc in range(8):
                for q4 in range(4):
                    k.dma("pool", wup[:, kc, q4 * 1024:(q4 + 1) * 1024], wup_d[:, kc, q4 * 1024:(q4 + 1) * 1024],
                          w=[wup], own=wup.sub("ld"))
            for fc in range(32):
                k.dma("pool", wdn[:, fc, :], wdn_d[:, fc, :], w=[wdn], own=wdn.sub("ld"))
            k.dma("pool", g_mlp[:], g_mlp_d[:, :], w=[g_mlp], own=g_mlp)
            for kc in range(2):
                k.dma("pool", wpp[:, kc, :], wpp_d[:, kc, :], w=[wpp], own=wpp.sub("ld"))
            for kc in range(8):
                k.dma("pool", wpg[:, kc, :], wpg_d[:, kc, :], w=[wpg], own=wpg.sub("ld"))
            yt2 = [k.sb("yt%d" % i, [128, 8, 512], BF16, esC) for i in range(2)]
            xc = [k.sb("xc%d" % i, [128, D], F32, esC) for i in range(2)]
            hc = [k.sb("hc%d" % i, [128, D], F32, esC) for i in range(2)]

            def c1_block(Tt, blk):
                yt = yt2[Tt % 2]
                r0 = Tt * 512 + blk * 128
                xb = xc[blk % 2]
                hb = hc[blk % 2]
                k.dma("sp", xb[:], x_d[r0:r0 + 128, :], w=[xb], own=xb)
                yield
                for n2 in range(2):
                    pp = nps(blk % 2)
                    for kc in range(8):
                        k.op("pe", lambda kc=kc, pp=pp, n2=n2: nc.tensor.matmul(
                            pp[:, :], lhsT=yt[:, kc, blk * 128:(blk + 1) * 128],
                            rhs=wout[:, kc, n2 * 512:(n2 + 1) * 512], start=(kc == 0), stop=(kc == 7)),
                            r=[yt, wout], w=[pp], inc=(kc == 7))
                    k.op("dve", lambda pp=pp, n2=n2: nc.vector.tensor_tensor(
                        out=hb[:, n2 * 512:(n2 + 1) * 512], in0=pp[:, :], in1=xb[:, n2 * 512:(n2 + 1) * 512],
                        op=ALU.add), r=[pp, xb], w=[hb])
                    yield
                k.dma("act", h1_d[r0:r0 + 128, :], hb[:], r=[hb], w=[h1_buf], own=hb.sub("st"))

            for Tt in range(NT):
                for kc in range(8):
                    k.dma("sp", yt2[Tt % 2][:, kc, :], yT_d[kc, :, Tt * 512:(Tt + 1) * 512], r=[yT_buf],
                          w=[yt2[Tt % 2]], own=yt2[Tt % 2])
                run_interleaved((c1_block(Tt, b) for b in range(4)), 2)
            k.wait_all("sp", hc)

        k.barrier()
        esD = ExitStack()
        with esD:
            ht = [k.sb("ht%d" % i, [128, D], F32, esD) for i in range(2)]
            ho = [k.sb("ho%d" % i, [128, D], F32, esD) for i in range(2)]
            ubm = [k.sb("ubm%d" % i, [128, D], BF16, esD) for i in range(2)]
            uTm = [k.sb("uTm%d" % i, [128, 8, 128], BF16, esD) for i in range(2)]
            ssm = [k.sb("ssm%d" % i, [128, 1], F32, esD) for i in range(2)]
            hT = [k.sb("hT%d" % i, [128, 32, 128], BF16, esD) for i in range(2)]
            rl = [[k.sb("rl%d_%d" % (i, j), [128, 512], BF16, esD) for j in range(2)] for i in range(2)]
            def mlp_block(blk):
                i2 = blk % 2
                r0 = blk * 128
                k.dma("sp", ht[i2][:], h1_d[r0:r0 + 128, :], r=[h1_buf], w=[ht[i2]], own=ht[i2])
                rms_tm(ht[i2][:], [ht[i2]], g_mlp, ubm[i2][:], [ubm[i2]], ubm[i2], ssm[i2])
                dd("ubm", ubm[i2], ubm[i2][:], [128, D], BF16)
                yield
                to_fm(ubm[i2], [ubm[i2]], uTm[i2], [uTm[i2]], 0, 8, p=i2)
                yield
                dd("uTm", uTm[i2], uTm[i2][:], [128, 8, 128], BF16)
                for f4 in range(8):
                    pp = nps(i2)
                    for j in range(4):
                        fc = f4 * 4 + j
                        for kc in range(8):
                            k.op("pe", lambda kc=kc, pp=pp, fc=fc, j=j, i2=i2: nc.tensor.matmul(
                                pp[:, j * 128:(j + 1) * 128], lhsT=wup[:, kc, fc * 128:(fc + 1) * 128],
                                rhs=uTm[i2][:, kc, :], start=(kc == 0), stop=(kc == 7)),
                                r=[wup, uTm[i2]], w=[pp], inc=(kc == 7 and j == 3))
                    rb = rl[i2][f4 % 2]
                    if dbg is not None and 'ppc' in dbg and blk == 0 and f4 == 0:
                        ppc = k.sb("ppc", [128, 512], F32, esD)
                        k.op("dve", lambda pp=pp: nc.vector.tensor_copy(out=ppc[:], in_=pp[:, :]), r=[pp], w=[ppc])
                        dd("ppc", ppc, ppc[:], [128, 512], F32)
                    k.op("act", lambda pp=pp, rb=rb: nc.scalar.activation(out=rb[:], in_=pp[:, :], func=AF.Relu),
                         r=[pp], w=[rb])
                    dd("rl", rb, rb[:], [128, 512], BF16)
                    k.op("pool", lambda rb=rb, f4=f4, i2=i2: nc.gpsimd.tensor_tensor(
                        out=hT[i2][:, f4 * 4:(f4 + 1) * 4, :], in0=rb[:].rearrange("p (j t) -> p j t", j=4),
                        in1=rb[:].rearrange("p (j t) -> p j t", j=4), op=ALU.mult), r=[rb], w=[hT[i2]])
                    yield
                dd("hT", hT[i2], hT[i2][:], [128, 32, 128], BF16)
                dd("wup", wup, wup[:, 0, :], [128, 4096], BF16)
                dd("wdn", wdn, wdn[:, 0, :], [128, 1024], BF16)
                for n2 in range(2):
                    pp = nps(i2)
                    for fc in range(32):
                        k.op("pe", lambda fc=fc, pp=pp, n2=n2, i2=i2: nc.tensor.matmul(
                            pp[:, :], lhsT=hT[i2][:, fc, :], rhs=wdn[:, fc, n2 * 512:(n2 + 1) * 512],
                            start=(fc == 0), stop=(fc == 31)), r=[hT[i2], wdn], w=[pp], inc=(fc == 31))
                    k.op("dve", lambda pp=pp, n2=n2, i2=i2: nc.vector.tensor_tensor(
                        out=ho[i2][:, n2 * 512:(n2 + 1) * 512], in0=pp[:, :], in1=ht[i2][:, n2 * 512:(n2 + 1) * 512],
                        op=ALU.add), r=[pp, ht[i2]], w=[ho[i2]])
                    yield
                k.dma("sp", h2_d[r0:r0 + 128, :], ho[i2][:], r=[ho[i2]], w=[h2_buf], own=ho[i2].sub("st"))
            run_interleaved((mlp_block(b) for b in range(32)), 2, stagger=6)
            k.wait_all("sp", ho)

        k.barrier()
        esW1.close()
        esE = ExitStack()
        with esE:
            gB = {}
            for nm, src in [("post", g_post_d), ("gate", g_gate_d), ("fin", g_fin_d)]:
                gB[nm] = k.sb("gB_" + nm, [128, D], F32, esE)
                k.dma("sp", gB[nm][:], src[:, :], w=[gB[nm]], own=gB[nm])
            h2t = [k.sb("h2t%d" % i, [128, D], F32, esE) for i in range(5)]
            pin = [k.sb("pin%d" % i, [128, 256], F32, esE) for i in range(5)]
            pbf = [k.sb("pbf%d" % i, [128, 256], BF16, esE) for i in range(5)]
            pTt = [k.sb("pTt%d" % i, [128, 2, 128], BF16, esE) for i in range(5)]
            er = [k.sb("er%d" % i, [128, D], F32, esE) for i in range(5)]
            en = [k.sb("en%d" % i, [128, D], F32, esE) for i in range(5)]
            ug = [k.sb("ug%d" % i, [128, D], BF16, esE) for i in range(5)]
            uTg = [k.sb("uTg%d" % i, [128, 8, 128], BF16, esE) for i in range(5)]
            gt = [k.sb("gt%d" % i, [128, D], F32, esE) for i in range(5)]
            h3 = [k.sb("h3%d" % i, [128, D], F32, esE) for i in range(5)]
            fo = [k.sb("fo%d" % i, [128, D], F32, esE) for i in range(5)]
            jk2 = [k.sb("jk%d" % i, [128, D], BF16, esE) for i in range(5)]
            sse = [k.sb("sse%d" % i, [128, 1], F32, esE) for i in range(5)]
            ssg = [k.sb("ssg%d" % i, [128, 1], F32, esE) for i in range(5)]
            ssf = [k.sb("ssf%d" % i, [128, 1], F32, esE) for i in range(5)]
            def ple_block(blk):
                i2 = blk % 5
                r0 = blk * 128
                k.dma("sp", h2t[i2][:], h2_d[r0:r0 + 128, :], r=[h2_buf], w=[h2t[i2]], own=h2t[i2])
                k.dma("sp", pin[i2][:], p_d[r0:r0 + 128, :], w=[pin[i2]], own=pin[i2])
                k.op("dve", lambda i2=i2: nc.vector.tensor_copy(out=pbf[i2][:], in_=pin[i2][:]),
                     r=[pin[i2]], w=[pbf[i2]])
                to_fm(pbf[i2], [pbf[i2]], pTt[i2], [pTt[i2]], 0, 2, p=None)
                yield
                for n2 in range(2):
                    pp = nps()
                    for kc in range(2):
                        k.op("pe", lambda kc=kc, pp=pp, n2=n2, i2=i2: nc.tensor.matmul(
                            pp[:, :], lhsT=pTt[i2][:, kc, :], rhs=wpp[:, kc, n2 * 512:(n2 + 1) * 512],
                            start=(kc == 0), stop=(kc == 1)), r=[pTt[i2], wpp], w=[pp], inc=(kc == 1))
                    k.op("act", lambda pp=pp, n2=n2, i2=i2: nc.scalar.copy(
                        out=er[i2][:, n2 * 512:(n2 + 1) * 512], in_=pp[:, :]), r=[pp], w=[er[i2]])
                yield
                rms_tm(er[i2][:], [er[i2]], gB["post"], en[i2][:], [en[i2]], jk2[i2], sse[i2])
                yield
                rms_tm(h2t[i2][:], [h2t[i2]], gB["gate"], ug[i2][:], [ug[i2]], jk2[i2], ssg[i2])
                yield
                to_fm(ug[i2], [ug[i2]], uTg[i2], [uTg[i2]], 0, 8, p=None)
                yield
                for n2 in range(2):
                    pp = nps()
                    for kc in range(8):
                        k.op("pe", lambda kc=kc, pp=pp, n2=n2, i2=i2: nc.tensor.matmul(
                            pp[:, :], lhsT=uTg[i2][:, kc, :], rhs=wpg[:, kc, n2 * 512:(n2 + 1) * 512],
                            start=(kc == 0), stop=(kc == 7)), r=[uTg[i2], wpg], w=[pp], inc=(kc == 7))
                    gs = gt[i2][:, n2 * 512:(n2 + 1) * 512]
                    k.op("act", lambda pp=pp, gs=gs: nc.scalar.activation(out=gs, in_=pp[:, :], func=AF.Exp,
                                                                          scale=-1.0), r=[pp], w=[gt[i2]])
                yield
                k.op("act", lambda i2=i2: nc.scalar.activation(out=gt[i2][:], in_=gt[i2][:], func=AF.Ln,
                                                               bias=oneb[:, :]), r=[gt[i2], oneb], w=[gt[i2]])
                k.op("act", lambda i2=i2: nc.scalar.activation(out=gt[i2][:], in_=gt[i2][:], func=AF.Exp,
                                                               scale=-1.0), r=[gt[i2]], w=[gt[i2]])
                k.op("pool", lambda i2=i2: nc.gpsimd.tensor_tensor(out=gt[i2][:], in0=gt[i2][:], in1=en[i2][:],
                                                                   op=ALU.mult), r=[gt[i2], en[i2]], w=[gt[i2]])
                k.op("dve", lambda i2=i2: nc.vector.tensor_tensor(out=h3[i2][:], in0=h2t[i2][:], in1=gt[i2][:],
                                                                  op=ALU.add), r=[h2t[i2], gt[i2]], w=[h3[i2]])
                yield
                rms_tm(h3[i2][:], [h3[i2]], gB["fin"], fo[i2][:], [fo[i2]], jk2[i2], ssf[i2])
                k.dma("sp", out_d[r0:r0 + 128, :], fo[i2][:], r=[fo[i2]], w=[], own=fo[i2].sub("st"))
            run_interleaved((ple_block(b) for b in range(32)), 5, stagger=2)
            k.wait_all("sp", fo)
        esW2.close()
        if dbg is not None:
            dsb = Buf("dbgdma")
            k.dma("sp", dbg_y[:, :, :], yT_d[:, :, :], r=[yT_buf], w=[], own=dsb)
            k.dma("sp", dbg_h1[:, :], h1_d[:, :], r=[h1_buf], w=[], own=dsb)
            k.dma("sp", dbg_h2[:, :], h2_d[:, :], r=[h2_buf], w=[], own=dsb)
            k.eng["sp"].wait_ge(dsb.sem, dsb.dtot)
    return nc


def _lay(w, kc):
    K, N = w.shape
    return np.ascontiguousarray(w.reshape(kc, 128, N).transpose(1, 0, 2))


def make_inmaps(inp):
    f = lambda a: np.ascontiguousarray(np.asarray(a, dtype=np.float32))
    bc128 = lambda v: np.ascontiguousarray(np.broadcast_to(f(v).reshape(1, -1), (128, v.size)))
    common = {
        "w_in": _lay(f(inp["w_in"][0]), 8),
        "w_out": _lay(f(inp["w_out"][0]), 8),
        "w_up": _lay(f(inp["w_up"][0]), 8),
        "w_down": _lay(f(inp["w_down"][0]), 32),
        "w_ple_proj": _lay(f(inp["w_ple_proj"][0]), 2),
        "w_ple_gate": _lay(f(inp["w_ple_gate"][0]), 8),
        "w_q_b": _lay(f(inp["w_q_b"][0]), 2),
        "w_kv_b": f(inp["w_kv_b"][0]),
        "g_mix": bc128(inp["mix_norm_w"][0]),
        "g_mlp": bc128(inp["mlp_norm_w"][0]),
        "g_post": bc128(inp["ple_post_norm_w"][0]),
        "g_gate": bc128(inp["ple_gate_norm_w"][0]),
        "g_fin": bc128(inp["final_norm_w"]),
        "g_gdn": np.ascontiguousarray(np.broadcast_to(np.tile(f(inp["gdn_norm_w"][0]), 8).reshape(1, 512), (64, 512))),
        "cw": np.ascontiguousarray(f(inp["conv_w"][0]).T.reshape(24, 64, 4).transpose(1, 0, 2)),
        "alog": np.ascontiguousarray(np.broadcast_to(f(inp["A_log"][0]).reshape(1, 8), (64, 8))),
        "dtb": np.ascontiguousarray(np.broadcast_to(f(inp["dt_bias"][0]).reshape(1, 8), (64, 8))),
        "gq": np.ascontiguousarray(f(inp["q_norm_w"][0]).reshape(2, 128).T),
        "gkv": np.ascontiguousarray(f(inp["kv_norm_w"][0]).reshape(1, 128).T),
        "gmo": np.ascontiguousarray(f(inp["mla_out_norm_w"][0]).reshape(4, 128).T),
    }
    invf = np.zeros((96, 1), np.float32)
    fr = (10000.0 ** (-np.arange(0, 32, 2, dtype=np.float32) / 32)).astype(np.float32)
    invf[64:80, 0] = fr
    invf[80:96, 0] = fr
    common["invf"] = invf
    maps = []
    x = np.asarray(inp["x"], dtype=np.float32)
    p = np.asarray(inp["p"], dtype=np.float32)
    pos = np.asarray(inp["positions"], dtype=np.int32)
    for b in range(8):
        m = dict(common)
        m["x"] = np.ascontiguousarray(x[b])
        m["p"] = np.ascontiguousarray(p[0, b])
        m["pos"] = np.ascontiguousarray(pos[b].reshape(1, S))
        maps.append(m)
    return maps


def kernel(**inputs):
    nc = build_nc()
    maps = make_inmaps(inputs)
    res = run_bass_kernel_spmd(nc, maps, core_ids=list(range(8)))
    return np.stack([np.asarray(r["out"], dtype=np.float32) for r in res.results], axis=0)
```
